# Optimizing a Trainium2 kernel written in Bass

```python
import math
import jax, jax.numpy as jnp
from jax import lax
import numpy as np

D_MODEL = 1024
BATCH = 32
SEQ = 256
DEPTH = 4
DEC_BATCH = 8
DEC_SEQ = 2048
PAST_LEN = 512

GRID_W = 64
W_BR = D_MODEL
N_AB = (DEPTH + 1) // 2
N_CD = DEPTH // 2
H_A = 4
DK_A = W_BR // (2 * H_A)
DV_A = W_BR // H_A
GLA_RANK = 16
GLA_GATE_NORM = 16.0
H_B = 4
DK_B = W_BR // (2 * H_B)
DV_B = W_BR // H_B
RET_MIN_EXP = 5.0
RET_MAX_EXP = 12.0
CHUNK = 64
HY_EMB = 33
HY_HID = 64
HY_SHORT = 3
HY_N_FILT = 4
HY_FAST_DECAY = 0.3
HY_SLOW_DECAY = 1.5
HY_TARGET = 1e-2
LRU_CONV = 4
LRU_BLOCKS = 8
LRU_BS = W_BR // LRU_BLOCKS
LRU_C = 8.0
AB_IN = 2 * H_A * DK_A + W_BR + 2 * GLA_RANK + W_BR + 2 * H_B * DK_B + 2 * W_BR
CD_IN = 3 * W_BR + W_BR + W_BR + W_BR
EPS = 1e-6

kernel_name = 'hybrid_gla_retnet_hyena_rglru_diffusion_step'


def rms_norm(x, g):
    xf = x.astype(jnp.float32)
    y = xf * lax.rsqrt(jnp.mean(jnp.square(xf), axis=-1, keepdims=True) + EPS)
    return (y * g).astype(x.dtype)


def head_norm(o, g, center):
    if center:
        o = o - jnp.mean(o, axis=-1, keepdims=True)
    o = o * lax.rsqrt(jnp.mean(jnp.square(o), axis=-1, keepdims=True) + EPS)
    B, H, L, dv = o.shape
    return o.transpose(0, 2, 1, 3).reshape(B, L, H * dv) * g


def split_cols(t, sizes):
    return jnp.split(t, np.cumsum(sizes)[:-1].tolist(), axis=-1)


def depthwise_conv(x, w, b, pad_left):
    width = w.shape[0]
    L = x.shape[1]
    xp = jnp.pad(x, ((0, 0), (pad_left, width - 1 - pad_left), (0, 0)))
    return sum(xp[:, k:k + L] * w[k] for k in range(width)) + b


def chunked_gla(q, k, v, log_g, s0):
    B, H, L, dk = q.shape
    dv = v.shape[-1]
    n = L // CHUNK
    q = q.reshape(B, H, n, CHUNK, dk)
    k = k.reshape(B, H, n, CHUNK, dk)
    v = v.reshape(B, H, n, CHUNK, dv)
    b = jnp.cumsum(log_g.reshape(B, H, n, CHUNK, dk), axis=3)
    b_last = b[:, :, :, -1:, :]
    q_in = q * jnp.exp(b)
    k_in = k * jnp.exp(-b)
    k_st = k * jnp.exp(b_last - b)
    lower = jnp.tril(jnp.ones((CHUNK, CHUNK), dtype=bool))
    att = jnp.where(lower, jnp.einsum('bhnid,bhnjd->bhnij', q_in, k_in), 0.0)
    o_intra = jnp.einsum('bhnij,bhnjv->bhniv', att, v)
    kv = jnp.einsum('bhnjd,bhnjv->bhndv', k_st, v)
    decay = jnp.exp(b_last[:, :, :, 0, :])

    def step(s, xs):
        q_c, dec_c, kv_c = xs
        o_c = jnp.einsum('bhid,bhdv->bhiv', q_c, s)
        return dec_c[..., None] * s + kv_c, o_c

    s_final, o_inter = lax.scan(step, s0, (jnp.moveaxis(q_in, 2, 0), jnp.moveaxis(decay, 2, 0),
                                           jnp.moveaxis(kv, 2, 0)))
    o = o_intra + jnp.moveaxis(o_inter, 0, 2)
    return o.reshape(B, H, L, dv), s_final


def bidirectional_gla(q, k, v, lg_f, lg_b, s_f, s_b):
    o_f, last_f = chunked_gla(q, k, v, lg_f, s_f)
    fl = lambda t: jnp.flip(t, axis=2)
    o_b, last_b = chunked_gla(fl(q), fl(k), fl(v), fl(lg_b), s_b)
    return o_f + fl(o_b), jnp.stack([last_f, last_b], axis=1)


def retention_log_decay(direction):
    step = (RET_MAX_EXP - RET_MIN_EXP) / (H_B - 1)
    expo = RET_MIN_EXP + step * (jnp.arange(H_B, dtype=jnp.float32) + 0.5 * direction)
    return jnp.log1p(-jnp.exp2(-expo))


def hyena_filters(L, w1, b1, freq1, w2, b2, freq2, w3, b3):
    f32 = jnp.float32
    t = jnp.linspace(0.0, 1.0, L, dtype=f32)[:, None]
    bands = (HY_EMB - 1) // 2
    f = jnp.linspace(1e-4, bands - 1, bands, dtype=f32)[None, :]
    ang = (2.0 * math.pi / L) * jnp.arange(L, dtype=f32)[:, None] * f
    feats = jnp.concatenate([t, jnp.cos(ang), -jnp.sin(ang)], axis=-1)
    g = jnp.sin(freq1 * (feats @ w1 + b1))
    g = jnp.sin(freq2 * (g @ w2 + b2))
    filt = (g @ w3 + b3).reshape(L, HY_N_FILT, W_BR)
    deltas = jnp.abs(jnp.linspace(math.log(HY_TARGET) / HY_FAST_DECAY,
                                  math.log(HY_TARGET) / HY_SLOW_DECAY, W_BR, dtype=f32))
    window = jnp.exp(-t * deltas[None, :])
    return filt * window[:, None, :]


def long_conv(u, hf_fwd, hf_bwd, skip):
    L = u.shape[1]
    uf = jnp.fft.rfft(u, n=2 * L, axis=1)
    y = jnp.fft.irfft(uf * (hf_fwd + jnp.conj(hf_bwd))[None], n=2 * L, axis=1)[:, :L]
    return y + u * skip


def rg_lru(x, w_r, b_r, w_i, b_i, lam, h0):
    B, L, W = x.shape
    xb = x.reshape(B, L, LRU_BLOCKS, LRU_BS)
    r = jax.nn.sigmoid(jnp.einsum('blnc,ncd->blnd', xb, w_r).reshape(B, L, W) + b_r)
    i = jax.nn.sigmoid(jnp.einsum('blnc,ncd->blnd', xb, w_i).reshape(B, L, W) + b_i)
    log_a = -LRU_C * r * jax.nn.softplus(-lam)
    a = jnp.exp(log_a)
    b = jnp.sqrt(-jnp.expm1(2.0 * log_a)) * (i * x)
    b = b.at[:, 0].add(a[:, 0] * h0)
    _, h = lax.associative_scan(lambda l, rr: (l[0] * rr[0], rr[0] * l[1] + rr[1]), (a, b), axis=1)
    return h, h[:, -1]


def ab_mixer(h, lp, s_gla, s_ret):
    B, L, _ = h.shape
    f32 = jnp.float32
    proj = jnp.einsum('bld,de->ble', h, lp['ab_w_in']).astype(f32)
    qa, ka, va, ga, za, qb, kb, vb, zb = split_cols(
        proj, (H_A * DK_A, H_A * DK_A, W_BR, 2 * GLA_RANK, W_BR, H_B * DK_B, H_B * DK_B, W_BR, W_BR))

    def heads(t, n_heads):
        return t.reshape(B, L, n_heads, -1).transpose(0, 2, 1, 3)

    gate_logit = jnp.einsum('bldr,drk->bldk', ga.reshape(B, L, 2, GLA_RANK), lp['ab_gate_w2']) + lp['ab_gate_b']
    log_g = jax.nn.log_sigmoid(gate_logit) / GLA_GATE_NORM
    s_gla = s_gla.astype(f32)
    o_a, st_a = bidirectional_gla(heads(qa, H_A) * DK_A ** -0.5, heads(ka, H_A), heads(va, H_A),
                                  heads(log_g[:, :, 0], H_A), heads(log_g[:, :, 1], H_A),
                                  s_gla[:, 0], s_gla[:, 1])
    qb, kb, vb = heads(qb, H_B), heads(kb, H_B) * DK_B ** -0.5, heads(vb, H_B)
    dec_f = jnp.broadcast_to(retention_log_decay(0.0)[None, :, None, None], qb.shape)
    dec_b = jnp.broadcast_to(retention_log_decay(1.0)[None, :, None, None], qb.shape)
    s_ret = s_ret.astype(f32)
    o_b, st_b = bidirectional_gla(qb, kb, vb, dec_f, dec_b, s_ret[:, 0], s_ret[:, 1])
    y_a = head_norm(o_a, lp['ab_head_g'][:W_BR], False) * jax.nn.silu(za)
    y_b = head_norm(o_b, lp['ab_head_g'][W_BR:], True) * jax.nn.silu(zb)
    out = jnp.einsum('ble,ed->bld', jnp.concatenate([y_a, y_b], axis=-1).astype(h.dtype), lp['ab_w_out'])
    return out, st_a, st_b


def cd_mixer(h, lp, s_lru):
    B, L, _ = h.shape
    f32 = jnp.float32
    proj = jnp.einsum('bld,de->ble', h, lp['cd_w_in']).astype(f32)
    u, z_h, x_r, z_r = split_cols(proj, (3 * W_BR, W_BR, W_BR, W_BR))
    u = depthwise_conv(u, lp['hy_short_w'], lp['hy_short_b'], (HY_SHORT - 1) // 2)
    v, x1, x2 = jnp.split(u, 3, axis=-1)
    filt = hyena_filters(L, lp['hy_w1'], lp['hy_b1'], lp['hy_freq1'], lp['hy_w2'], lp['hy_b2'],
                         lp['hy_freq2'], lp['hy_w3'], lp['hy_b3'])
    filt_f = jnp.fft.rfft(filt, n=2 * L, axis=0)
    skip = lp['hy_skip']
    z = x1 * long_conv(v, filt_f[:, 0], filt_f[:, 1], skip[0])
    z = x2 * long_conv(z, filt_f[:, 2], filt_f[:, 3], skip[1])
    y_h = z * jax.nn.silu(z_h)
    xc = depthwise_conv(x_r, lp['lru_conv_w'], lp['lru_conv_b'], LRU_CONV // 2)
    s0 = s_lru.astype(f32)
    h_f, last_f = rg_lru(xc, lp['lru_w_r'][0], lp['lru_b_r'][0], lp['lru_w_i'][0], lp['lru_b_i'][0],
                         lp['lru_lambda'][0], s0[:, 0])
    h_b, last_b = rg_lru(jnp.flip(xc, axis=1), lp['lru_w_r'][1], lp['lru_b_r'][1], lp['lru_w_i'][1],
                         lp['lru_b_i'][1], lp['lru_lambda'][1], s0[:, 1])
    y_r = (h_f + jnp.flip(h_b, axis=1)) * jax.nn.silu(z_r)
    out = jnp.einsum('ble,ed->bld', jnp.concatenate([y_h, y_r], axis=-1).astype(h.dtype), lp['cd_w_out'])
    return out, jnp.stack([last_f, last_b], axis=1)


def run_trunk(x, cond, s_gla, s_ret, s_lru, p, collect):
    ab_keys = ('ab_w_in', 'ab_gate_w2', 'ab_gate_b', 'ab_head_g', 'ab_w_out')
    cd_keys = ('cd_w_in', 'hy_short_w', 'hy_short_b', 'hy_w1', 'hy_b1', 'hy_freq1', 'hy_w2', 'hy_b2',
               'hy_freq2', 'hy_w3', 'hy_b3', 'hy_skip', 'lru_conv_w', 'lru_conv_b', 'lru_w_r', 'lru_b_r',
               'lru_w_i', 'lru_b_i', 'lru_lambda', 'cd_w_out')
    new_gla, new_ret, new_lru = [], [], []
    for layer in range(DEPTH):
        i = layer // 2
        mod = jnp.einsum('bd,de->be', jax.nn.silu(cond), p['w_mod'][layer]) + p['b_mod'][layer]
        shift, scale, gate = jnp.split(mod[:, None, :], 3, axis=-1)
        h = rms_norm(x, p['norm_g'][layer]) * (1.0 + scale) + shift
        if layer % 2 == 0:
            lp = {name: p[name][i] for name in ab_keys}
            out, sg, sr = ab_mixer(h, lp, s_gla[:, i], s_ret[:, i])
            if collect:
                new_gla.append(sg)
                new_ret.append(sr)
        else:
            lp = {name: p[name][i] for name in cd_keys}
            out, sl = cd_mixer(h, lp, s_lru[:, i])
            if collect:
                new_lru.append(sl)
        x = x + (gate * out).astype(x.dtype)
    y = rms_norm(x, p['final_g'])
    return y, new_gla, new_ret, new_lru


def setup_inputs(seed: int = 0) -> dict:
    key = jax.random.key(seed)
    keys = iter(jax.random.split(key, 48))
    f32 = jnp.float32

    def nrm(shape, scale):
        return scale * jax.random.normal(next(keys), shape, f32)

    def gain(shape):
        return 1.0 + nrm(shape, 0.02)

    lam_u = jax.random.uniform(next(keys), (N_CD, 2, W_BR), f32, 0.9, 0.999)
    lam_a = lam_u ** (1.0 / LRU_C)
    lru_lambda = jnp.log(lam_a) - jnp.log1p(-lam_a)
    return {
        'x_prompt': nrm((BATCH, SEQ, D_MODEL), 1.0),
        'x_sample': nrm((DEC_BATCH, DEC_SEQ, D_MODEL), 1.0),
        'c': nrm((DEC_BATCH, D_MODEL), 1.0),
        'state_gla': nrm((DEC_BATCH, N_AB, 2, H_A, DK_A, DV_A), 0.1),
        'state_ret': nrm((DEC_BATCH, N_AB, 2, H_B, DK_B, DV_B), 0.1),
        'state_lru': nrm((DEC_BATCH, N_CD, 2, W_BR), 0.5),
        'c_ctx': nrm((D_MODEL,), 1.0),
        'norm_g': gain((DEPTH, D_MODEL)),
        'w_mod': nrm((DEPTH, D_MODEL, 3 * D_MODEL), 0.5 * D_MODEL ** -0.5),
        'b_mod': nrm((DEPTH, 3 * D_MODEL), 0.02),
        'ab_w_in': nrm((N_AB, D_MODEL, AB_IN), D_MODEL ** -0.5),
        'ab_gate_w2': nrm((N_AB, 2, GLA_RANK, H_A * DK_A), GLA_RANK ** -0.5),
        'ab_gate_b': nrm((N_AB, 2, H_A * DK_A), 0.1),
        'ab_head_g': gain((N_AB, 2 * W_BR)),
        'ab_w_out': nrm((N_AB, 2 * W_BR, D_MODEL), (2 * W_BR) ** -0.5),
        'cd_w_in': nrm((N_CD, D_MODEL, CD_IN), D_MODEL ** -0.5),
        'hy_short_w': nrm((N_CD, HY_SHORT, 3 * W_BR), HY_SHORT ** -0.5),
        'hy_short_b': nrm((N_CD, 3 * W_BR), 0.02),
        'hy_w1': nrm((N_CD, HY_EMB, HY_HID), HY_EMB ** -0.5),
        'hy_b1': nrm((N_CD, HY_HID), 0.1),
        'hy_freq1': gain((N_CD, HY_HID)),
        'hy_w2': nrm((N_CD, HY_HID, HY_HID), HY_HID ** -0.5),
        'hy_b2': nrm((N_CD, HY_HID), 0.1),
        'hy_freq2': gain((N_CD, HY_HID)),
        'hy_w3': nrm((N_CD, HY_HID, HY_N_FILT * W_BR), 0.05 * HY_HID ** -0.5),
        'hy_b3': nrm((N_CD, HY_N_FILT * W_BR), 0.01),
        'hy_skip': nrm((N_CD, 2, W_BR), 0.3),
        'lru_conv_w': nrm((N_CD, LRU_CONV, W_BR), LRU_CONV ** -0.5),
        'lru_conv_b': nrm((N_CD, W_BR), 0.02),
        'lru_w_r': nrm((N_CD, 2, LRU_BLOCKS, LRU_BS, LRU_BS), LRU_BS ** -0.5),
        'lru_b_r': nrm((N_CD, 2, W_BR), 0.1),
        'lru_w_i': nrm((N_CD, 2, LRU_BLOCKS, LRU_BS, LRU_BS), LRU_BS ** -0.5),
        'lru_b_i': nrm((N_CD, 2, W_BR), 0.1),
        'lru_lambda': lru_lambda,
        'cd_w_out': nrm((N_CD, 2 * W_BR, D_MODEL), (2 * W_BR) ** -0.5),
        'final_g': gain((D_MODEL,)),
    }


def reference(x_prompt, x_sample, c, state_gla, state_ret, state_lru, c_ctx, norm_g, w_mod, b_mod,
              ab_w_in, ab_gate_w2, ab_gate_b, ab_head_g, ab_w_out, cd_w_in, hy_short_w, hy_short_b,
              hy_w1, hy_b1, hy_freq1, hy_w2, hy_b2, hy_freq2, hy_w3, hy_b3, hy_skip, lru_conv_w,
              lru_conv_b, lru_w_r, lru_b_r, lru_w_i, lru_b_i, lru_lambda, cd_w_out, final_g):
    params = dict(norm_g=norm_g, w_mod=w_mod, b_mod=b_mod, ab_w_in=ab_w_in, ab_gate_w2=ab_gate_w2,
                  ab_gate_b=ab_gate_b, ab_head_g=ab_head_g, ab_w_out=ab_w_out, cd_w_in=cd_w_in,
                  hy_short_w=hy_short_w, hy_short_b=hy_short_b, hy_w1=hy_w1, hy_b1=hy_b1,
                  hy_freq1=hy_freq1, hy_w2=hy_w2, hy_b2=hy_b2, hy_freq2=hy_freq2, hy_w3=hy_w3,
                  hy_b3=hy_b3, hy_skip=hy_skip, lru_conv_w=lru_conv_w, lru_conv_b=lru_conv_b,
                  lru_w_r=lru_w_r, lru_b_r=lru_b_r, lru_w_i=lru_w_i, lru_b_i=lru_b_i,
                  lru_lambda=lru_lambda, cd_w_out=cd_w_out, final_g=final_g)
    B = x_prompt.shape[0]
    f32 = jnp.float32
    zero_gla = jnp.zeros((B, N_AB, 2, H_A, DK_A, DV_A), f32)
    zero_ret = jnp.zeros((B, N_AB, 2, H_B, DK_B, DV_B), f32)
    zero_lru = jnp.zeros((B, N_CD, 2, W_BR), f32)
    y_prompt, gla_list, ret_list, lru_list = run_trunk(x_prompt, c_ctx[None, :], zero_gla, zero_ret,
                                                       zero_lru, params, True)
    new_state_gla = jnp.stack(gla_list, axis=1).astype(x_prompt.dtype)
    new_state_ret = jnp.stack(ret_list, axis=1).astype(x_prompt.dtype)
    new_state_lru = jnp.stack(lru_list, axis=1).astype(x_prompt.dtype)
    y_sample = run_trunk(x_sample, c, state_gla, state_ret, state_lru, params, False)[0]
    return (y_prompt, y_sample, new_state_gla, new_state_ret, new_state_lru)
```

```python
import os
import sys
import numpy as np
import ml_dtypes
import concourse.bass as bass
import concourse.mybir as mybir
from concourse.bass_utils import run_bass_kernel_spmd

F32 = mybir.dt.float32
BF16 = mybir.dt.bfloat16
U8 = mybir.dt.uint8
AF = mybir.ActivationFunctionType
ALU = mybir.AluOpType
DSIZE = {F32: 4, BF16: 2, U8: 1}

SAME_ENGINE_SYNC = True
N_DMA_SEMS = 48
STG_N = 768
N_STG = 4
CAST_ENG = "pool"


class Tok:
    __slots__ = ("sem", "val", "op", "key")

    def __init__(self, sem, val, op, key):
        self.sem, self.val, self.op, self.key = sem, val, op, key


class Op:
    __slots__ = ("eng", "fn", "waits", "tok", "is_dma", "needs_inc", "seq")


class Buf:
    def __init__(self, ap, name=""):
        self.ap = ap
        self.name = name
        self.writes = {}
        self.reads = {}

    def __getitem__(self, idx):
        return V(self, self.ap[idx])

    def v(self):
        return V(self, self.ap)


class V:
    __slots__ = ("buf", "ap")

    def __init__(self, buf, ap):
        self.buf, self.ap = buf, ap

    def __getitem__(self, idx):
        return V(self.buf, self.ap[idx])

    def bitcast(self, dt):
        return V(self.buf, self.ap.bitcast(dt))

    def rearrange(self, *a, **k):
        return V(self.buf, self.ap.rearrange(*a, **k))


class Sched:
    ENG = ("pe", "act", "dve", "pool", "sp")

    def __init__(self, nc):
        self.nc = nc
        self.q = {e: [] for e in self.ENG}
        self.esem = {e: nc.alloc_semaphore("es_" + e) for e in self.ENG}
        self.dsems = [nc.alloc_semaphore("ds%d" % i) for i in range(N_DMA_SEMS)]
        self.dval = [0] * N_DMA_SEMS
        self.dlast = [None] * N_DMA_SEMS
        self.dnext = 0
        self.last_tok = {e: None for e in self.ENG}
        self.all_dma = []
        self.sb_base = 16384 + 1024
        self.sb_top = 207 * 1024
        self.live = []
        self.dead = []
        self.opseq = 0
        self.sb_ptr = self.sb_base
        self.uid = 0
        self.sb_max = 0
        self.stg = [self.sbuf([128, STG_N], F32, "stg%d" % i) for i in range(N_STG)]
        self.stgi = 0

    def sbuf(self, shape, dt, name="t"):
        per = int(np.prod(shape[1:])) * DSIZE[dt]
        off = (self.sb_ptr + 63) // 64 * 64
        assert off + per <= self.sb_top, "SBUF overflow %s need %d at %d" % (name, per, off)
        self.sb_ptr = off + per
        self.sb_max = max(self.sb_max, self.sb_ptr)
        self.uid += 1
        t = self.nc.alloc_sbuf_tensor_at("%s_%d" % (name, self.uid), list(shape), dt, offset=off)
        b = Buf(t.ap(), name)
        end = off + per
        keep = []
        for (o2, e2, b2) in self.dead:
            if o2 < end and off < e2:
                for tk in list(b2.writes.values()) + list(b2.reads.values()):
                    old = b.writes.get(tk.key)
                    if old is None or old.op.seq < tk.op.seq:
                        b.writes[tk.key] = tk
                if not (off <= o2 and e2 <= end):
                    keep.append((o2, e2, b2))
            else:
                keep.append((o2, e2, b2))
        self.dead = keep
        self.live.append((off, end, b))
        return b

    def mark(self):
        return self.sb_ptr

    def release(self, m):
        self.sb_ptr = m
        nl = []
        for (o2, e2, b2) in self.live:
            if o2 >= m:
                if b2.writes or b2.reads:
                    self.dead.append((o2, e2, b2))
            else:
                nl.append((o2, e2, b2))
        self.live = nl

    def add(self, eng, fn, reads=(), writes=(), dma=False):
        lim = int(os.environ.get("DBG_LIMIT", "0"))
        self.nrec = getattr(self, "nrec", 0) + 1
        if lim and self.nrec > lim:
            return Tok(self.esem[eng], 0, None, ("x", eng))
        if lim and self.nrec == lim:
            f = sys._getframe(2)
            print("LAST OP #%d eng=%s line=%d / caller line=%d" % (self.nrec, eng, f.f_lineno, f.f_back.f_lineno))
        op = Op()
        op.eng, op.fn, op.is_dma, op.needs_inc = eng, fn, dma, False
        self.opseq += 1
        op.seq = self.opseq
        deps = {}

        def dep(t):
            if t is None:
                return
            deps[id(t)] = t

        wb = set(id(w.buf) for w in writes)
        for r in reads:
            for t in r.buf.writes.values():
                dep(t)
        for w in writes:
            for t in w.buf.writes.values():
                dep(t)
            for t in w.buf.reads.values():
                dep(t)
        if dma:
            i = self.dnext
            self.dnext = (self.dnext + 1) % N_DMA_SEMS
            dep(self.dlast[i])
            self.dval[i] += 16
            tok = Tok(self.dsems[i], self.dval[i], op, ("d", i))
            self.dlast[i] = tok
            self.all_dma.append(tok)
        else:
            tok = Tok(self.esem[eng], None, op, ("e", eng))
        op.tok = tok
        waits = []
        for t in deps.values():
            if (not t.op.is_dma) and t.op.eng == eng:
                if eng == "pe" or not SAME_ENGINE_SYNC:
                    continue
            if not t.op.is_dma:
                t.op.needs_inc = True
            waits.append(t)
        op.waits = waits
        for w in writes:
            w.buf.writes = {tok.key: tok}
            w.buf.reads = {}
        for r in reads:
            if id(r.buf) not in wb:
                r.buf.reads[tok.key] = tok
        self.q[eng].append(op)
        if not dma:
            self.last_tok[eng] = tok
        return tok

    def barrier(self):
        toks = [t for t in self.last_tok.values() if t is not None]
        toks += [t for t in self.dlast if t is not None]
        for e in self.ENG:
            op = Op()
            op.eng, op.fn, op.is_dma, op.needs_inc = e, None, False, False
            self.opseq += 1
            op.seq = self.opseq
            op.tok = Tok(self.esem[e], None, op, ("e", e))
            op.waits = []
            for t in toks:
                if (not t.op.is_dma) and t.op.eng == e:
                    continue
                if not t.op.is_dma:
                    t.op.needs_inc = True
                op.waits.append(t)
            self.q[e].append(op)

    def emit(self):
        nc = self.nc
        for e in self.ENG:
            c = 0
            for op in self.q[e]:
                if not op.is_dma and op.needs_inc:
                    c += 1
                if not op.is_dma:
                    op.tok.val = c
        engobj = {"pe": "tensor", "act": "scalar", "dve": "vector", "pool": "gpsimd", "sp": "sync"}
        self.stats = {}

        def replay(e, eng):
            seen = {}
            nw = 0
            dump = os.environ.get("DBG_DUMP") == e
            for oi, op in enumerate(self.q[e]):
                if dump and oi >= len(self.q[e]) - 12:
                    print("  [%s %d] fn=%s dma=%s inc=%s val=%s waits=%s" % (e, oi, "none" if op.fn is None else "op", op.is_dma, op.needs_inc, op.tok.val,
                          [(t.sem.name, t.val) for t in op.waits]))
                for t in op.waits:
                    k = t.sem.num
                    if seen.get(k, 0) >= t.val:
                        continue
                    seen[k] = t.val
                    eng.wait_ge(t.sem, t.val)
                    nw += 1
                if op.fn is None:
                    continue
                ins = op.fn(eng)
                if op.is_dma:
                    ins.then_inc(op.tok.sem, 16)
                elif op.needs_inc:
                    ins.then_inc(op.tok.sem, 1)
            self.stats[e] = (len(self.q[e]), nw)

        with nc.Block() as block:
            @block.tensor
            def _(eng):
                replay("pe", eng)

            @block.scalar
            def _(eng):
                replay("act", eng)

            @block.vector
            def _(eng):
                replay("dve", eng)

            @block.gpsimd
            def _(eng):
                replay("pool", eng)

            @block.sync
            def _(eng):
                replay("sp", eng)

    def dma(self, out, in_, eng="sp", **kw):
        if eng == "pool":
            return self.load_cast(out, in_)
        return self.add(eng, lambda e: e.dma_start(out=out.ap, in_=in_.ap, **kw),
                        reads=[in_], writes=[out], dma=True)

    def cast_eng(self):
        self.casti = getattr(self, "casti", 0) + 1
        return ("act", "dve")[self.casti % 2]

    def load_cast(self, dst, src):
        shp = list(dst.ap.shape)
        P = shp[0]
        tok = None
        if len(shp) == 2:
            n = shp[1]
            for c0 in range(0, n, STG_N):
                c1 = min(n, c0 + STG_N)
                st = self.stg[self.stgi % N_STG]
                self.stgi += 1
                self.dma(st[0:P, 0:c1 - c0], src[:, c0:c1])
                tok = self.copy(dst[:, c0:c1], st[0:P, 0:c1 - c0], eng=self.cast_eng())
        else:
            assert len(shp) == 3
            A, Bn = shp[1], shp[2]
            if Bn > STG_N:
                for a_ in range(A):
                    tok = self.load_cast(dst[:, a_, :], src[:, a_, :])
                return tok
            step = max(1, STG_N // Bn)
            for a0 in range(0, A, step):
                a1 = min(A, a0 + step)
                st = self.stg[self.stgi % N_STG]
                self.stgi += 1
                sv = st[0:P, 0:(a1 - a0) * Bn].rearrange("p (a b) -> p a b", b=Bn)
                self.dma(sv, src[:, a0:a1, :])
                tok = self.copy(dst[:, a0:a1, :], sv, eng=self.cast_eng())
        return tok

    def mm(self, out, lhsT, rhs, start=True, stop=True):
        return self.add("pe", lambda e: e.matmul(out.ap, lhsT.ap, rhs.ap, start=start, stop=stop),
                        reads=[lhsT, rhs], writes=[out])

    def transpose(self, out, in_, ident):
        return self.add("pe", lambda e: e.transpose(out.ap, in_.ap, ident.ap),
                        reads=[in_, ident], writes=[out])

    def act(self, out, in_, func, bias=None, scale=None, accum_out=None, eng="act"):
        reads = [in_]
        kw = {}
        if bias is not None:
            if isinstance(bias, V):
                reads.append(bias)
                kw["bias"] = bias.ap
            else:
                kw["bias"] = bias
        if scale is not None:
            if isinstance(scale, V):
                reads.append(scale)
                kw["scale"] = scale.ap
            else:
                kw["scale"] = scale
        writes = [out]
        if accum_out is not None:
            writes.append(accum_out)
            kw["accum_out"] = accum_out.ap
        return self.add(eng, lambda e: e.activation(out.ap, in_.ap, func, **kw), reads=reads, writes=writes)

    def tt(self, out, in0, in1, op, eng="dve"):
        return self.add(eng, lambda e: e.tensor_tensor(out.ap, in0.ap, in1.ap, op), reads=[in0, in1], writes=[out])

    def ts(self, out, in0, s1, s2, op0, op1=None, eng="dve"):
        reads = [in0]
        a1 = s1.ap if isinstance(s1, V) else s1
        a2 = s2.ap if isinstance(s2, V) else s2
        if isinstance(s1, V):
            reads.append(s1)
        if isinstance(s2, V):
            reads.append(s2)
        if op1 is None:
            return self.add(eng, lambda e: e.tensor_scalar(out.ap, in0.ap, a1, a2, op0), reads=reads, writes=[out])
        return self.add(eng, lambda e: e.tensor_scalar(out.ap, in0.ap, a1, a2, op0, op1), reads=reads, writes=[out])

    def stt(self, out, in0, scalar, in1, op0, op1):
        reads = [in0, in1]
        a = scalar.ap if isinstance(scalar, V) else scalar
        if isinstance(scalar, V):
            reads.append(scalar)
        return self.add("dve", lambda e: e.scalar_tensor_tensor(out.ap, in0.ap, a, in1.ap, op0, op1),
                        reads=reads, writes=[out])

    def scan(self, out, d0, d1, init, op0=None, op1=None):
        reads = [d0, d1]
        a = init.ap if isinstance(init, V) else init
        if isinstance(init, V):
            reads.append(init)
        o0 = op0 or ALU.mult
        o1 = op1 or ALU.add
        return self.add("dve", lambda e: e.tensor_tensor_scan(out.ap, d0.ap, d1.ap, a, o0, o1),
                        reads=reads, writes=[out])

    def copy(self, out, in_, eng="dve"):
        if eng == "act":
            return self.add("act", lambda e: e.copy(out.ap, in_.ap), reads=[in_], writes=[out])
        return self.add(eng, lambda e: e.tensor_copy(out.ap, in_.ap), reads=[in_], writes=[out])

    def memset(self, out, val, eng="dve"):
        return self.add(eng, lambda e: e.memset(out.ap, val), reads=[], writes=[out])

    def bn_stats(self, out, in_):
        return self.add("dve", lambda e: e.bn_stats(out.ap, in_.ap), reads=[in_], writes=[out])

    def bn_aggr(self, out, in_):
        return self.add("dve", lambda e: e.bn_aggr(out.ap, in_.ap), reads=[in_], writes=[out])

    def recip(self, out, in_):
        return self.add("dve", lambda e: e.reciprocal(out.ap, in_.ap), reads=[in_], writes=[out])

import math

D = 1024
EPS = 1e-6
NP_SEQ = 4
LP = 256
LS = 2048
TP = NP_SEQ * LP
TTOT = TP + LS
CW = 256
HY_DELTA = np.abs(np.linspace(math.log(1e-2) / 0.3, math.log(1e-2) / 1.5, 1024, dtype=np.float32)).astype(np.float32)

FM = {}
_o = 0
for _n, _w in [("norm_g", 32), ("hsw", 144), ("hsb", 48), ("lcw", 64), ("lcb", 16), ("lbr", 32),
               ("lbi", 32), ("llam", 32), ("stl", 32)]:
    FM[_n] = _o
    _o += _w
FM_W = _o


def _bf(a):
    return np.ascontiguousarray(a).astype(ml_dtypes.bfloat16)


def make_consts():
    c = {}
    c["ident"] = np.eye(128, dtype=np.float32)
    tp = np.arange(128)[:, None]
    t = np.arange(128)[None, :]
    tri = np.zeros((128, 4, 128), np.float32)
    tri[:, 0] = (tp <= t) * (-1.0 / 16)
    tri[:, 1] = (tp >= t) * (-1.0 / 16)
    tri[:, 2] = (tp > t) * (-1.0 / 16)
    tri[:, 3] = (tp < t) * (-1.0 / 16)
    c["tri"] = tri
    mask = np.zeros((128, 2, 128), np.float32)
    mask[:, 0] = (tp <= t)
    mask[:, 1] = (tp >= t)
    c["mask"] = mask
    step = (12.0 - 5.0) / 3.0
    rsp = np.zeros((128, 4, 2, 128), np.float32)
    for h in range(4):
        for d in range(2):
            expo = np.float32(5.0 + step * (h + 0.5 * d))
            lg = np.log1p(-np.exp2(-np.float64(expo)))
            rsp[:, h, d, :] = -16.0 * lg
    c["retsp"] = rsp
    c["delta"] = HY_DELTA[None, :].copy()
    for L in (LP, LS):
        nT = L // 128
        nft = 2 * nT + 1
        tt = np.arange(L, dtype=np.float64)
        FW = np.zeros((nft, L, 128), np.float64)
        IV = np.zeros((nft, 128, L), np.float64)
        for j in range(nT):
            f = (np.arange(128) + 128 * j).astype(np.float64)
            th = np.pi * np.outer(tt, f) / L
            FW[j] = np.cos(th)
            FW[nT + 1 + j] = -np.sin(th)
            w = np.where(f == 0, 1.0, 2.0)
            IV[j] = (w[:, None] * np.cos(th.T)) / (2 * L)
            IV[nT + 1 + j] = (-2.0 * np.sin(th.T)) / (2 * L)
        FW[nT, :, 0] = np.cos(np.pi * tt)
        IV[nT, 0, :] = np.cos(np.pi * tt) / (2 * L)
        c["fw%d" % L] = _bf(FW.reshape(nft, nT, 128, 128).transpose(0, 2, 1, 3))
        c["iv%d" % L] = _bf(IV)
        tl = np.linspace(0.0, 1.0, L, dtype=np.float32)[:, None]
        bands = 16
        fb = np.linspace(1e-4, bands - 1, bands, dtype=np.float32)[None, :]
        ang = (np.float32(2.0 * math.pi / L) * np.arange(L, dtype=np.float32)[:, None] * fb).astype(np.float32)
        feats = np.concatenate([tl, np.cos(ang), -np.sin(ang)], axis=-1).astype(np.float32)
        c["feats%d" % L] = np.ascontiguousarray(feats.T)
        c["tcol%d" % L] = np.ascontiguousarray((-tl[:, 0]).reshape(nT, 128).T).astype(np.float32)
    return c


def prep_weights(inp):
    w = {}
    g = lambda k: np.asarray(inp[k], dtype=np.float32)
    w["w_mod"] = g("w_mod")
    w["b_mod"] = g("b_mod")
    abw = g("ab_w_in")
    offs = dict(qa=0, ka=512, va=1024, ga=2048, za=2080, qb=3104, kb=3616, vb=4128, zb=5152)
    wh = np.zeros((2, 8, 128, 8, 768), np.float32)
    wga = np.zeros((2, 128, 8, 32), np.float32)
    for i in range(2):
        for hh in range(8):
            h = hh % 4
            if hh < 4:
                cols = [abw[i][:, offs["qa"] + h * 128: offs["qa"] + (h + 1) * 128],
                        abw[i][:, offs["ka"] + h * 128: offs["ka"] + (h + 1) * 128],
                        abw[i][:, offs["va"] + h * 256: offs["va"] + (h + 1) * 256],
                        abw[i][:, offs["za"] + h * 256: offs["za"] + (h + 1) * 256]]
            else:
                cols = [abw[i][:, offs["qb"] + h * 128: offs["qb"] + (h + 1) * 128],
                        abw[i][:, offs["kb"] + h * 128: offs["kb"] + (h + 1) * 128],
                        abw[i][:, offs["vb"] + h * 256: offs["vb"] + (h + 1) * 256],
                        abw[i][:, offs["zb"] + h * 256: offs["zb"] + (h + 1) * 256]]
            m = np.concatenate(cols, axis=1)
            wh[i, hh] = m.reshape(8, 128, 768).transpose(1, 0, 2)
        wga[i] = abw[i][:, 2048:2080].reshape(8, 128, 32).transpose(1, 0, 2)
    w["ab_wh"] = wh.reshape(2, 8, 128, 8 * 768)
    w["ab_wga"] = wga.reshape(2, 128, 8 * 32)
    gw2 = g("ab_gate_w2")
    gb = g("ab_gate_b")
    w2b = np.zeros((2, 33, 1024), np.float32)
    for i in range(2):
        w2b[i, 0:16, 0:512] = gw2[i, 0]
        w2b[i, 16:32, 512:1024] = gw2[i, 1]
        w2b[i, 32, 0:512] = gb[i, 0]
        w2b[i, 32, 512:1024] = gb[i, 1]
    w["ab_w2b"] = w2b
    w["ab_head_g"] = g("ab_head_g")
    w["ab_w_out"] = np.ascontiguousarray(g("ab_w_out").reshape(2, 16, 128, 1024).transpose(0, 2, 1, 3)).reshape(2, 128, 16 * 1024)
    w["cd_w_out"] = np.ascontiguousarray(g("cd_w_out").reshape(2, 16, 128, 1024).transpose(0, 2, 1, 3)).reshape(2, 128, 16 * 1024)
    cdw = g("cd_w_in")
    wg = np.zeros((2, 8, 128, 8, 768), np.float32)
    for i in range(2):
        for gg in range(8):
            cols = [cdw[i][:, s * 1024 + gg * 128: s * 1024 + (gg + 1) * 128] for s in range(6)]
            m = np.concatenate(cols, axis=1)
            wg[i, gg] = m.reshape(8, 128, 768).transpose(1, 0, 2)
    w["cd_wg"] = wg.reshape(2, 8, 128, 8 * 768)
    fm = np.zeros((128, FM_W), np.float32)
    fm[:, FM["norm_g"]:FM["norm_g"] + 32] = g("norm_g").reshape(4, 8, 128).transpose(2, 0, 1).reshape(128, 32)
    hsw = g("hy_short_w")
    fm[:, FM["hsw"]:FM["hsw"] + 144] = hsw.reshape(2, 3, 3, 8, 128).transpose(4, 0, 2, 3, 1).reshape(128, 144)
    fm[:, FM["hsb"]:FM["hsb"] + 48] = g("hy_short_b").reshape(2, 3, 8, 128).transpose(3, 0, 1, 2).reshape(128, 48)
    fm[:, FM["lcw"]:FM["lcw"] + 64] = g("lru_conv_w").reshape(2, 4, 8, 128).transpose(3, 0, 2, 1).reshape(128, 64)
    fm[:, FM["lcb"]:FM["lcb"] + 16] = g("lru_conv_b").reshape(2, 8, 128).transpose(2, 0, 1).reshape(128, 16)
    for nm, key in (("lbr", "lru_b_r"), ("lbi", "lru_b_i"), ("llam", "lru_lambda")):
        fm[:, FM[nm]:FM[nm] + 32] = g(key).reshape(2, 2, 8, 128).transpose(3, 0, 1, 2).reshape(128, 32)
    w["fm"] = fm
    w["hy_skip"] = g("hy_skip")
    w["lru_w_r"] = g("lru_w_r")
    w["lru_w_i"] = g("lru_w_i")
    w["hy_w1"] = g("hy_w1")
    w["hy_w2"] = g("hy_w2")
    hyp = np.zeros((2, 64, 4), np.float32)
    hyp[:, :, 0] = g("hy_b1")
    hyp[:, :, 1] = g("hy_freq1")
    hyp[:, :, 2] = g("hy_b2")
    hyp[:, :, 3] = g("hy_freq2")
    w["hyp"] = hyp
    w["w3aug"] = np.concatenate([g("hy_w3"), g("hy_b3")[:, None, :]], axis=1)
    w["final_g"] = g("final_g").reshape(1, 1024)
    return w


DRAM_IN_SPECS = None


class Prog:
    def __init__(self, n_layers=4, groups=("P", "S")):
        self.n_layers = n_layers
        self.groups = groups
        nc = bass.Bass("TRN2", target_bir_lowering=False)
        self.nc = nc
        self.S = Sched(nc)
        self.din = {}
        self.dout = {}

    def inp(self, name, shape, dt=F32):
        ap = self.nc.dram_tensor(name, list(shape), dt, kind="ExternalInput").ap()
        self.din[name] = Buf(ap, name)
        return self.din[name]

    def outp(self, name, shape):
        ap = self.nc.dram_tensor(name, list(shape), F32, kind="ExternalOutput").ap()
        self.dout[name] = ap
        return ap

    def ps(self):
        b = self.PS[self.ps_i % len(self.PS)]
        self.ps_i += 1
        return b

    def declare(self, consts, wts):
        for k, v in consts.items():
            self.inp("c_" + k, v.shape, BF16 if v.dtype == ml_dtypes.bfloat16 else F32)
        for k, v in wts.items():
            self.inp("w_" + k, v.shape, F32)
        self.inp("xp", (TP, D))
        self.inp("xs", (LS, D))
        self.inp("condT", (128, 16))
        self.inp("st_gla", (2, 2, 4, 128, 256))
        self.inp("st_ret", (2, 2, 4, 128, 256))
        self.outp("y_p", (TP, D))
        self.outp("y_s", (LS, D))
        self.outp("ns_gla", (NP_SEQ, 2, 2, 4, 128, 256))
        self.outp("ns_ret", (NP_SEQ, 2, 2, 4, 128, 256))
        self.outp("ns_lru", (NP_SEQ, 2, 2, 1024))
        nc = self.nc
        self.Xd = nc.dram_tensor("x_scr", [TTOT, D], F32, kind="Internal").ap()
        self.X = [Buf(self.Xd[t * 128:(t + 1) * 128, :], "x%d" % t) for t in range(TTOT // 128)]
        self.YTd = nc.dram_tensor("yt_scr", [2048, TTOT], BF16, kind="Internal").ap()
        self.YT = [[Buf(self.YTd[e * 128:(e + 1) * 128, b * 512:(b + 1) * 512], "yt") for b in range(TTOT // 512)]
                   for e in range(16)]
        self.MODd = nc.dram_tensor("mod_scr", [4, 2, 1024], F32, kind="Internal").ap()
        self.MOD = Buf(self.MODd, "modscr")
        self.YP = [Buf(self.dout["y_p"][t * 128:(t + 1) * 128, :]) for t in range(TP // 128)]
        self.YS = [Buf(self.dout["y_s"][t * 128:(t + 1) * 128, :]) for t in range(LS // 128)]
        self.OUTB = Buf(self.dout["ns_gla"], "o")
        self.PS = [Buf(nc.alloc_psum_tensor("ps%d" % i, [128, 512], F32).ap(), "ps%d" % i) for i in range(8)]
        self.ps_i = 0

    def build(self):
        S = self.S
        d = self.din
        self.ident = S.sbuf([128, 128], BF16, "ident")
        S.dma(self.ident.v(), d["c_ident"].v(), eng="pool")
        self.halfpi = S.sbuf([128, 1], F32, "halfpi")
        S.memset(self.halfpi.v(), math.pi / 2)
        self.fm = S.sbuf([128, FM_W], F32, "fm")
        S.dma(self.fm.v(), d["w_fm"].v())
        self.zeros = S.sbuf([128, 16], F32, "zeros")
        S.memset(self.zeros.v(), 0.0)
        self.modT = S.sbuf([128, 4, 2, 16], F32, "modT")
        self.gs = S.sbuf([128, 4, 2, 8], F32, "gs")
        self.prologue()
        S.barrier()
        base_mark = S.mark()
        for layer in range(self.n_layers):
            for gname in self.groups:
                if gname == "P":
                    grp = dict(name="P", tok0=0, T=TP, cond=0,
                               seqs=[dict(o=j * LP, L=LP, b=j, init=False) for j in range(NP_SEQ)])
                else:
                    grp = dict(name="S", tok0=TP, T=LS, cond=1, seqs=[dict(o=0, L=LS, b=None, init=True)])
                S.release(base_mark)
                self.phase_a(layer, grp)
                if layer % 2 == 0:
                    self.ab_layer(layer // 2, layer, grp)
                else:
                    self.cd_layer(layer // 2, layer, grp)
                S.release(self.mix_mark)
                self.phase_c(layer, grp, last=(layer == self.n_layers - 1))
        S.barrier()
        S.emit()

    def prologue(self):
        S = self.S
        d = self.din
        m0 = S.mark()
        condT = S.sbuf([128, 8, 2], F32, "condT")
        S.dma(condT.v(), d["condT"].v().rearrange("p (k c) -> p k c", c=2))
        scT = S.sbuf([128, 8, 2], F32, "scT")
        S.act(scT.v(), condT.v(), AF.Silu)
        wm = [S.sbuf([128, 8, 512], F32, "wm%d" % i) for i in range(4)]
        bm = S.sbuf([2, 3072], F32, "bm")
        modrow = S.sbuf([2, 3072], F32, "modrow")
        id2 = S.sbuf([2, 2], F32, "id2")
        S.dma(id2.v(), d["c_ident"][0:2, 0:2])
        cnt = 0
        for l in range(self.n_layers):
            S.dma(bm[0:1, :], d["w_b_mod"][l:l + 1, :])
            S.dma(bm[1:2, :], d["w_b_mod"][l:l + 1, :])
            for nb in range(6):
                w = wm[cnt % 4]
                cnt += 1
                S.dma(w.v(), d["w_w_mod"][l, :, nb * 512:(nb + 1) * 512].rearrange("(k p) n -> p k n", p=128))
                p = self.ps()
                for k in range(8):
                    S.mm(p[0:2, :], scT[:, k, :], w[:, k, :], start=(k == 0), stop=(k == 7))
                S.tt(modrow[:, nb * 512:(nb + 1) * 512], p[0:2, :], bm[:, nb * 512:(nb + 1) * 512], ALU.add)
            S.dma(V(self.MOD, self.MODd[l]), modrow[:, 2048:3072])
            p = self.ps()
            for j in range(16):
                S.transpose(p[:, j * 2:(j + 1) * 2], modrow[:, j * 128:(j + 1) * 128], id2.v())
            for c in range(2):
                S.copy(self.modT[:, l, c, :], p[:, 0:32].rearrange("p (j c) -> p c j", c=2)[:, c, :], eng="dve")
                S.ts(self.gs[:, l, c, :], self.modT[:, l, c, 8:16], 1.0, None, ALU.add)
                S.tt(self.gs[:, l, c, :], self.gs[:, l, c, :], self.fm[:, FM["norm_g"] + l * 8: FM["norm_g"] + (l + 1) * 8], ALU.mult)
        S.barrier()
        S.release(m0)

    def xsrc(self, layer, grp, ti):
        if layer == 0:
            src = self.din["xp"] if grp["name"] == "P" else self.din["xs"]
            return src[ti * 128:(ti + 1) * 128, :]
        return self.X[grp["tok0"] // 128 + ti].v()

    def hT_cols(self, gtok, n):
        blk = gtok // 512
        c0 = gtok % 512
        assert c0 + n <= 512
        return blk, slice(c0, c0 + n)

    def phase_a(self, layer, grp):
        S = self.S
        T = grp["T"]
        c = grp["cond"]
        self.hT = [S.sbuf([128, 8, 512], BF16, "hT%d" % b) for b in range(T // 512)]
        self.wout_pre = None
        if False and grp["name"] == "P":
            i_ = layer // 2
            wname_ = "w_ab_w_out" if layer % 2 == 0 else "w_cd_w_out"
            self.wout_pre = S.sbuf([128, 16, 1024], BF16, "woutp")
        self.mix_mark = S.mark()
        m0 = S.mark()
        xt = [S.sbuf([128, 1024], F32, "xt%d" % i) for i in range(2)]
        xn = [S.sbuf([128, 1024], BF16, "xn%d" % i) for i in range(2)]
        junk = S.sbuf([128, 1024], BF16, "junk")
        ss = [S.sbuf([128, 1], F32, "ss%d" % i) for i in range(2)]
        nTt = T // 128

        def stage1(ti):
            xb = xt[ti % 2]
            S.dma(xb.v(), self.xsrc(layer, grp, ti))
            s_ = ss[ti % 2]
            S.act(junk.v(), xb.v(), AF.Square, accum_out=s_.v())
            S.act(s_.v(), s_.v(), AF.Sqrt, bias=EPS, scale=1.0 / D)
            S.recip(s_.v(), s_.v())
            S.ts(xn[ti % 2].v(), xb.v(), s_.v(), None, ALU.mult)

        def stage2(ti):
            xnb = xn[ti % 2]
            pst = self.ps().v().bitcast(BF16)
            for k in range(8):
                S.transpose(pst[:, k * 128:(k + 1) * 128], xnb[:, k * 128:(k + 1) * 128], self.ident.v())
            blk, cs = self.hT_cols(ti * 128, 128)
            for k in range(8):
                if ti % 2 == 0:
                    S.act(self.hT[blk][:, k, cs], pst[:, k * 128:(k + 1) * 128], AF.Identity,
                          scale=self.gs[:, layer, c, k:k + 1], bias=self.modT[:, layer, c, k:k + 1])
                else:
                    S.ts(self.hT[blk][:, k, cs], pst[:, k * 128:(k + 1) * 128],
                         self.gs[:, layer, c, k:k + 1], self.modT[:, layer, c, k:k + 1], ALU.mult, ALU.add)

        for ti in range(nTt + 1):
            if ti < nTt:
                stage1(ti)
            if ti >= 1:
                stage2(ti - 1)
        if self.wout_pre is not None:
            for q in range(4):
                S.dma(self.wout_pre[:, q * 4:(q + 1) * 4, :],
                      self.din[wname_][i_, :, q * 4096:(q + 1) * 4096].rearrange("p (e n) -> p e n", n=1024), eng="pool")
        S.release(m0)

    def phase_c(self, layer, grp, last):
        S = self.S
        d = self.din
        T = grp["T"]
        c = grp["cond"]
        i = layer // 2
        wname = "w_ab_w_out" if layer % 2 == 0 else "w_cd_w_out"
        if self.wout_pre is not None:
            wout = self.wout_pre
        else:
            wout = S.sbuf([128, 16, 1024], BF16, "wout")
            wst = [S.sbuf([128, 4096], F32, "wst%d" % j) for j in range(2)]
            for q in range(4):
                S.dma(wst[q % 2].v(), d[wname][i, :, q * 4096:(q + 1) * 4096])
                S.copy(wout[:, q * 4:(q + 1) * 4, :].rearrange("p e n -> p (e n)"), wst[q % 2].v(),
                       eng=("act" if q % 2 == 0 else "dve"))
        gate = S.sbuf([128, 1024], F32, "gate")
        S.dma(gate.v(), V(self.MOD, self.MODd[layer, c, :].partition_broadcast(128)))
        if last:
            fg = S.sbuf([128, 1024], F32, "fg")
            S.dma(fg.v(), V(d["w_final_g"], d["w_final_g"].ap[0, :].partition_broadcast(128)))
            junk = S.sbuf([128, 1024], BF16, "junkc")
        ytb = [S.sbuf([128, 16, 512], BF16, "ytb%d" % j) for j in range(2)]
        xt = [S.sbuf([128, 1024], F32, "xtc%d" % j) for j in range(2)]
        tmp = [S.sbuf([128, 1024], F32, "tmpc%d" % j) for j in range(2)]
        ss = [S.sbuf([128, 1], F32, "ssc%d" % j) for j in range(2)]
        nblk = T // 512

        def load_yt(blk_):
            gb_ = (grp["tok0"] + blk_ * 512) // 512
            yb_ = ytb[blk_ % 2]
            for e in range(16):
                S.dma(yb_[:, e, :], self.YT[e][gb_].v())

        def load_x(ti_):
            S.dma(xt[ti_ % 2].v(), self.xsrc(layer, grp, ti_))

        load_yt(0)
        load_x(0)
        for blk in range(nblk):
            yb = ytb[blk % 2]
            if blk + 1 < nblk:
                load_yt(blk + 1)
            for tt_ in range(4):
                ti = blk * 4 + tt_
                xi = grp["tok0"] // 128 + ti
                xb = xt[ti % 2]
                if ti + 1 < nblk * 4:
                    load_x(ti + 1)
                tb = tmp[ti % 2]
                pp = [self.ps(), self.ps()]
                for e in range(16):
                    for nb in range(2):
                        S.mm(pp[nb].v(), yb[:, e, tt_ * 128:(tt_ + 1) * 128], wout[:, e, nb * 512:(nb + 1) * 512],
                             start=(e == 0), stop=(e == 15))
                for nb in range(2):
                    S.tt(tb[:, nb * 512:(nb + 1) * 512], pp[nb].v(), gate[:, nb * 512:(nb + 1) * 512], ALU.mult)
                S.tt(xb.v(), tb.v(), xb.v(), ALU.add)
                if not last:
                    S.dma(self.X[xi].v(), xb.v())
                else:
                    s_ = ss[ti % 2]
                    S.act(junk.v(), xb.v(), AF.Square, accum_out=s_.v())
                    S.act(s_.v(), s_.v(), AF.Sqrt, bias=EPS, scale=1.0 / D)
                    S.recip(s_.v(), s_.v())
                    S.stt(tb.v(), xb.v(), s_.v(), fg.v(), ALU.mult, ALU.mult)
                    ob = self.YP[ti] if grp["name"] == "P" else self.YS[ti]
                    S.dma(ob.v(), tb.v())

    def ab_layer(self, i, layer, grp):
        S = self.S
        d = self.din
        T = grp["T"]
        bs = 512
        self.trif = S.sbuf([128, 4, 128], F32, "trif")
        S.dma(self.trif.v(), d["c_tri"].v())
        self.trib = S.sbuf([128, 4, 128], BF16, "trib")
        S.dma(self.trib.v(), d["c_tri"].v(), eng="pool")
        self.mask = S.sbuf([128, 2, 128], F32, "mask")
        S.dma(self.mask.v(), d["c_mask"].v())
        self.retsp = S.sbuf([128, 4, 2, 128], F32, "retsp")
        S.dma(self.retsp.v(), d["c_retsp"].v())
        wga = S.sbuf([128, 8, 32], BF16, "wga")
        S.dma(wga.v(), d["w_ab_wga"][i].rearrange("p (k n) -> p k n", n=32), eng="pool")
        w2b = S.sbuf([33, 1024], BF16, "w2b")
        S.dma(w2b.v(), d["w_ab_w2b"][i], eng="pool")
        gaT = S.sbuf([33, T], BF16, "gaT")
        S.memset(gaT[32:33, :], 1.0)
        for blk in range(T // bs):
            p = self.ps()
            for k in range(8):
                S.mm(p[0:32, :], wga[:, k, :], self.hT[blk][:, k, :], start=(k == 0), stop=(k == 7))
            S.copy(gaT[0:32, blk * bs:(blk + 1) * bs], p[0:32, :], eng="act")
        NT = T // 128
        B = {}
        B["wh"] = [S.sbuf([128, 8, 768], BF16, "wh%d" % j) for j in range(2)]
        B["gbc"] = [S.sbuf([128, 256], F32, "gbc%d" % j) for j in range(2)]
        PSET = 2 if T <= 1024 else 1
        B["qT"] = [S.sbuf([128, T], BF16, "qT%d" % j) for j in range(PSET)]
        B["kT"] = [S.sbuf([128, T], BF16, "kT%d" % j) for j in range(PSET)]
        B["ktm"] = [S.sbuf([128, NT, 128], BF16, "ktm%d" % j) for j in range(PSET)]
        B["vtm"] = [S.sbuf([128, NT, 256], BF16, "vtm%d" % j) for j in range(PSET)]
        B["gz"] = [S.sbuf([128, NT, 256], BF16, "gz%d" % j) for j in range(PSET)]
        B["sp"] = [S.sbuf([128, NT, 256], BF16, "sp%d" % j) for j in range(PSET)]
        B["qinb"] = S.sbuf([128, NT, 128], BF16, "qinb")
        B["attb"] = S.sbuf([128, NT, 128], BF16, "attb")
        B["Sbb"] = S.sbuf([128, NT, 256], BF16, "Sbb")
        B["osb"] = S.sbuf([128, NT, 256], F32, "osb")
        B["yTs"] = S.sbuf([128, 2, T], BF16, "yTs")
        B["stats"] = S.sbuf([128, NT, 6], F32, "stats")
        B["mv"] = S.sbuf([128, NT, 2], F32, "mv")
        B["ms"] = S.sbuf([128, NT], F32, "ms")
        B["rstd"] = S.sbuf([128, NT], F32, "rstd")
        DP = 4
        B["etmp"] = [S.sbuf([128, 256], F32, "etmp%d" % j) for j in range(2)]
        B["sz"] = [S.sbuf([128, 256], F32, "sz%d" % j) for j in range(2)]
        B["E"] = [S.sbuf([128, 3, 128], F32, "E%d" % j) for j in range(DP)]
        B["Ec"] = [S.sbuf([128, 3, 128], F32, "Ec%d" % j) for j in range(2)]
        B["qin"] = [S.sbuf([128, 128], BF16, "qin%d" % j) for j in range(DP)]
        B["kin"] = [S.sbuf([128, 128], BF16, "kin%d" % j) for j in range(DP)]
        B["kst"] = [S.sbuf([128, 128], BF16, "kst%d" % j) for j in range(DP)]
        B["att"] = [S.sbuf([128, 128], BF16, "att%d" % j) for j in range(DP)]
        B["Sf"] = [S.sbuf([128, 256], F32, "Sf%d" % j) for j in range(2)]
        B["Sb16"] = [S.sbuf([128, 256], BF16, "Sb16%d" % j) for j in range(2)]
        B["tn"] = [S.sbuf([128, 256], F32, "tn%d" % j) for j in range(2)]
        B["ybf"] = [S.sbuf([128, 256], BF16, "ybf%d" % j) for j in range(2)]
        heads = list(getattr(self, "dbg_heads", range(8)))

        def load_head(hh, slot):
            S.dma(B["wh"][slot].v(), d["w_ab_wh"][i, hh].rearrange("p (k n) -> p k n", n=768), eng="pool")
            S.dma(B["gbc"][slot].v(), V(d["w_ab_head_g"], d["w_ab_head_g"].ap[i, hh * 256:(hh + 1) * 256].partition_broadcast(128)))
            S.ts(B["gbc"][slot].v(), B["gbc"][slot].v(), 0.5, None, ALU.mult)

        if heads:
            load_head(heads[0], 0)
        for hi, hh in enumerate(heads):
            ret = hh >= 4
            h = hh % 4
            if hi + 1 < len(heads):
                load_head(heads[hi + 1], (hi + 1) % 2)
            if ret:
                for dr in range(2):
                    p = self.ps()
                    spc = self.retsp[:, h, dr, :]
                    S.mm(p[:, 0:128], spc, self.trif[:, dr, :])
                    S.mm(p[:, 128:256], self.trif[:, 2 + dr, :], spc)
                    Ec = B["Ec"][dr]
                    S.act(Ec[:, 0, :], p[:, 0:128], AF.Exp)
                    S.act(Ec[:, 1, :], p[:, 0:128], AF.Exp, scale=-1.0)
                    S.act(Ec[:, 2, :], p[:, 128:256], AF.Exp)
            Bs = dict(B)
            for k_ in ("qT", "kT", "ktm", "vtm", "gz", "sp"):
                Bs[k_] = B[k_][hi % PSET]
            self.ab_head(i, hh, grp, Bs, gaT, w2b, B["wh"][hi % 2], B["gbc"][hi % 2], DP)

    def ab_head(self, i, hh, grp, B, gaT, w2b, wh, gbc, DP):
        S = self.S
        d = self.din
        ret = hh >= 4
        h = hh % 4
        sc = 128.0 ** -0.5
        sc_q, sc_k = (1.0, sc) if ret else (sc, 1.0)
        for sq in grp["seqs"]:
            L = sq["L"]
            o = sq["o"]
            nT = L // 128
            sub = min(512, L)
            for sb_ in range(L // sub):
                blk, cs = self.hT_cols(o + sb_ * sub, sub)
                for (dst, c0, scl) in ((B["qT"], 0, sc_q), (B["kT"], 128, sc_k)):
                    p = self.ps()
                    for k in range(8):
                        S.mm(p[:, 0:sub], wh[:, k, c0:c0 + 128], self.hT[blk][:, k, cs], start=(k == 0), stop=(k == 7))
                    S.act(dst[:, o + sb_ * sub:o + (sb_ + 1) * sub], p[:, 0:sub], AF.Copy, scale=scl)
            for n in range(nT):
                gc = o // 128 + n
                blk, cs = self.hT_cols(o + n * 128, 128)
                pA = self.ps()
                pB = self.ps()
                for k in range(8):
                    S.mm(pA[:, 0:384], self.hT[blk][:, k, cs], wh[:, k, 128:512], start=(k == 0), stop=(k == 7))
                    S.mm(pB[:, 0:256], self.hT[blk][:, k, cs], wh[:, k, 512:768], start=(k == 0), stop=(k == 7))
                S.ts(B["ktm"][:, gc, :], pA[:, 0:128], sc_k, None, ALU.mult)
                S.ts(B["vtm"][:, gc, :], pA[:, 128:384], 1.0, None, ALU.mult)
                sz = B["sz"][gc % 2]
                zc = B["etmp"][gc % 2]
                S.act(sz.v(), pB[:, 0:256], AF.Tanh, scale=0.5)
                S.act(zc.v(), pB[:, 0:256], AF.Copy)
                S.stt(sz.v(), sz.v(), 1.0, zc.v(), ALU.add, ALU.mult)
                S.tt(B["gz"][:, gc, :], sz.v(), gbc.v(), ALU.mult, eng="pool")
                if not ret:
                    p = self.ps()
                    gt = gaT[:, o + n * 128: o + (n + 1) * 128]
                    S.mm(p[:, 0:128], gt, w2b[:, h * 128:(h + 1) * 128])
                    S.mm(p[:, 128:256], gt, w2b[:, 512 + h * 128: 512 + (h + 1) * 128])
                    S.act(B["sp"][:, gc, :], p[:, 0:256], AF.Exp, scale=-1.0)
        if not ret:
            spf = B["sp"].v().rearrange("p a c -> p (a c)")
            S.act(spf, spf, AF.Ln, bias=1.0)
        if getattr(self, "dbg_stage", 9) < 2:
            return
        for dr in (1, 0):
            its = []
            for sq in grp["seqs"]:
                nT = sq["L"] // 128
                order = list(range(nT - 1, -1, -1)) if dr == 1 else list(range(nT))
                for j_, n in enumerate(order):
                    its.append(dict(sq=sq, n=n, gc=sq["o"] // 128 + n, first=(j_ == 0), last=(j_ == nT - 1)))
            NI = len(its)
            st = dict(cur=0, p2={})
            Sf = B["Sf"]

            def E_of(k):
                return B["Ec"][dr] if ret else B["E"][k % DP]

            def qin_of(k):
                return B["qinb"][:, its[k]["gc"], :] if dr == 1 else B["qin"][k % DP].v()

            def att_of(k):
                return B["attb"][:, its[k]["gc"], :] if dr == 1 else B["att"][k % DP].v()

            def st1(k):
                if ret:
                    return
                gc = its[k]["gc"]
                E = B["E"][k % DP]
                p = self.ps()
                spn = B["sp"][:, gc, dr * 128:(dr + 1) * 128]
                S.mm(p[:, 0:128], spn, self.trib[:, dr, :])
                S.mm(p[:, 128:256], self.trib[:, 2 + dr, :], spn)
                S.act(E[:, 0, :], p[:, 0:128], AF.Exp)
                S.act(E[:, 1, :], p[:, 0:128], AF.Exp, scale=-1.0)
                S.act(E[:, 2, :], p[:, 128:256], AF.Exp)

            def st2(k):
                it = its[k]
                gc = it["gc"]
                E = E_of(k)
                S.tt(qin_of(k), B["qT"][:, gc * 128:(gc + 1) * 128], E[:, 0, :], ALU.mult)
                S.tt(B["kin"][k % DP].v(), B["kT"][:, gc * 128:(gc + 1) * 128], E[:, 1, :], ALU.mult, eng="pool")
                S.tt(B["kst"][k % DP].v(), B["ktm"][:, gc, :], E[:, 2, :], ALU.mult)

            def st3(k):
                gc = its[k]["gc"]
                p2 = self.ps()
                st["p2"][k] = p2
                S.mm(p2[:, 0:128], B["kin"][k % DP].v(), qin_of(k))
                S.mm(p2[:, 128:384], B["kst"][k % DP].v(), B["vtm"][:, gc, :])
                S.tt(att_of(k), p2[:, 0:128], self.mask[:, dr, :], ALU.mult)

            def st4(k):
                it = its[k]
                gc = it["gc"]
                sq = it["sq"]
                E = E_of(k)
                p2 = st["p2"].pop(k)
                cur = st["cur"]
                if it["first"]:
                    if sq["init"]:
                        S.dma(Sf[cur].v(), d["st_ret" if ret else "st_gla"][i, dr, h])
                    else:
                        S.memset(Sf[cur].v(), 0.0)
                dec = E[:, 0, 127:128] if dr == 0 else E[:, 0, 0:1]
                sb16 = B["Sbb"][:, gc, :] if dr == 1 else B["Sb16"][k % 2].v()
                S.copy(sb16, Sf[cur].v(), eng="act")
                S.stt(Sf[1 - cur].v(), Sf[cur].v(), dec, p2[:, 128:384], ALU.mult, ALU.add)
                st["cur"] = 1 - cur
                if dr == 0:
                    po = self.ps()
                    v_n = B["vtm"][:, gc, :]
                    S.mm(po[:, 0:256], att_of(k), v_n, start=True, stop=False)
                    S.mm(po[:, 0:256], qin_of(k), sb16, start=False, stop=False)
                    S.mm(po[:, 0:256], B["attb"][:, gc, :], v_n, start=False, stop=False)
                    S.mm(po[:, 0:256], B["qinb"][:, gc, :], B["Sbb"][:, gc, :], start=False, stop=True)
                    S.copy(B["osb"][:, gc, :], po[:, 0:256], eng="act")
                    S.bn_stats(B["stats"][:, gc, :], B["osb"][:, gc, :])
                if it["last"] and not sq["init"]:
                    dst = self.dout["ns_ret" if ret else "ns_gla"][sq["b"], i, dr, h]
                    S.dma(V(self.OUTB, dst), Sf[st["cur"]].v())

            for s_ in range(NI + 3):
                if s_ < NI:
                    st1(s_)
                if 0 <= s_ - 1 < NI:
                    st2(s_ - 1)
                if 0 <= s_ - 2 < NI:
                    st3(s_ - 2)
                if 0 <= s_ - 3 < NI:
                    st4(s_ - 3)
        if getattr(self, "dbg_stage", 9) < 3:
            return
        NT = grp["T"] // 128
        for gc in range(NT):
            S.bn_aggr(B["mv"][:, gc, :], B["stats"][:, gc, :])
        mean = B["mv"][:, :, 0]
        var = B["mv"][:, :, 1]
        ms = B["ms"].v()
        rstd = B["rstd"].v()
        if ret:
            S.act(rstd, var, AF.Sqrt, bias=EPS)
        else:
            S.tt(ms, mean, mean, ALU.mult)
            S.tt(ms, ms, var, ALU.add)
            S.act(rstd, ms, AF.Sqrt, bias=EPS)
        S.recip(rstd, rstd)
        for gc in range(NT):
            tn = B["tn"][gc % 2]
            mcol = B["mv"][:, gc, 0:1] if ret else self.zeros[:, 0:1]
            S.ts(tn.v(), B["osb"][:, gc, :], mcol, B["rstd"][:, gc:gc + 1], ALU.subtract, ALU.mult)
            yb = B["ybf"][gc % 2]
            S.tt(yb.v(), tn.v(), B["gz"][:, gc, :], ALU.mult, eng="pool")
            pt = self.ps().v().bitcast(BF16)
            S.transpose(pt[:, 0:128], yb[:, 0:128], self.ident.v())
            S.transpose(pt[:, 128:256], yb[:, 128:256], self.ident.v())
            S.copy(B["yTs"][:, :, gc * 128:(gc + 1) * 128], pt[:, 0:256].rearrange("p (j c) -> p j c", c=128), eng="act")
        for sq in grp["seqs"]:
            o, L = sq["o"], sq["L"]
            self.store_yT(grp, sq, hh * 2, B["yTs"][:, 0, o:o + L])
            self.store_yT(grp, sq, hh * 2 + 1, B["yTs"][:, 1, o:o + L])

    def store_yT(self, grp, sq, e, src):
        S = self.S
        L = sq["L"]
        tok = grp["tok0"] + sq["o"]
        sub = min(512, L)
        for sb_ in range(L // sub):
            t0 = tok + sb_ * sub
            gb = t0 // 512
            c0 = t0 % 512
            S.dma(self.YT[e][gb][:, c0:c0 + sub], src[:, sb_ * sub:(sb_ + 1) * sub])

    def cd_layer(self, i, layer, grp):
        S = self.S
        d = self.din
        m_cd = S.mark()
        if os.environ.get("DBG_CD", "") != "hy":
            self.cd_lru(i, grp)
        S.release(m_cd)
        if os.environ.get("DBG_CD", "") != "lru":
            self.cd_hyena(i, grp)

    def cd_proj_fm(self, wseg, grp, sq, dst_fn):
        S = self.S
        L = sq["L"]
        sub = min(512, L)
        for sb_ in range(L // sub):
            blk, cs = self.hT_cols(sq["o"] + sb_ * sub, sub)
            p = self.ps()
            for k in range(8):
                S.mm(p[:, 0:sub], wseg[:, k, :], self.hT[blk][:, k, cs], start=(k == 0), stop=(k == 7))
            dst_fn(sb_ * sub, sub, p[:, 0:sub])

    def cd_lru(self, i, grp):
        S = self.S
        d = self.din
        Lmax = max(s["L"] for s in grp["seqs"])
        NS = 2 if Lmax <= 256 else 1
        wls = [S.sbuf([128, 8, 256], BF16, "wl%d" % j) for j in range(2)]
        wris = [S.sbuf([128, 2, 2, 128], BF16, "wri%d" % j) for j in range(2)]
        nsls = [S.sbuf([128, 4], F32, "nsl%d" % j) for j in range(2)]
        sets = []
        for j in range(NS):
            sets.append(dict(
                xrp=S.sbuf([128, Lmax + 3], F32, "xrp%d" % j), xc=S.sbuf([128, Lmax], F32, "xc%d" % j),
                xcb=S.sbuf([128, Lmax], BF16, "xcb%d" % j), zrs=S.sbuf([128, Lmax], BF16, "zrs%d" % j),
                rg=[S.sbuf([128, Lmax], F32, "rg%d_%d" % (j, q)) for q in range(2)],
                ig=[S.sbuf([128, Lmax], F32, "ig%d_%d" % (j, q)) for q in range(2)],
                aa=[S.sbuf([128, Lmax], F32, "aa%d_%d" % (j, q)) for q in range(2)],
                a2=[S.sbuf([128, Lmax], F32, "a2%d_%d" % (j, q)) for q in range(2)],
                bb=[S.sbuf([128, Lmax], F32, "bb%d_%d" % (j, q)) for q in range(2)],
                hd=[S.sbuf([128, Lmax], F32, "hd%d_%d" % (j, q)) for q in range(2)],
                yr=S.sbuf([128, Lmax], BF16, "yr%d" % j)))

        def load_g(g, slot):
            S.dma(wls[slot].v(), d["w_cd_wg"][i, g].rearrange("p (k n) -> p k n", n=768)[:, :, 512:768], eng="pool")
            for dr in range(2):
                S.dma(wris[slot][:, dr, 0, :], d["w_lru_w_r"][i, dr, g], eng="pool")
                S.dma(wris[slot][:, dr, 1, :], d["w_lru_w_i"][i, dr, g], eng="pool")

        load_g(0, 0)
        si = 0
        for g in range(8):
            wl = wls[g % 2]
            wri = wris[g % 2]
            nsl = nsls[g % 2]
            if g + 1 < 8:
                load_g(g + 1, (g + 1) % 2)
            for dr in range(2):
                lam = self.fm[:, FM["llam"] + i * 16 + dr * 8 + g: FM["llam"] + i * 16 + dr * 8 + g + 1]
                S.act(nsl[:, dr:dr + 1], lam, AF.Exp, scale=-1.0)
                S.act(nsl[:, dr:dr + 1], nsl[:, dr:dr + 1], AF.Ln, bias=1.0)
                S.ts(nsl[:, 2 + dr:3 + dr], nsl[:, dr:dr + 1], -16.0, None, ALU.mult)
                S.ts(nsl[:, dr:dr + 1], nsl[:, dr:dr + 1], -8.0, None, ALU.mult)
            for sq in grp["seqs"]:
                bs_ = sets[si % NS]
                si += 1
                xrp, xc, xcb, zrs, yr = bs_["xrp"], bs_["xc"], bs_["xcb"], bs_["zrs"], bs_["yr"]
                hd_ = bs_["hd"]
                L = sq["L"]
                S.memset(xrp[:, 0:2], 0.0)
                S.memset(xrp[:, L + 2:L + 3], 0.0)
                self.cd_proj_fm(wl[:, :, 0:128], grp, sq,
                                lambda t0, n, pv: S.copy(xrp[:, 2 + t0:2 + t0 + n], pv, eng="act"))
                self.cd_proj_fm(wl[:, :, 128:256], grp, sq,
                                lambda t0, n, pv: S.act(zrs[:, t0:t0 + n], pv, AF.Silu))
                cw0 = FM["lcw"] + i * 32 + g * 4
                cb = self.fm[:, FM["lcb"] + i * 8 + g: FM["lcb"] + i * 8 + g + 1]
                S.ts(xc[:, 0:L], xrp[:, 0:L], self.fm[:, cw0:cw0 + 1], cb, ALU.mult, ALU.add)
                for k in range(1, 4):
                    S.stt(xc[:, 0:L], xrp[:, k:k + L], self.fm[:, cw0 + k:cw0 + k + 1], xc[:, 0:L], ALU.mult, ALU.add)
                S.copy(xcb[:, 0:L], xc[:, 0:L], eng="act")
                sub = min(512, L)
                for dr in range(2):
                    rg, ig, aa, a2, bb = bs_["rg"][dr], bs_["ig"][dr], bs_["aa"][dr], bs_["a2"][dr], bs_["bb"][dr]
                    br = self.fm[:, FM["lbr"] + i * 16 + dr * 8 + g: FM["lbr"] + i * 16 + dr * 8 + g + 1]
                    bi = self.fm[:, FM["lbi"] + i * 16 + dr * 8 + g: FM["lbi"] + i * 16 + dr * 8 + g + 1]
                    for sb_ in range(L // sub):
                        cs = slice(sb_ * sub, (sb_ + 1) * sub)
                        pr = self.ps()
                        S.mm(pr[:, 0:sub], wri[:, dr, 0, :], xcb[:, cs])
                        S.act(rg[:, cs], pr[:, 0:sub], AF.Sigmoid, bias=br)
                        pi_ = self.ps()
                        S.mm(pi_[:, 0:sub], wri[:, dr, 1, :], xcb[:, cs])
                        S.act(ig[:, cs], pi_[:, 0:sub], AF.Sigmoid, bias=bi)
                    S.act(aa[:, 0:L], rg[:, 0:L], AF.Exp, scale=nsl[:, dr:dr + 1])
                    S.act(a2[:, 0:L], rg[:, 0:L], AF.Exp, scale=nsl[:, 2 + dr:3 + dr])
                    S.act(a2[:, 0:L], a2[:, 0:L], AF.Sqrt, scale=-1.0, bias=1.0)
                    S.tt(ig[:, 0:L], ig[:, 0:L], xc[:, 0:L], ALU.mult, eng=("dve" if dr == 0 else "pool"))
                    S.tt(bb[:, 0:L], a2[:, 0:L], ig[:, 0:L], ALU.mult, eng=("pool" if dr == 0 else "dve"))
                    if sq["init"]:
                        c_ = FM["stl"] + i * 16 + dr * 8 + g
                        h0 = self.fm[:, c_:c_ + 1]
                    else:
                        h0 = 0.0
                    hh_ = hd_[dr]
                    if dr == 0:
                        S.scan(hh_[:, 0:L], aa[:, 0:L], bb[:, 0:L], h0)
                        last = hh_[:, L - 1:L]
                    else:
                        S.scan(hh_[:, 0:L][:, ::-1], aa[:, 0:L][:, ::-1], bb[:, 0:L][:, ::-1], h0)
                        last = hh_[:, 0:1]
                    if not sq["init"]:
                        dst = self.dout["ns_lru"][sq["b"], i, dr, g * 128:(g + 1) * 128].rearrange("(p o) -> p o", o=1)
                        S.dma(V(self.OUTB, dst), last)
                S.tt(hd_[0][:, 0:L], hd_[0][:, 0:L], hd_[1][:, 0:L], ALU.add)
                S.tt(yr[:, 0:L], hd_[0][:, 0:L], zrs[:, 0:L], ALU.mult, eng="pool")
                self.store_yT(grp, sq, 8 + g, yr[:, 0:L])

    def cd_hyena(self, i, grp):
        S = self.S
        d = self.din
        L = grp["seqs"][0]["L"]
        nT = L // 128
        nft = 2 * nT + 1
        sub = min(512, L)
        nsub = L // sub
        CW = 512 if L <= 256 else 256
        self.CW = CW
        ngi = CW // 128
        fwd = d["c_fw%d" % L]
        ivd = d["c_iv%d" % L]
        g2a = S.sbuf([65, L], BF16, "g2a")
        S.memset(g2a[64:65, :], 1.0)
        m1 = S.mark()
        feats = S.sbuf([33, L], F32, "feats")
        S.dma(feats.v(), d["c_feats%d" % L].v())
        w1 = S.sbuf([33, 64], F32, "w1")
        S.dma(w1.v(), d["w_hy_w1"][i])
        w2 = S.sbuf([64, 64], F32, "w2")
        S.dma(w2.v(), d["w_hy_w2"][i])
        hyp = S.sbuf([64, 4], F32, "hyp")
        S.dma(hyp.v(), d["w_hyp"][i])
        hsc = S.sbuf([64, 4], F32, "hsc")
        for j in range(2):
            S.ts(hsc[:, 2 * j:2 * j + 1], hyp[:, 2 * j + 1:2 * j + 2], 0.5, None, ALU.mult)
            S.tt(hsc[:, 2 * j + 1:2 * j + 2], hsc[:, 2 * j:2 * j + 1], hyp[:, 2 * j:2 * j + 1], ALU.mult)
        g1 = S.sbuf([64, L], F32, "g1")
        sh = S.sbuf([64, 512], F32, "sh")
        ch = S.sbuf([64, 512], F32, "ch")
        for stage in range(2):
            for sb_ in range(nsub):
                cs = slice(sb_ * sub, (sb_ + 1) * sub)
                p = self.ps()
                if stage == 0:
                    S.mm(p[0:64, 0:sub], w1.v(), feats[:, cs])
                else:
                    S.mm(p[0:64, 0:sub], w2.v(), g1[:, cs])
                scl = hsc[:, 2 * stage:2 * stage + 1]
                bia = hsc[:, 2 * stage + 1:2 * stage + 2]
                S.act(sh[:, 0:sub], p[0:64, 0:sub], AF.Sin, scale=scl, bias=bia)
                S.act(ch[:, 0:sub], p[0:64, 0:sub], AF.Abs, scale=scl, bias=bia)
                S.act(ch[:, 0:sub], ch[:, 0:sub], AF.Sin, scale=-1.0, bias=self.halfpi[0:64, :])
                dst = g1[:, cs] if stage == 0 else g2a[0:64, cs]
                S.stt(dst, sh[:, 0:sub], 2.0, ch[:, 0:sub], ALU.mult, ALU.mult)
        S.release(m1)
        delt = S.sbuf([128, CW], F32, "delt")
        tcol = S.sbuf([128, nT], F32, "tcol")
        S.dma(tcol.v(), d["c_tcol%d" % L].v())
        skipr = S.sbuf([1, 2, CW], F32, "skipr")
        G = [S.sbuf([128, 2, CW], BF16, "G%d" % f) for f in range(nft)]
        NSET = 2 if L <= 256 else 1
        sets = []
        for q_ in range(NSET):
            sets.append(dict(
                Y=[S.sbuf([128, CW], BF16, "Y%d_%d" % (q_, f)) for f in range(nft)],
                vtm=S.sbuf([128, nT, CW], BF16, "vtmh%d" % q_),
                x1s=[S.sbuf([128, L], BF16, "x1s%d_%d" % (q_, j)) for j in range(ngi)],
                fmv=[S.sbuf([128, L], BF16, "fmv%d_%d" % (q_, j)) for j in range(ngi)]))
        raws = [S.sbuf([128, L + 2], BF16, "raw%d" % j) for j in range(2)]
        self.fwb = [S.sbuf([128, max(nT * 128, L)], BF16, "fwb%d" % j) for j in range(4)]
        wseg = S.sbuf([128, 8, 256], BF16, "wseg")
        wsegs = None
        if NSET > 1:
            wsegs = [[S.sbuf([128, 8, 256], BF16, "wsg%d_%d" % (gi, pt)) for pt in range(2)] for gi in range(ngi)]
        ct = [S.sbuf([128, CW], F32, "ct%d" % j) for j in range(8)]
        c3ts = [S.sbuf([128, 512], F32, "c3t%d" % j) for j in range(2)]
        c3t = c3ts[0]
        self.cti = 0
        self.c3i = 0
        taps = S.sbuf([128, nT * 2 * CW], BF16, "taps")
        tapv = taps.v().rearrange("p (a s c) -> p a s c", s=2, c=CW)
        if NSET == 1:
            sets[0]["zhs"] = [taps[:, j * L:(j + 1) * L] for j in range(ngi)]
        else:
            for q_ in range(NSET):
                sets[q_]["zhs"] = [S.sbuf([128, L], BF16, "zhs%d_%d" % (q_, j)).v() for j in range(ngi)]
        w3q = S.sbuf([65, 4, CW], BF16, "w3q")
        wn1 = [S.sbuf([128, CW], F32, "wn%d" % j) for j in range(2)]
        self.fwi = 0

        def conv3(seg, g, wv, sq, dst):
            Ls = sq["L"]
            raw = raws[self.c3i % 2]
            self.c3i += 1
            S.memset(raw[:, 0:1], 0.0)
            S.memset(raw[:, Ls + 1:Ls + 2], 0.0)
            self.cd_proj_fm(wv, grp, sq, lambda t0, n, pv: S.copy(raw[:, 1 + t0:1 + t0 + n], pv, eng="act"))
            w0 = FM["hsw"] + i * 72 + seg * 24 + g * 3
            b0 = FM["hsb"] + i * 24 + seg * 8 + g
            for sb_ in range(Ls // sub):
                c_ = sb_ * sub
                c3t = c3ts[(self.c3i + sb_) % 2]
                S.ts(c3t[:, 0:sub], raw[:, c_:c_ + sub], self.fm[:, w0:w0 + 1], self.fm[:, b0:b0 + 1], ALU.mult, ALU.add)
                S.stt(c3t[:, 0:sub], raw[:, c_ + 1:c_ + 1 + sub], self.fm[:, w0 + 1:w0 + 2], c3t[:, 0:sub], ALU.mult, ALU.add)
                S.stt(dst[:, c_:c_ + sub], raw[:, c_ + 2:c_ + 2 + sub], self.fm[:, w0 + 2:w0 + 3], c3t[:, 0:sub], ALU.mult, ALU.add)

        def make_taps(cv, c0):
            for a in range(nT):
                pff = self.ps()
                pfb = self.ps()
                S.mm(pff[:, 0:CW], g2a[:, a * 128:(a + 1) * 128], w3q[:, 2 * cv, :])
                S.mm(pfb[:, 0:CW], g2a[:, a * 128:(a + 1) * 128], w3q[:, 2 * cv + 1, :])
                wn = wn1[a % 2]
                S.act(wn.v(), delt.v(), AF.Exp, scale=tcol[:, a:a + 1])
                hf = ct[(2 * a) % 8]
                hb = ct[(2 * a + 1) % 8]
                S.tt(hf.v(), pff[:, 0:CW], wn.v(), ALU.mult)
                S.tt(hb.v(), pfb[:, 0:CW], wn.v(), ALU.mult)
                if a == 0:
                    S.tt(hf[0:1, :], hf[0:1, :], skipr[0:1, cv, :], ALU.add)
                    S.tt(tapv[:, a, 0, :], hf.v(), hb.v(), ALU.add, eng="pool")
                    S.tt(hf[0:1, :], hf[0:1, :], skipr[0:1, cv, :], ALU.subtract)
                    S.tt(tapv[:, a, 1, :], hf.v(), hb.v(), ALU.subtract, eng="pool")
                else:
                    S.tt(tapv[:, a, 0, :], hf.v(), hb.v(), ALU.add, eng="pool")
                    S.tt(tapv[:, a, 1, :], hf.v(), hb.v(), ALU.subtract, eng="pool")

        for cq in range(1024 // CW):
            c0 = cq * CW
            S.dma(delt.v(), V(d["c_delta"], d["c_delta"].ap[0, c0:c0 + CW].partition_broadcast(128)))
            S.dma(skipr.v(), d["w_hy_skip"][i:i + 1, :, c0:c0 + CW])
            for k in range(4):
                S.dma(w3q[:, k, :], d["w_w3aug"][i, :, k * 1024 + c0:k * 1024 + c0 + CW], eng="pool")
            if wsegs is not None:
                for gi in range(ngi):
                    g = cq * ngi + gi
                    wv_ = d["w_cd_wg"][i, g].rearrange("p (k n) -> p k n", n=768)
                    S.dma(wsegs[gi][0].v(), wv_[:, :, 0:256], eng="pool")
                    S.dma(wsegs[gi][1].v(), wv_[:, :, 256:512], eng="pool")
            firstseq = True
            for si_, sq in enumerate(grp["seqs"]):
                bs_ = sets[si_ % NSET]
                Y, vtm, x1s, fmv, zhs = bs_["Y"], bs_["vtm"], bs_["x1s"], bs_["fmv"], bs_["zhs"]
                for gi in range(ngi):
                    g = cq * ngi + gi
                    if wsegs is None:
                        S.dma(wseg.v(), d["w_cd_wg"][i, g].rearrange("p (k n) -> p k n", n=768)[:, :, 0:256], eng="pool")
                        w_ = wseg
                    else:
                        w_ = wsegs[gi][0]
                    conv3(0, g, w_[:, :, 0:128], sq, fmv[gi])
                    conv3(1, g, w_[:, :, 128:256], sq, x1s[gi])
                    self.to_tm(fmv[gi], vtm, gi, nT)
                if firstseq:
                    make_taps(0, c0)
                self.hy_fwd(fwd, nT, vtm, G, Y, 0, tapv if firstseq else None, ct)
                self.hy_inv(ivd, nT, L, Y, lambda gi, cs, pv, fmv=fmv, x1s=x1s: S.tt(fmv[gi][:, cs], pv, x1s[gi][:, cs], ALU.mult))
                for gi in range(ngi):
                    self.to_tm(fmv[gi], vtm, gi, nT)
                if firstseq:
                    make_taps(1, c0)
                self.hy_fwd(fwd, nT, vtm, G, Y, 1, tapv if firstseq else None, ct)
                for gi in range(ngi):
                    g = cq * ngi + gi
                    if wsegs is None:
                        S.dma(wseg.v(), d["w_cd_wg"][i, g].rearrange("p (k n) -> p k n", n=768)[:, :, 256:512], eng="pool")
                        w_ = wseg
                    else:
                        w_ = wsegs[gi][1]
                    conv3(2, g, w_[:, :, 0:128], sq, x1s[gi])
                    self.cd_proj_fm(w_[:, :, 128:256], grp, sq,
                                    lambda t0, n, pv, gi=gi, zhs=zhs: S.act(zhs[gi][:, t0:t0 + n], pv, AF.Silu))

                def fin(gi, cs, pv, fmv=fmv, x1s=x1s, zhs=zhs):
                    n_ = cs.stop - cs.start
                    c3f = c3ts[self.c3i % 2]
                    self.c3i += 1
                    S.tt(c3f[:, 0:n_], pv, x1s[gi][:, cs], ALU.mult)
                    S.tt(fmv[gi][:, cs], c3f[:, 0:n_], zhs[gi][:, cs], ALU.mult, eng="pool")
                self.hy_inv(ivd, nT, L, Y, fin)
                for gi in range(ngi):
                    g = cq * ngi + gi
                    self.store_yT(grp, sq, g, fmv[gi][:, 0:L])
                firstseq = False

    def to_tm(self, src, vtm, gi, nT):
        S = self.S
        for a0 in range(0, nT, 8):
            na = min(8, nT - a0)
            pt = self.ps().v().bitcast(BF16)
            for a in range(na):
                S.transpose(pt[:, a * 128:(a + 1) * 128], src[:, (a0 + a) * 128:(a0 + a + 1) * 128], self.ident.v())
            S.copy(vtm[:, a0:a0 + na, gi * 128:(gi + 1) * 128],
                   pt[:, 0:na * 128].rearrange("p (a c) -> p a c", c=128), eng="act")

    def hy_fwd(self, fwd, nT, vtm, G, Y, cv, tapv, ct):
        S = self.S
        CW = self.CW
        fwb = self.fwb

        def load(ft):
            b = fwb[self.fwi % len(fwb)]
            self.fwi += 1
            S.dma(b[:, 0:nT * 128], fwd[ft].rearrange("p a q -> p (a q)"))
            return b

        def dft(b, M, rhs_fn):
            p = self.ps()
            for a in range(nT):
                S.mm(p[0:M, 0:CW], b[:, a * 128:a * 128 + M], rhs_fn(a), start=(a == 0), stop=(a == nT - 1))
            return p

        for j in range(nT + 1):
            if j < nT:
                fts = (j, nT + 1 + j)
                M = 128
            else:
                fts = (nT,)
                M = 1
            pu = []
            for idx, ft in enumerate(fts):
                b = load(ft)
                if tapv is not None:
                    sel = 0 if idx == 0 else 1
                    pg = self.ps()
                    pd = self.ps()
                    for a in range(nT):
                        S.mm(pg[0:M, 0:CW], b[:, a * 128:a * 128 + M], tapv[:, a, sel, :], start=(a == 0), stop=(a == nT - 1))
                        S.mm(pd[0:M, 0:CW], b[:, a * 128:a * 128 + M], vtm[:, a, :], start=(a == 0), stop=(a == nT - 1))
                    S.copy(G[ft][0:M, cv, :], pg[0:M, 0:CW], eng="act")
                    pu.append(pd)
                else:
                    pu.append(dft(b, M, lambda a: vtm[:, a, :]))
            if j < nT:
                ur, ui = pu
                gr = G[fts[0]][:, cv, :]
                gi_ = G[fts[1]][:, cv, :]
                c0_ = 4 * (j % 2)
                S.tt(ct[c0_].v(), ur[:, 0:CW], gr, ALU.mult)
                S.tt(ct[c0_ + 1].v(), ui[:, 0:CW], gi_, ALU.mult)
                S.tt(Y[fts[0]].v(), ct[c0_].v(), ct[c0_ + 1].v(), ALU.subtract, eng="pool")
                S.tt(ct[c0_ + 2].v(), ur[:, 0:CW], gi_, ALU.mult)
                S.tt(ct[c0_ + 3].v(), ui[:, 0:CW], gr, ALU.mult)
                S.tt(Y[fts[1]].v(), ct[c0_ + 2].v(), ct[c0_ + 3].v(), ALU.add, eng="pool")
            else:
                S.tt(Y[nT][0:1, :], pu[0][0:1, 0:CW], G[nT][0:1, cv, :], ALU.mult)

    def hy_inv(self, ivd, nT, L, Y, evac):
        S = self.S
        nft = 2 * nT + 1
        sub = min(512, L)
        nsub = L // sub
        CW = self.CW
        ngi = CW // 128
        acc = [[self.ps() for _ in range(nsub)] for _ in range(ngi)]
        for ft in range(nft):
            b = self.fwb[self.fwi % len(self.fwb)]
            self.fwi += 1
            S.dma(b[:, 0:L], ivd[ft])
            K = 1 if ft == nT else 128
            for gi in range(ngi):
                for sb_ in range(nsub):
                    S.mm(acc[gi][sb_][:, 0:sub], Y[ft][0:K, gi * 128:(gi + 1) * 128], b[0:K, sb_ * sub:(sb_ + 1) * sub],
                         start=(ft == 0), stop=(ft == nft - 1))
        for gi in range(ngi):
            for sb_ in range(nsub):
                evac(gi, slice(sb_ * sub, (sb_ + 1) * sub), acc[gi][sb_][:, 0:sub])


_PROG_CACHE = {}


def _get_prog(consts, wts, n_layers=4, groups=("P", "S")):
    key = (n_layers, groups)
    if key not in _PROG_CACHE:
        pr = Prog(n_layers=n_layers, groups=groups)
        pr.declare(consts, wts)
        pr.build()
        _PROG_CACHE[key] = pr
    return _PROG_CACHE[key]


def make_in_maps(inputs, consts, wts):
    xp = np.asarray(inputs["x_prompt"], np.float32)
    xs = np.asarray(inputs["x_sample"], np.float32)
    c = np.asarray(inputs["c"], np.float32)
    cctx = np.asarray(inputs["c_ctx"], np.float32)
    sg = np.asarray(inputs["state_gla"], np.float32)
    sr = np.asarray(inputs["state_ret"], np.float32)
    sl = np.asarray(inputs["state_lru"], np.float32)
    maps = []
    for core in range(8):
        m = {}
        for k, v in consts.items():
            m["c_" + k] = v
        for k, v in wts.items():
            m["w_" + k] = v
        fm = wts["fm"].copy()
        fm[:, FM["stl"]:FM["stl"] + 32] = sl[core].reshape(2, 2, 8, 128).transpose(3, 0, 1, 2).reshape(128, 32)
        m["w_fm"] = fm
        m["xp"] = np.ascontiguousarray(xp[core * NP_SEQ:(core + 1) * NP_SEQ].reshape(TP, D))
        m["xs"] = np.ascontiguousarray(xs[core])
        cond = np.stack([cctx, c[core]], 0)
        m["condT"] = np.ascontiguousarray(cond.reshape(2, 8, 128).transpose(2, 1, 0).reshape(128, 16))
        m["st_gla"] = np.ascontiguousarray(sg[core])
        m["st_ret"] = np.ascontiguousarray(sr[core])
        maps.append(m)
    return maps


def kernel(**inputs):
    consts = make_consts()
    wts = prep_weights(inputs)
    prog = _get_prog(consts, wts)
    maps = make_in_maps(inputs, consts, wts)
    res = run_bass_kernel_spmd(prog.nc, maps, core_ids=list(range(8)))
    r = res.results
    y_p = np.concatenate([r[c]["y_p"].reshape(NP_SEQ, LP, D) for c in range(8)], 0).astype(np.float32)
    y_s = np.stack([r[c]["y_s"] for c in range(8)], 0).astype(np.float32)
    ng = np.concatenate([r[c]["ns_gla"] for c in range(8)], 0).astype(np.float32)
    nr = np.concatenate([r[c]["ns_ret"] for c in range(8)], 0).astype(np.float32)
    nl = np.concatenate([r[c]["ns_lru"] for c in range(8)], 0).astype(np.float32)
    return (y_p, y_s, ng, nr, nl)
```

```python
import os
import sys
import numpy as np
import ml_dtypes
import concourse.bass as bass
import concourse.mybir as mybir
from concourse.bass_utils import run_bass_kernel_spmd

F32 = mybir.dt.float32
BF16 = mybir.dt.bfloat16
U8 = mybir.dt.uint8
AF = mybir.ActivationFunctionType
ALU = mybir.AluOpType
DSIZE = {F32: 4, BF16: 2, U8: 1}

SAME_ENGINE_SYNC = True
N_DMA_SEMS = 48
STG_N = 768
N_STG = 4
CAST_ENG = "pool"


class Tok:
    __slots__ = ("sem", "val", "op", "key")

    def __init__(self, sem, val, op, key):
        self.sem, self.val, self.op, self.key = sem, val, op, key


class Op:
    __slots__ = ("eng", "fn", "waits", "tok", "is_dma", "needs_inc", "seq")


class Buf:
    def __init__(self, ap, name=""):
        self.ap = ap
        self.name = name
        self.writes = {}
        self.reads = {}

    def __getitem__(self, idx):
        return V(self, self.ap[idx])

    def v(self):
        return V(self, self.ap)


class V:
    __slots__ = ("buf", "ap")

    def __init__(self, buf, ap):
        self.buf, self.ap = buf, ap

    def __getitem__(self, idx):
        return V(self.buf, self.ap[idx])

    def bitcast(self, dt):
        return V(self.buf, self.ap.bitcast(dt))

    def rearrange(self, *a, **k):
        return V(self.buf, self.ap.rearrange(*a, **k))


class Sched:
    ENG = ("pe", "act", "dve", "pool", "sp")

    def __init__(self, nc):
        self.nc = nc
        self.q = {e: [] for e in self.ENG}
        self.esem = {e: nc.alloc_semaphore("es_" + e) for e in self.ENG}
        self.dsems = [nc.alloc_semaphore("ds%d" % i) for i in range(N_DMA_SEMS)]
        self.dval = [0] * N_DMA_SEMS
        self.dlast = [None] * N_DMA_SEMS
        self.dnext = 0
        self.last_tok = {e: None for e in self.ENG}
        self.all_dma = []
        self.sb_base = 16384 + 1024
        self.sb_top = 207 * 1024
        self.live = []
        self.dead = []
        self.opseq = 0
        self.sb_ptr = self.sb_base
        self.uid = 0
        self.sb_max = 0
        self.stg = [self.sbuf([128, STG_N], F32, "stg%d" % i) for i in range(N_STG)]
        self.stgi = 0

    def sbuf(self, shape, dt, name="t"):
        per = int(np.prod(shape[1:])) * DSIZE[dt]
        off = (self.sb_ptr + 63) // 64 * 64
        assert off + per <= self.sb_top, "SBUF overflow %s need %d at %d" % (name, per, off)
        self.sb_ptr = off + per
        self.sb_max = max(self.sb_max, self.sb_ptr)
        self.uid += 1
        t = self.nc.alloc_sbuf_tensor_at("%s_%d" % (name, self.uid), list(shape), dt, offset=off)
        b = Buf(t.ap(), name)
        end = off + per
        keep = []
        for (o2, e2, b2) in self.dead:
            if o2 < end and off < e2:
                for tk in list(b2.writes.values()) + list(b2.reads.values()):
                    old = b.writes.get(tk.key)
                    if old is None or old.op.seq < tk.op.seq:
                        b.writes[tk.key] = tk
                if not (off <= o2 and e2 <= end):
                    keep.append((o2, e2, b2))
            else:
                keep.append((o2, e2, b2))
        self.dead = keep
        self.live.append((off, end, b))
        return b

    def mark(self):
        return self.sb_ptr

    def release(self, m):
        self.sb_ptr = m
        nl = []
        for (o2, e2, b2) in self.live:
            if o2 >= m:
                if b2.writes or b2.reads:
                    self.dead.append((o2, e2, b2))
            else:
                nl.append((o2, e2, b2))
        self.live = nl

    def add(self, eng, fn, reads=(), writes=(), dma=False):
        lim = int(os.environ.get("DBG_LIMIT", "0"))
        self.nrec = getattr(self, "nrec", 0) + 1
        if lim and self.nrec > lim:
            return Tok(self.esem[eng], 0, None, ("x", eng))
        if lim and self.nrec == lim:
            f = sys._getframe(2)
            print("LAST OP #%d eng=%s line=%d / caller line=%d" % (self.nrec, eng, f.f_lineno, f.f_back.f_lineno))
        op = Op()
        op.eng, op.fn, op.is_dma, op.needs_inc = eng, fn, dma, False
        self.opseq += 1
        op.seq = self.opseq
        deps = {}

        def dep(t):
            if t is None:
                return
            deps[id(t)] = t

        wb = set(id(w.buf) for w in writes)
        for r in reads:
            for t in r.buf.writes.values():
                dep(t)
        for w in writes:
            for t in w.buf.writes.values():
                dep(t)
            for t in w.buf.reads.values():
                dep(t)
        if dma:
            i = self.dnext
            self.dnext = (self.dnext + 1) % N_DMA_SEMS
            dep(self.dlast[i])
            self.dval[i] += 16
            tok = Tok(self.dsems[i], self.dval[i], op, ("d", i))
            self.dlast[i] = tok
            self.all_dma.append(tok)
        else:
            tok = Tok(self.esem[eng], None, op, ("e", eng))
        op.tok = tok
        waits = []
        for t in deps.values():
            if (not t.op.is_dma) and t.op.eng == eng:
                if eng == "pe" or not SAME_ENGINE_SYNC:
                    continue
            if not t.op.is_dma:
                t.op.needs_inc = True
            waits.append(t)
        op.waits = waits
        for w in writes:
            w.buf.writes = {tok.key: tok}
            w.buf.reads = {}
        for r in reads:
            if id(r.buf) not in wb:
                r.buf.reads[tok.key] = tok
        self.q[eng].append(op)
        if not dma:
            self.last_tok[eng] = tok
        return tok

    def barrier(self):
        toks = [t for t in self.last_tok.values() if t is not None]
        toks += [t for t in self.dlast if t is not None]
        for e in self.ENG:
            op = Op()
            op.eng, op.fn, op.is_dma, op.needs_inc = e, None, False, False
            self.opseq += 1
            op.seq = self.opseq
            op.tok = Tok(self.esem[e], None, op, ("e", e))
            op.waits = []
            for t in toks:
                if (not t.op.is_dma) and t.op.eng == e:
                    continue
                if not t.op.is_dma:
                    t.op.needs_inc = True
                op.waits.append(t)
            self.q[e].append(op)

    def emit(self):
        nc = self.nc
        for e in self.ENG:
            c = 0
            for op in self.q[e]:
                if not op.is_dma and op.needs_inc:
                    c += 1
                if not op.is_dma:
                    op.tok.val = c
        engobj = {"pe": "tensor", "act": "scalar", "dve": "vector", "pool": "gpsimd", "sp": "sync"}
        self.stats = {}

        def replay(e, eng):
            seen = {}
            nw = 0
            dump = os.environ.get("DBG_DUMP") == e
            for oi, op in enumerate(self.q[e]):
                if dump and oi >= len(self.q[e]) - 12:
                    print("  [%s %d] fn=%s dma=%s inc=%s val=%s waits=%s" % (e, oi, "none" if op.fn is None else "op", op.is_dma, op.needs_inc, op.tok.val,
                          [(t.sem.name, t.val) for t in op.waits]))
                for t in op.waits:
                    k = t.sem.num
                    if seen.get(k, 0) >= t.val:
                        continue
                    seen[k] = t.val
                    eng.wait_ge(t.sem, t.val)
                    nw += 1
                if op.fn is None:
                    continue
                ins = op.fn(eng)
                if op.is_dma:
                    ins.then_inc(op.tok.sem, 16)
                elif op.needs_inc:
                    ins.then_inc(op.tok.sem, 1)
            self.stats[e] = (len(self.q[e]), nw)

        with nc.Block() as block:
            @block.tensor
            def _(eng):
                replay("pe", eng)

            @block.scalar
            def _(eng):
                replay("act", eng)

            @block.vector
            def _(eng):
                replay("dve", eng)

            @block.gpsimd
            def _(eng):
                replay("pool", eng)

            @block.sync
            def _(eng):
                replay("sp", eng)

    def dma(self, out, in_, eng="sp", **kw):
        if eng == "pool":
            return self.load_cast(out, in_)
        return self.add(eng, lambda e: e.dma_start(out=out.ap, in_=in_.ap, **kw),
                        reads=[in_], writes=[out], dma=True)

    def cast_eng(self):
        self.casti = getattr(self, "casti", 0) + 1
        return ("act", "dve")[self.casti % 2]

    def load_cast(self, dst, src):
        shp = list(dst.ap.shape)
        P = shp[0]
        tok = None
        if len(shp) == 2:
            n = shp[1]
            for c0 in range(0, n, STG_N):
                c1 = min(n, c0 + STG_N)
                st = self.stg[self.stgi % N_STG]
                self.stgi += 1
                self.dma(st[0:P, 0:c1 - c0], src[:, c0:c1])
                tok = self.copy(dst[:, c0:c1], st[0:P, 0:c1 - c0], eng=self.cast_eng())
        else:
            assert len(shp) == 3
            A, Bn = shp[1], shp[2]
            if Bn > STG_N:
                for a_ in range(A):
                    tok = self.load_cast(dst[:, a_, :], src[:, a_, :])
                return tok
            step = max(1, STG_N // Bn)
            for a0 in range(0, A, step):
                a1 = min(A, a0 + step)
                st = self.stg[self.stgi % N_STG]
                self.stgi += 1
                sv = st[0:P, 0:(a1 - a0) * Bn].rearrange("p (a b) -> p a b", b=Bn)
                self.dma(sv, src[:, a0:a1, :])
                tok = self.copy(dst[:, a0:a1, :], sv, eng=self.cast_eng())
        return tok

    def mm(self, out, lhsT, rhs, start=True, stop=True):
        return self.add("pe", lambda e: e.matmul(out.ap, lhsT.ap, rhs.ap, start=start, stop=stop),
                        reads=[lhsT, rhs], writes=[out])

    def transpose(self, out, in_, ident):
        return self.add("pe", lambda e: e.transpose(out.ap, in_.ap, ident.ap),
                        reads=[in_, ident], writes=[out])

    def act(self, out, in_, func, bias=None, scale=None, accum_out=None, eng="act"):
        reads = [in_]
        kw = {}
        if bias is not None:
            if isinstance(bias, V):
                reads.append(bias)
                kw["bias"] = bias.ap
            else:
                kw["bias"] = bias
        if scale is not None:
            if isinstance(scale, V):
                reads.append(scale)
                kw["scale"] = scale.ap
            else:
                kw["scale"] = scale
        writes = [out]
        if accum_out is not None:
            writes.append(accum_out)
            kw["accum_out"] = accum_out.ap
        return self.add(eng, lambda e: e.activation(out.ap, in_.ap, func, **kw), reads=reads, writes=writes)

    def tt(self, out, in0, in1, op, eng="dve"):
        return self.add(eng, lambda e: e.tensor_tensor(out.ap, in0.ap, in1.ap, op), reads=[in0, in1], writes=[out])

    def ts(self, out, in0, s1, s2, op0, op1=None, eng="dve"):
        reads = [in0]
        a1 = s1.ap if isinstance(s1, V) else s1
        a2 = s2.ap if isinstance(s2, V) else s2
        if isinstance(s1, V):
            reads.append(s1)
        if isinstance(s2, V):
            reads.append(s2)
        if op1 is None:
            return self.add(eng, lambda e: e.tensor_scalar(out.ap, in0.ap, a1, a2, op0), reads=reads, writes=[out])
        return self.add(eng, lambda e: e.tensor_scalar(out.ap, in0.ap, a1, a2, op0, op1), reads=reads, writes=[out])

    def stt(self, out, in0, scalar, in1, op0, op1):
        reads = [in0, in1]
        a = scalar.ap if isinstance(scalar, V) else scalar
        if isinstance(scalar, V):
            reads.append(scalar)
        return self.add("dve", lambda e: e.scalar_tensor_tensor(out.ap, in0.ap, a, in1.ap, op0, op1),
                        reads=reads, writes=[out])

    def scan(self, out, d0, d1, init, op0=None, op1=None):
        reads = [d0, d1]
        a = init.ap if isinstance(init, V) else init
        if isinstance(init, V):
            reads.append(init)
        o0 = op0 or ALU.mult
        o1 = op1 or ALU.add
        return self.add("dve", lambda e: e.tensor_tensor_scan(out.ap, d0.ap, d1.ap, a, o0, o1),
                        reads=reads, writes=[out])

    def copy(self, out, in_, eng="dve"):
        if eng == "act":
            return self.add("act", lambda e: e.copy(out.ap, in_.ap), reads=[in_], writes=[out])
        return self.add(eng, lambda e: e.tensor_copy(out.ap, in_.ap), reads=[in_], writes=[out])

    def memset(self, out, val, eng="dve"):
        return self.add(eng, lambda e: e.memset(out.ap, val), reads=[], writes=[out])

    def bn_stats(self, out, in_):
        return self.add("dve", lambda e: e.bn_stats(out.ap, in_.ap), reads=[in_], writes=[out])

    def bn_aggr(self, out, in_):
        return self.add("dve", lambda e: e.bn_aggr(out.ap, in_.ap), reads=[in_], writes=[out])

    def recip(self, out, in_):
        return self.add("dve", lambda e: e.reciprocal(out.ap, in_.ap), reads=[in_], writes=[out])

import math

D = 1024
EPS = 1e-6
NP_SEQ = 4
LP = 256
LS = 2048
TP = NP_SEQ * LP
TTOT = TP + LS
CW = 256
HY_DELTA = np.abs(np.linspace(math.log(1e-2) / 0.3, math.log(1e-2) / 1.5, 1024, dtype=np.float32)).astype(np.float32)

FM = {}
_o = 0
for _n, _w in [("norm_g", 32), ("hsw", 144), ("hsb", 48), ("lcw", 64), ("lcb", 16), ("lbr", 32),
               ("lbi", 32), ("llam", 32), ("stl", 32)]:
    FM[_n] = _o
    _o += _w
FM_W = _o


def _bf(a):
    return np.ascontiguousarray(a).astype(ml_dtypes.bfloat16)


def make_consts():
    c = {}
    c["ident"] = np.eye(128, dtype=np.float32)
    tp = np.arange(128)[:, None]
    t = np.arange(128)[None, :]
    tri = np.zeros((128, 4, 128), np.float32)
    tri[:, 0] = (tp <= t) * (-1.0 / 16)
    tri[:, 1] = (tp >= t) * (-1.0 / 16)
    tri[:, 2] = (tp > t) * (-1.0 / 16)
    tri[:, 3] = (tp < t) * (-1.0 / 16)
    c["tri"] = tri
    mask = np.zeros((128, 2, 128), np.float32)
    mask[:, 0] = (tp <= t)
    mask[:, 1] = (tp >= t)
    c["mask"] = mask
    step = (12.0 - 5.0) / 3.0
    rsp = np.zeros((128, 4, 2, 128), np.float32)
    for h in range(4):
        for d in range(2):
            expo = np.float32(5.0 + step * (h + 0.5 * d))
            lg = np.log1p(-np.exp2(-np.float64(expo)))
            rsp[:, h, d, :] = -16.0 * lg
    c["retsp"] = rsp
    c["delta"] = HY_DELTA[None, :].copy()
    for L in (LP, LS):
        nT = L // 128
        nft = 2 * nT + 1
        tt = np.arange(L, dtype=np.float64)
        FW = np.zeros((nft, L, 128), np.float64)
        IV = np.zeros((nft, 128, L), np.float64)
        for j in range(nT):
            f = (np.arange(128) + 128 * j).astype(np.float64)
            th = np.pi * np.outer(tt, f) / L
            FW[j] = np.cos(th)
            FW[nT + 1 + j] = -np.sin(th)
            w = np.where(f == 0, 1.0, 2.0)
            IV[j] = (w[:, None] * np.cos(th.T)) / (2 * L)
            IV[nT + 1 + j] = (-2.0 * np.sin(th.T)) / (2 * L)
        FW[nT, :, 0] = np.cos(np.pi * tt)
        IV[nT, 0, :] = np.cos(np.pi * tt) / (2 * L)
        c["fw%d" % L] = _bf(FW.reshape(nft, nT, 128, 128).transpose(0, 2, 1, 3))
        c["iv%d" % L] = _bf(IV)
        tl = np.linspace(0.0, 1.0, L, dtype=np.float32)[:, None]
        bands = 16
        fb = np.linspace(1e-4, bands - 1, bands, dtype=np.float32)[None, :]
        ang = (np.float32(2.0 * math.pi / L) * np.arange(L, dtype=np.float32)[:, None] * fb).astype(np.float32)
        feats = np.concatenate([tl, np.cos(ang), -np.sin(ang)], axis=-1).astype(np.float32)
        c["feats%d" % L] = np.ascontiguousarray(feats.T)
        c["tcol%d" % L] = np.ascontiguousarray((-tl[:, 0]).reshape(nT, 128).T).astype(np.float32)
    return c


def prep_weights(inp):
    w = {}
    g = lambda k: np.asarray(inp[k], dtype=np.float32)
    w["w_mod"] = g("w_mod")
    w["b_mod"] = g("b_mod")
    abw = g("ab_w_in")
    offs = dict(qa=0, ka=512, va=1024, ga=2048, za=2080, qb=3104, kb=3616, vb=4128, zb=5152)
    wh = np.zeros((2, 8, 128, 8, 768), np.float32)
    wga = np.zeros((2, 128, 8, 32), np.float32)
    for i in range(2):
        for hh in range(8):
            h = hh % 4
            if hh < 4:
                cols = [abw[i][:, offs["qa"] + h * 128: offs["qa"] + (h + 1) * 128],
                        abw[i][:, offs["ka"] + h * 128: offs["ka"] + (h + 1) * 128],
                        abw[i][:, offs["va"] + h * 256: offs["va"] + (h + 1) * 256],
                        abw[i][:, offs["za"] + h * 256: offs["za"] + (h + 1) * 256]]
            else:
                cols = [abw[i][:, offs["qb"] + h * 128: offs["qb"] + (h + 1) * 128],
                        abw[i][:, offs["kb"] + h * 128: offs["kb"] + (h + 1) * 128],
                        abw[i][:, offs["vb"] + h * 256: offs["vb"] + (h + 1) * 256],
                        abw[i][:, offs["zb"] + h * 256: offs["zb"] + (h + 1) * 256]]
            m = np.concatenate(cols, axis=1)
            wh[i, hh] = m.reshape(8, 128, 768).transpose(1, 0, 2)
        wga[i] = abw[i][:, 2048:2080].reshape(8, 128, 32).transpose(1, 0, 2)
    w["ab_wh"] = wh.reshape(2, 8, 128, 8 * 768)
    w["ab_wga"] = wga.reshape(2, 128, 8 * 32)
    gw2 = g("ab_gate_w2")
    gb = g("ab_gate_b")
    w2b = np.zeros((2, 33, 1024), np.float32)
    for i in range(2):
        w2b[i, 0:16, 0:512] = gw2[i, 0]
        w2b[i, 16:32, 512:1024] = gw2[i, 1]
        w2b[i, 32, 0:512] = gb[i, 0]
        w2b[i, 32, 512:1024] = gb[i, 1]
    w["ab_w2b"] = w2b
    w["ab_head_g"] = g("ab_head_g")
    w["ab_w_out"] = np.ascontiguousarray(g("ab_w_out").reshape(2, 16, 128, 1024).transpose(0, 2, 1, 3)).reshape(2, 128, 16 * 1024)
    w["cd_w_out"] = np.ascontiguousarray(g("cd_w_out").reshape(2, 16, 128, 1024).transpose(0, 2, 1, 3)).reshape(2, 128, 16 * 1024)
    cdw = g("cd_w_in")
    wg = np.zeros((2, 8, 128, 8, 768), np.float32)
    for i in range(2):
        for gg in range(8):
            cols = [cdw[i][:, s * 1024 + gg * 128: s * 1024 + (gg + 1) * 128] for s in range(6)]
            m = np.concatenate(cols, axis=1)
            wg[i, gg] = m.reshape(8, 128, 768).transpose(1, 0, 2)
    w["cd_wg"] = wg.reshape(2, 8, 128, 8 * 768)
    fm = np.zeros((128, FM_W), np.float32)
    fm[:, FM["norm_g"]:FM["norm_g"] + 32] = g("norm_g").reshape(4, 8, 128).transpose(2, 0, 1).reshape(128, 32)
    hsw = g("hy_short_w")
    fm[:, FM["hsw"]:FM["hsw"] + 144] = hsw.reshape(2, 3, 3, 8, 128).transpose(4, 0, 2, 3, 1).reshape(128, 144)
    fm[:, FM["hsb"]:FM["hsb"] + 48] = g("hy_short_b").reshape(2, 3, 8, 128).transpose(3, 0, 1, 2).reshape(128, 48)
    fm[:, FM["lcw"]:FM["lcw"] + 64] = g("lru_conv_w").reshape(2, 4, 8, 128).transpose(3, 0, 2, 1).reshape(128, 64)
    fm[:, FM["lcb"]:FM["lcb"] + 16] = g("lru_conv_b").reshape(2, 8, 128).transpose(2, 0, 1).reshape(128, 16)
    for nm, key in (("lbr", "lru_b_r"), ("lbi", "lru_b_i"), ("llam", "lru_lambda")):
        fm[:, FM[nm]:FM[nm] + 32] = g(key).reshape(2, 2, 8, 128).transpose(3, 0, 1, 2).reshape(128, 32)
    w["fm"] = fm
    w["hy_skip"] = g("hy_skip")
    w["lru_w_r"] = g("lru_w_r")
    w["lru_w_i"] = g("lru_w_i")
    w["hy_w1"] = g("hy_w1")
    w["hy_w2"] = g("hy_w2")
    hyp = np.zeros((2, 64, 4), np.float32)
    hyp[:, :, 0] = g("hy_b1")
    hyp[:, :, 1] = g("hy_freq1")
    hyp[:, :, 2] = g("hy_b2")
    hyp[:, :, 3] = g("hy_freq2")
    w["hyp"] = hyp
    w["w3aug"] = np.concatenate([g("hy_w3"), g("hy_b3")[:, None, :]], axis=1)
    w["final_g"] = g("final_g").reshape(1, 1024)
    return w


DRAM_IN_SPECS = None


class Prog:
    def __init__(self, n_layers=4, groups=("P", "S")):
        self.n_layers = n_layers
        self.groups = groups
        nc = bass.Bass("TRN2", target_bir_lowering=False)
        self.nc = nc
        self.S = Sched(nc)
        self.din = {}
        self.dout = {}

    def inp(self, name, shape, dt=F32):
        ap = self.nc.dram_tensor(name, list(shape), dt, kind="ExternalInput").ap()
        self.din[name] = Buf(ap, name)
        return self.din[name]

    def outp(self, name, shape):
        ap = self.nc.dram_tensor(name, list(shape), F32, kind="ExternalOutput").ap()
        self.dout[name] = ap
        return ap

    def ps(self):
        b = self.PS[self.ps_i % len(self.PS)]
        self.ps_i += 1
        return b

    def declare(self, consts, wts):
        for k, v in consts.items():
            self.inp("c_" + k, v.shape, BF16 if v.dtype == ml_dtypes.bfloat16 else F32)
        for k, v in wts.items():
            self.inp("w_" + k, v.shape, F32)
        self.inp("xp", (TP, D))
        self.inp("xs", (LS, D))
        self.inp("condT", (128, 16))
        self.inp("st_gla", (2, 2, 4, 128, 256))
        self.inp("st_ret", (2, 2, 4, 128, 256))
        self.outp("y_p", (TP, D))
        self.outp("y_s", (LS, D))
        self.outp("ns_gla", (NP_SEQ, 2, 2, 4, 128, 256))
        self.outp("ns_ret", (NP_SEQ, 2, 2, 4, 128, 256))
        self.outp("ns_lru", (NP_SEQ, 2, 2, 1024))
        nc = self.nc
        self.Xd = nc.dram_tensor("x_scr", [TTOT, D], F32, kind="Internal").ap()
        self.X = [Buf(self.Xd[t * 128:(t + 1) * 128, :], "x%d" % t) for t in range(TTOT // 128)]
        self.YTd = nc.dram_tensor("yt_scr", [2048, TTOT], BF16, kind="Internal").ap()
        self.YT = [[Buf(self.YTd[e * 128:(e + 1) * 128, b * 512:(b + 1) * 512], "yt") for b in range(TTOT // 512)]
                   for e in range(16)]
        self.MODd = nc.dram_tensor("mod_scr", [4, 2, 1024], F32, kind="Internal").ap()
        self.MOD = Buf(self.MODd, "modscr")
        self.YP = [Buf(self.dout["y_p"][t * 128:(t + 1) * 128, :]) for t in range(TP // 128)]
        self.YS = [Buf(self.dout["y_s"][t * 128:(t + 1) * 128, :]) for t in range(LS // 128)]
        self.OUTB = Buf(self.dout["ns_gla"], "o")
        self.PS = [Buf(nc.alloc_psum_tensor("ps%d" % i, [128, 512], F32).ap(), "ps%d" % i) for i in range(8)]
        self.ps_i = 0

    def build(self):
        S = self.S
        d = self.din
        self.ident = S.sbuf([128, 128], BF16, "ident")
        S.dma(self.ident.v(), d["c_ident"].v(), eng="pool")
        self.halfpi = S.sbuf([128, 1], F32, "halfpi")
        S.memset(self.halfpi.v(), math.pi / 2)
        self.fm = S.sbuf([128, FM_W], F32, "fm")
        S.dma(self.fm.v(), d["w_fm"].v())
        self.zeros = S.sbuf([128, 16], F32, "zeros")
        S.memset(self.zeros.v(), 0.0)
        self.modT = S.sbuf([128, 4, 2, 16], F32, "modT")
        self.gs = S.sbuf([128, 4, 2, 8], F32, "gs")
        self.prologue()
        S.barrier()
        base_mark = S.mark()
        for layer in range(self.n_layers):
            for gname in self.groups:
                if gname == "P":
                    grp = dict(name="P", tok0=0, T=TP, cond=0,
                               seqs=[dict(o=j * LP, L=LP, b=j, init=False) for j in range(NP_SEQ)])
                else:
                    grp = dict(name="S", tok0=TP, T=LS, cond=1, seqs=[dict(o=0, L=LS, b=None, init=True)])
                S.release(base_mark)
                self.phase_a(layer, grp)
                if layer % 2 == 0:
                    self.ab_layer(layer // 2, layer, grp)
                else:
                    self.cd_layer(layer // 2, layer, grp)
                S.release(self.mix_mark)
                self.phase_c(layer, grp, last=(layer == self.n_layers - 1))
        S.barrier()
        S.emit()

    def prologue(self):
        S = self.S
        d = self.din
        m0 = S.mark()
        condT = S.sbuf([128, 8, 2], F32, "condT")
        S.dma(condT.v(), d["condT"].v().rearrange("p (k c) -> p k c", c=2))
        scT = S.sbuf([128, 8, 2], F32, "scT")
        S.act(scT.v(), condT.v(), AF.Silu)
        wm = [S.sbuf([128, 8, 512], F32, "wm%d" % i) for i in range(6)]
        bm = S.sbuf([2, 3072], F32, "bm")
        modrow = S.sbuf([2, 3072], F32, "modrow")
        id2 = S.sbuf([2, 2], F32, "id2")
        S.dma(id2.v(), d["c_ident"][0:2, 0:2])
        cnt = 0
        for l in range(self.n_layers):
            S.dma(bm[0:1, :], d["w_b_mod"][l:l + 1, :])
            S.dma(bm[1:2, :], d["w_b_mod"][l:l + 1, :])
            for nb in range(6):
                w = wm[cnt % 6]
                cnt += 1
                S.dma(w.v(), d["w_w_mod"][l, :, nb * 512:(nb + 1) * 512].rearrange("(k p) n -> p k n", p=128))
                p = self.ps()
                for k in range(8):
                    S.mm(p[0:2, :], scT[:, k, :], w[:, k, :], start=(k == 0), stop=(k == 7))
                S.tt(modrow[:, nb * 512:(nb + 1) * 512], p[0:2, :], bm[:, nb * 512:(nb + 1) * 512], ALU.add)
            S.dma(V(self.MOD, self.MODd[l]), modrow[:, 2048:3072])
            p = self.ps()
            for j in range(16):
                S.transpose(p[:, j * 2:(j + 1) * 2], modrow[:, j * 128:(j + 1) * 128], id2.v())
            for c in range(2):
                S.copy(self.modT[:, l, c, :], p[:, 0:32].rearrange("p (j c) -> p c j", c=2)[:, c, :], eng="dve")
                S.ts(self.gs[:, l, c, :], self.modT[:, l, c, 8:16], 1.0, None, ALU.add)
                S.tt(self.gs[:, l, c, :], self.gs[:, l, c, :], self.fm[:, FM["norm_g"] + l * 8: FM["norm_g"] + (l + 1) * 8], ALU.mult)
        S.barrier()
        S.release(m0)

    def xsrc(self, layer, grp, ti):
        if layer == 0:
            src = self.din["xp"] if grp["name"] == "P" else self.din["xs"]
            return src[ti * 128:(ti + 1) * 128, :]
        return self.X[grp["tok0"] // 128 + ti].v()

    def hT_cols(self, gtok, n):
        blk = gtok // 512
        c0 = gtok % 512
        assert c0 + n <= 512
        return blk, slice(c0, c0 + n)

    def phase_a(self, layer, grp):
        S = self.S
        T = grp["T"]
        c = grp["cond"]
        self.hT = [S.sbuf([128, 8, 512], BF16, "hT%d" % b) for b in range(T // 512)]
        self.wout_pre = None
        if False and grp["name"] == "P":
            i_ = layer // 2
            wname_ = "w_ab_w_out" if layer % 2 == 0 else "w_cd_w_out"
            self.wout_pre = S.sbuf([128, 16, 1024], BF16, "woutp")
        self.mix_mark = S.mark()
        m0 = S.mark()
        xt = [S.sbuf([128, 1024], F32, "xt%d" % i) for i in range(2)]
        xn = [S.sbuf([128, 1024], BF16, "xn%d" % i) for i in range(2)]
        junk = S.sbuf([128, 1024], BF16, "junk")
        ss = [S.sbuf([128, 1], F32, "ss%d" % i) for i in range(2)]
        nTt = T // 128

        def stage1(ti):
            xb = xt[ti % 2]
            S.dma(xb.v(), self.xsrc(layer, grp, ti))
            s_ = ss[ti % 2]
            S.act(junk.v(), xb.v(), AF.Square, accum_out=s_.v())
            S.act(s_.v(), s_.v(), AF.Sqrt, bias=EPS, scale=1.0 / D)
            S.recip(s_.v(), s_.v())
            S.ts(xn[ti % 2].v(), xb.v(), s_.v(), None, ALU.mult)

        def stage2(ti):
            xnb = xn[ti % 2]
            pst = self.ps().v().bitcast(BF16)
            for k in range(8):
                S.transpose(pst[:, k * 128:(k + 1) * 128], xnb[:, k * 128:(k + 1) * 128], self.ident.v())
            blk, cs = self.hT_cols(ti * 128, 128)
            for k in range(8):
                if ti % 2 == 0:
                    S.act(self.hT[blk][:, k, cs], pst[:, k * 128:(k + 1) * 128], AF.Identity,
                          scale=self.gs[:, layer, c, k:k + 1], bias=self.modT[:, layer, c, k:k + 1])
                else:
                    S.ts(self.hT[blk][:, k, cs], pst[:, k * 128:(k + 1) * 128],
                         self.gs[:, layer, c, k:k + 1], self.modT[:, layer, c, k:k + 1], ALU.mult, ALU.add)

        for ti in range(nTt + 1):
            if ti < nTt:
                stage1(ti)
            if ti >= 1:
                stage2(ti - 1)
        if self.wout_pre is not None:
            for q in range(4):
                S.dma(self.wout_pre[:, q * 4:(q + 1) * 4, :],
                      self.din[wname_][i_, :, q * 4096:(q + 1) * 4096].rearrange("p (e n) -> p e n", n=1024), eng="pool")
        S.release(m0)

    def phase_c(self, layer, grp, last):
        S = self.S
        d = self.din
        T = grp["T"]
        c = grp["cond"]
        i = layer // 2
        wname = "w_ab_w_out" if layer % 2 == 0 else "w_cd_w_out"
        if self.wout_pre is not None:
            wout = self.wout_pre
        else:
            wout = S.sbuf([128, 16, 1024], BF16, "wout")
            for q in range(4):
                S.dma(wout[:, q * 4:(q + 1) * 4, :],
                      d[wname][i, :, q * 4096:(q + 1) * 4096].rearrange("p (e n) -> p e n", n=1024), eng="pool")
        gate = S.sbuf([128, 1024], F32, "gate")
        S.dma(gate.v(), V(self.MOD, self.MODd[layer, c, :].partition_broadcast(128)))
        if last:
            fg = S.sbuf([128, 1024], F32, "fg")
            S.dma(fg.v(), V(d["w_final_g"], d["w_final_g"].ap[0, :].partition_broadcast(128)))
            junk = S.sbuf([128, 1024], BF16, "junkc")
        ytb = [S.sbuf([128, 16, 512], BF16, "ytb%d" % j) for j in range(2)]
        xt = [S.sbuf([128, 1024], F32, "xtc%d" % j) for j in range(2)]
        tmp = [S.sbuf([128, 1024], F32, "tmpc%d" % j) for j in range(2)]
        ss = [S.sbuf([128, 1], F32, "ssc%d" % j) for j in range(2)]
        nblk = T // 512

        def load_yt(blk_):
            gb_ = (grp["tok0"] + blk_ * 512) // 512
            yb_ = ytb[blk_ % 2]
            for e in range(16):
                S.dma(yb_[:, e, :], self.YT[e][gb_].v())

        def load_x(ti_):
            S.dma(xt[ti_ % 2].v(), self.xsrc(layer, grp, ti_))

        load_yt(0)
        load_x(0)
        for blk in range(nblk):
            yb = ytb[blk % 2]
            if blk + 1 < nblk:
                load_yt(blk + 1)
            for tt_ in range(4):
                ti = blk * 4 + tt_
                xi = grp["tok0"] // 128 + ti
                xb = xt[ti % 2]
                if ti + 1 < nblk * 4:
                    load_x(ti + 1)
                tb = tmp[ti % 2]
                pp = [self.ps(), self.ps()]
                for e in range(16):
                    for nb in range(2):
                        S.mm(pp[nb].v(), yb[:, e, tt_ * 128:(tt_ + 1) * 128], wout[:, e, nb * 512:(nb + 1) * 512],
                             start=(e == 0), stop=(e == 15))
                for nb in range(2):
                    S.tt(tb[:, nb * 512:(nb + 1) * 512], pp[nb].v(), gate[:, nb * 512:(nb + 1) * 512], ALU.mult)
                S.tt(xb.v(), tb.v(), xb.v(), ALU.add)
                if not last:
                    S.dma(self.X[xi].v(), xb.v())
                else:
                    s_ = ss[ti % 2]
                    S.act(junk.v(), xb.v(), AF.Square, accum_out=s_.v())
                    S.act(s_.v(), s_.v(), AF.Sqrt, bias=EPS, scale=1.0 / D)
                    S.recip(s_.v(), s_.v())
                    S.stt(tb.v(), xb.v(), s_.v(), fg.v(), ALU.mult, ALU.mult)
                    ob = self.YP[ti] if grp["name"] == "P" else self.YS[ti]
                    S.dma(ob.v(), tb.v())

    def ab_layer(self, i, layer, grp):
        S = self.S
        d = self.din
        T = grp["T"]
        bs = 512
        self.trif = S.sbuf([128, 4, 128], F32, "trif")
        S.dma(self.trif.v(), d["c_tri"].v())
        self.trib = S.sbuf([128, 4, 128], BF16, "trib")
        S.dma(self.trib.v(), d["c_tri"].v(), eng="pool")
        self.mask = S.sbuf([128, 2, 128], F32, "mask")
        S.dma(self.mask.v(), d["c_mask"].v())
        self.retsp = S.sbuf([128, 4, 2, 128], F32, "retsp")
        S.dma(self.retsp.v(), d["c_retsp"].v())
        wga = S.sbuf([128, 8, 32], BF16, "wga")
        S.dma(wga.v(), d["w_ab_wga"][i].rearrange("p (k n) -> p k n", n=32), eng="pool")
        w2b = S.sbuf([33, 1024], BF16, "w2b")
        S.dma(w2b.v(), d["w_ab_w2b"][i], eng="pool")
        gaT = S.sbuf([33, T], BF16, "gaT")
        S.memset(gaT[32:33, :], 1.0)
        for blk in range(T // bs):
            p = self.ps()
            for k in range(8):
                S.mm(p[0:32, :], wga[:, k, :], self.hT[blk][:, k, :], start=(k == 0), stop=(k == 7))
            S.copy(gaT[0:32, blk * bs:(blk + 1) * bs], p[0:32, :], eng="act")
        NT = T // 128
        B = {}
        B["wh"] = [S.sbuf([128, 8, 768], BF16, "wh%d" % j) for j in range(2)]
        B["gbc"] = [S.sbuf([128, 256], F32, "gbc%d" % j) for j in range(2)]
        PSET = 2 if T <= 1024 else 1
        B["qT"] = [S.sbuf([128, T], BF16, "qT%d" % j) for j in range(PSET)]
        B["kT"] = [S.sbuf([128, T], BF16, "kT%d" % j) for j in range(PSET)]
        B["ktm"] = [S.sbuf([128, NT, 128], BF16, "ktm%d" % j) for j in range(PSET)]
        B["vtm"] = [S.sbuf([128, NT, 256], BF16, "vtm%d" % j) for j in range(PSET)]
        B["gz"] = [S.sbuf([128, NT, 256], BF16, "gz%d" % j) for j in range(PSET)]
        B["sp"] = [S.sbuf([128, NT, 256], BF16, "sp%d" % j) for j in range(PSET)]
        B["qinb"] = S.sbuf([128, NT, 128], BF16, "qinb")
        B["attb"] = S.sbuf([128, NT, 128], BF16, "attb")
        B["Sbb"] = S.sbuf([128, NT, 256], BF16, "Sbb")
        B["osb"] = S.sbuf([128, NT, 256], F32, "osb")
        B["yTs"] = S.sbuf([128, 2, T], BF16, "yTs")
        B["stats"] = S.sbuf([128, NT, 6], F32, "stats")
        B["mv"] = S.sbuf([128, NT, 2], F32, "mv")
        B["ms"] = S.sbuf([128, NT], F32, "ms")
        B["rstd"] = S.sbuf([128, NT], F32, "rstd")
        DP = 4
        B["etmp"] = [S.sbuf([128, 256], F32, "etmp%d" % j) for j in range(2)]
        B["sz"] = [S.sbuf([128, 256], F32, "sz%d" % j) for j in range(2)]
        B["E"] = [S.sbuf([128, 3, 128], F32, "E%d" % j) for j in range(DP)]
        B["Ec"] = [S.sbuf([128, 3, 128], F32, "Ec%d" % j) for j in range(2)]
        B["qin"] = [S.sbuf([128, 128], BF16, "qin%d" % j) for j in range(DP)]
        B["kin"] = [S.sbuf([128, 128], BF16, "kin%d" % j) for j in range(DP)]
        B["kst"] = [S.sbuf([128, 128], BF16, "kst%d" % j) for j in range(DP)]
        B["att"] = [S.sbuf([128, 128], BF16, "att%d" % j) for j in range(DP)]
        B["Sf"] = [S.sbuf([128, 256], F32, "Sf%d" % j) for j in range(2)]
        B["Sb16"] = [S.sbuf([128, 256], BF16, "Sb16%d" % j) for j in range(2)]
        B["tn"] = [S.sbuf([128, 256], F32, "tn%d" % j) for j in range(2)]
        B["ybf"] = [S.sbuf([128, 256], BF16, "ybf%d" % j) for j in range(2)]
        heads = list(getattr(self, "dbg_heads", range(8)))

        def load_head(hh, slot):
            S.dma(B["wh"][slot].v(), d["w_ab_wh"][i, hh].rearrange("p (k n) -> p k n", n=768), eng="pool")
            S.dma(B["gbc"][slot].v(), V(d["w_ab_head_g"], d["w_ab_head_g"].ap[i, hh * 256:(hh + 1) * 256].partition_broadcast(128)))
            S.ts(B["gbc"][slot].v(), B["gbc"][slot].v(), 0.5, None, ALU.mult)

        if heads:
            load_head(heads[0], 0)
        for hi, hh in enumerate(heads):
            ret = hh >= 4
            h = hh % 4
            if hi + 1 < len(heads):
                load_head(heads[hi + 1], (hi + 1) % 2)
            if ret:
                for dr in range(2):
                    p = self.ps()
                    spc = self.retsp[:, h, dr, :]
                    S.mm(p[:, 0:128], spc, self.trif[:, dr, :])
                    S.mm(p[:, 128:256], self.trif[:, 2 + dr, :], spc)
                    Ec = B["Ec"][dr]
                    S.act(Ec[:, 0, :], p[:, 0:128], AF.Exp)
                    S.act(Ec[:, 1, :], p[:, 0:128], AF.Exp, scale=-1.0)
                    S.act(Ec[:, 2, :], p[:, 128:256], AF.Exp)
            Bs = dict(B)
            for k_ in ("qT", "kT", "ktm", "vtm", "gz", "sp"):
                Bs[k_] = B[k_][hi % PSET]
            self.ab_head(i, hh, grp, Bs, gaT, w2b, B["wh"][hi % 2], B["gbc"][hi % 2], DP)

    def ab_head(self, i, hh, grp, B, gaT, w2b, wh, gbc, DP):
        S = self.S
        d = self.din
        ret = hh >= 4
        h = hh % 4
        sc = 128.0 ** -0.5
        sc_q, sc_k = (1.0, sc) if ret else (sc, 1.0)
        for sq in grp["seqs"]:
            L = sq["L"]
            o = sq["o"]
            nT = L // 128
            sub = min(512, L)
            for sb_ in range(L // sub):
                blk, cs = self.hT_cols(o + sb_ * sub, sub)
                for (dst, c0, scl) in ((B["qT"], 0, sc_q), (B["kT"], 128, sc_k)):
                    p = self.ps()
                    for k in range(8):
                        S.mm(p[:, 0:sub], wh[:, k, c0:c0 + 128], self.hT[blk][:, k, cs], start=(k == 0), stop=(k == 7))
                    S.act(dst[:, o + sb_ * sub:o + (sb_ + 1) * sub], p[:, 0:sub], AF.Copy, scale=scl)
            for n in range(nT):
                gc = o // 128 + n
                blk, cs = self.hT_cols(o + n * 128, 128)
                pA = self.ps()
                pB = self.ps()
                for k in range(8):
                    S.mm(pA[:, 0:384], self.hT[blk][:, k, cs], wh[:, k, 128:512], start=(k == 0), stop=(k == 7))
                    S.mm(pB[:, 0:256], self.hT[blk][:, k, cs], wh[:, k, 512:768], start=(k == 0), stop=(k == 7))
                S.ts(B["ktm"][:, gc, :], pA[:, 0:128], sc_k, None, ALU.mult)
                S.ts(B["vtm"][:, gc, :], pA[:, 128:384], 1.0, None, ALU.mult)
                sz = B["sz"][gc % 2]
                zc = B["etmp"][gc % 2]
                S.act(sz.v(), pB[:, 0:256], AF.Tanh, scale=0.5)
                S.act(zc.v(), pB[:, 0:256], AF.Copy)
                S.stt(sz.v(), sz.v(), 1.0, zc.v(), ALU.add, ALU.mult)
                S.tt(B["gz"][:, gc, :], sz.v(), gbc.v(), ALU.mult, eng="pool")
                if not ret:
                    p = self.ps()
                    gt = gaT[:, o + n * 128: o + (n + 1) * 128]
                    S.mm(p[:, 0:128], gt, w2b[:, h * 128:(h + 1) * 128])
                    S.mm(p[:, 128:256], gt, w2b[:, 512 + h * 128: 512 + (h + 1) * 128])
                    S.act(B["sp"][:, gc, :], p[:, 0:256], AF.Exp, scale=-1.0)
        if not ret:
            spf = B["sp"].v().rearrange("p a c -> p (a c)")
            S.act(spf, spf, AF.Ln, bias=1.0)
        if getattr(self, "dbg_stage", 9) < 2:
            return
        for dr in (1, 0):
            its = []
            for sq in grp["seqs"]:
                nT = sq["L"] // 128
                order = list(range(nT - 1, -1, -1)) if dr == 1 else list(range(nT))
                for j_, n in enumerate(order):
                    its.append(dict(sq=sq, n=n, gc=sq["o"] // 128 + n, first=(j_ == 0), last=(j_ == nT - 1)))
            NI = len(its)
            st = dict(cur=0, p2={})
            Sf = B["Sf"]

            def E_of(k):
                return B["Ec"][dr] if ret else B["E"][k % DP]

            def qin_of(k):
                return B["qinb"][:, its[k]["gc"], :] if dr == 1 else B["qin"][k % DP].v()

            def att_of(k):
                return B["attb"][:, its[k]["gc"], :] if dr == 1 else B["att"][k % DP].v()

            def st1(k):
                if ret:
                    return
                gc = its[k]["gc"]
                E = B["E"][k % DP]
                p = self.ps()
                spn = B["sp"][:, gc, dr * 128:(dr + 1) * 128]
                S.mm(p[:, 0:128], spn, self.trib[:, dr, :])
                S.mm(p[:, 128:256], self.trib[:, 2 + dr, :], spn)
                S.act(E[:, 0, :], p[:, 0:128], AF.Exp)
                S.act(E[:, 1, :], p[:, 0:128], AF.Exp, scale=-1.0)
                S.act(E[:, 2, :], p[:, 128:256], AF.Exp)

            def st2(k):
                it = its[k]
                gc = it["gc"]
                E = E_of(k)
                S.tt(qin_of(k), B["qT"][:, gc * 128:(gc + 1) * 128], E[:, 0, :], ALU.mult)
                S.tt(B["kin"][k % DP].v(), B["kT"][:, gc * 128:(gc + 1) * 128], E[:, 1, :], ALU.mult, eng="pool")
                S.tt(B["kst"][k % DP].v(), B["ktm"][:, gc, :], E[:, 2, :], ALU.mult)

            def st3(k):
                gc = its[k]["gc"]
                p2 = self.ps()
                st["p2"][k] = p2
                S.mm(p2[:, 0:128], B["kin"][k % DP].v(), qin_of(k))
                S.mm(p2[:, 128:384], B["kst"][k % DP].v(), B["vtm"][:, gc, :])
                S.tt(att_of(k), p2[:, 0:128], self.mask[:, dr, :], ALU.mult)

            def st4(k):
                it = its[k]
                gc = it["gc"]
                sq = it["sq"]
                E = E_of(k)
                p2 = st["p2"].pop(k)
                cur = st["cur"]
                if it["first"]:
                    if sq["init"]:
                        S.dma(Sf[cur].v(), d["st_ret" if ret else "st_gla"][i, dr, h])
                    else:
                        S.memset(Sf[cur].v(), 0.0)
                dec = E[:, 0, 127:128] if dr == 0 else E[:, 0, 0:1]
                sb16 = B["Sbb"][:, gc, :] if dr == 1 else B["Sb16"][k % 2].v()
                S.copy(sb16, Sf[cur].v(), eng="act")
                S.stt(Sf[1 - cur].v(), Sf[cur].v(), dec, p2[:, 128:384], ALU.mult, ALU.add)
                st["cur"] = 1 - cur
                if dr == 0:
                    po = self.ps()
                    v_n = B["vtm"][:, gc, :]
                    S.mm(po[:, 0:256], att_of(k), v_n, start=True, stop=False)
                    S.mm(po[:, 0:256], qin_of(k), sb16, start=False, stop=False)
                    S.mm(po[:, 0:256], B["attb"][:, gc, :], v_n, start=False, stop=False)
                    S.mm(po[:, 0:256], B["qinb"][:, gc, :], B["Sbb"][:, gc, :], start=False, stop=True)
                    S.copy(B["osb"][:, gc, :], po[:, 0:256], eng="act")
                    S.bn_stats(B["stats"][:, gc, :], B["osb"][:, gc, :])
                if it["last"] and not sq["init"]:
                    dst = self.dout["ns_ret" if ret else "ns_gla"][sq["b"], i, dr, h]
                    S.dma(V(self.OUTB, dst), Sf[st["cur"]].v())

            for s_ in range(NI + 3):
                if s_ < NI:
                    st1(s_)
                if 0 <= s_ - 1 < NI:
                    st2(s_ - 1)
                if 0 <= s_ - 2 < NI:
                    st3(s_ - 2)
                if 0 <= s_ - 3 < NI:
                    st4(s_ - 3)
        if getattr(self, "dbg_stage", 9) < 3:
            return
        NT = grp["T"] // 128
        for gc in range(NT):
            S.bn_aggr(B["mv"][:, gc, :], B["stats"][:, gc, :])
        mean = B["mv"][:, :, 0]
        var = B["mv"][:, :, 1]
        ms = B["ms"].v()
        rstd = B["rstd"].v()
        if ret:
            S.act(rstd, var, AF.Sqrt, bias=EPS)
        else:
            S.tt(ms, mean, mean, ALU.mult)
            S.tt(ms, ms, var, ALU.add)
            S.act(rstd, ms, AF.Sqrt, bias=EPS)
        S.recip(rstd, rstd)
        for gc in range(NT):
            tn = B["tn"][gc % 2]
            mcol = B["mv"][:, gc, 0:1] if ret else self.zeros[:, 0:1]
            S.ts(tn.v(), B["osb"][:, gc, :], mcol, B["rstd"][:, gc:gc + 1], ALU.subtract, ALU.mult)
            yb = B["ybf"][gc % 2]
            S.tt(yb.v(), tn.v(), B["gz"][:, gc, :], ALU.mult, eng="pool")
            pt = self.ps().v().bitcast(BF16)
            S.transpose(pt[:, 0:128], yb[:, 0:128], self.ident.v())
            S.transpose(pt[:, 128:256], yb[:, 128:256], self.ident.v())
            S.copy(B["yTs"][:, :, gc * 128:(gc + 1) * 128], pt[:, 0:256].rearrange("p (j c) -> p j c", c=128), eng="act")
        for sq in grp["seqs"]:
            o, L = sq["o"], sq["L"]
            self.store_yT(grp, sq, hh * 2, B["yTs"][:, 0, o:o + L])
            self.store_yT(grp, sq, hh * 2 + 1, B["yTs"][:, 1, o:o + L])

    def store_yT(self, grp, sq, e, src):
        S = self.S
        L = sq["L"]
        tok = grp["tok0"] + sq["o"]
        sub = min(512, L)
        for sb_ in range(L // sub):
            t0 = tok + sb_ * sub
            gb = t0 // 512
            c0 = t0 % 512
            S.dma(self.YT[e][gb][:, c0:c0 + sub], src[:, sb_ * sub:(sb_ + 1) * sub])

    def cd_layer(self, i, layer, grp):
        S = self.S
        d = self.din
        m_cd = S.mark()
        if os.environ.get("DBG_CD", "") != "hy":
            self.cd_lru(i, grp)
        S.release(m_cd)
        if os.environ.get("DBG_CD", "") != "lru":
            self.cd_hyena(i, grp)

    def cd_proj_fm(self, wseg, grp, sq, dst_fn):
        S = self.S
        L = sq["L"]
        sub = min(512, L)
        for sb_ in range(L // sub):
            blk, cs = self.hT_cols(sq["o"] + sb_ * sub, sub)
            p = self.ps()
            for k in range(8):
                S.mm(p[:, 0:sub], wseg[:, k, :], self.hT[blk][:, k, cs], start=(k == 0), stop=(k == 7))
            dst_fn(sb_ * sub, sub, p[:, 0:sub])

    def cd_lru(self, i, grp):
        S = self.S
        d = self.din
        Lmax = max(s["L"] for s in grp["seqs"])
        NS = 2 if Lmax <= 256 else 1
        wls = [S.sbuf([128, 8, 256], BF16, "wl%d" % j) for j in range(2)]
        wris = [S.sbuf([128, 2, 2, 128], BF16, "wri%d" % j) for j in range(2)]
        nsls = [S.sbuf([128, 4], F32, "nsl%d" % j) for j in range(2)]
        sets = []
        for j in range(NS):
            sets.append(dict(
                xrp=S.sbuf([128, Lmax + 3], F32, "xrp%d" % j), xc=S.sbuf([128, Lmax], F32, "xc%d" % j),
                xcb=S.sbuf([128, Lmax], BF16, "xcb%d" % j), zrs=S.sbuf([128, Lmax], BF16, "zrs%d" % j),
                rg=[S.sbuf([128, Lmax], F32, "rg%d_%d" % (j, q)) for q in range(2)],
                ig=[S.sbuf([128, Lmax], F32, "ig%d_%d" % (j, q)) for q in range(2)],
                aa=[S.sbuf([128, Lmax], F32, "aa%d_%d" % (j, q)) for q in range(2)],
                a2=[S.sbuf([128, Lmax], F32, "a2%d_%d" % (j, q)) for q in range(2)],
                bb=[S.sbuf([128, Lmax], F32, "bb%d_%d" % (j, q)) for q in range(2)],
                hd=[S.sbuf([128, Lmax], F32, "hd%d_%d" % (j, q)) for q in range(2)],
                yr=S.sbuf([128, Lmax], BF16, "yr%d" % j)))

        def load_g(g, slot):
            S.dma(wls[slot].v(), d["w_cd_wg"][i, g].rearrange("p (k n) -> p k n", n=768)[:, :, 512:768], eng="pool")
            for dr in range(2):
                S.dma(wris[slot][:, dr, 0, :], d["w_lru_w_r"][i, dr, g], eng="pool")
                S.dma(wris[slot][:, dr, 1, :], d["w_lru_w_i"][i, dr, g], eng="pool")

        load_g(0, 0)
        si = 0
        for g in range(8):
            wl = wls[g % 2]
            wri = wris[g % 2]
            nsl = nsls[g % 2]
            if g + 1 < 8:
                load_g(g + 1, (g + 1) % 2)
            for dr in range(2):
                lam = self.fm[:, FM["llam"] + i * 16 + dr * 8 + g: FM["llam"] + i * 16 + dr * 8 + g + 1]
                S.act(nsl[:, dr:dr + 1], lam, AF.Exp, scale=-1.0)
                S.act(nsl[:, dr:dr + 1], nsl[:, dr:dr + 1], AF.Ln, bias=1.0)
                S.ts(nsl[:, 2 + dr:3 + dr], nsl[:, dr:dr + 1], -16.0, None, ALU.mult)
                S.ts(nsl[:, dr:dr + 1], nsl[:, dr:dr + 1], -8.0, None, ALU.mult)
            for sq in grp["seqs"]:
                bs_ = sets[si % NS]
                si += 1
                xrp, xc, xcb, zrs, yr = bs_["xrp"], bs_["xc"], bs_["xcb"], bs_["zrs"], bs_["yr"]
                hd_ = bs_["hd"]
                L = sq["L"]
                S.memset(xrp[:, 0:2], 0.0)
                S.memset(xrp[:, L + 2:L + 3], 0.0)
                self.cd_proj_fm(wl[:, :, 0:128], grp, sq,
                                lambda t0, n, pv: S.copy(xrp[:, 2 + t0:2 + t0 + n], pv, eng="act"))
                self.cd_proj_fm(wl[:, :, 128:256], grp, sq,
                                lambda t0, n, pv: S.act(zrs[:, t0:t0 + n], pv, AF.Silu))
                cw0 = FM["lcw"] + i * 32 + g * 4
                cb = self.fm[:, FM["lcb"] + i * 8 + g: FM["lcb"] + i * 8 + g + 1]
                S.ts(xc[:, 0:L], xrp[:, 0:L], self.fm[:, cw0:cw0 + 1], cb, ALU.mult, ALU.add)
                for k in range(1, 4):
                    S.stt(xc[:, 0:L], xrp[:, k:k + L], self.fm[:, cw0 + k:cw0 + k + 1], xc[:, 0:L], ALU.mult, ALU.add)
                S.copy(xcb[:, 0:L], xc[:, 0:L], eng="act")
                sub = min(512, L)
                for dr in range(2):
                    rg, ig, aa, a2, bb = bs_["rg"][dr], bs_["ig"][dr], bs_["aa"][dr], bs_["a2"][dr], bs_["bb"][dr]
                    br = self.fm[:, FM["lbr"] + i * 16 + dr * 8 + g: FM["lbr"] + i * 16 + dr * 8 + g + 1]
                    bi = self.fm[:, FM["lbi"] + i * 16 + dr * 8 + g: FM["lbi"] + i * 16 + dr * 8 + g + 1]
                    for sb_ in range(L // sub):
                        cs = slice(sb_ * sub, (sb_ + 1) * sub)
                        pr = self.ps()
                        S.mm(pr[:, 0:sub], wri[:, dr, 0, :], xcb[:, cs])
                        S.act(rg[:, cs], pr[:, 0:sub], AF.Sigmoid, bias=br)
                        pi_ = self.ps()
                        S.mm(pi_[:, 0:sub], wri[:, dr, 1, :], xcb[:, cs])
                        S.act(ig[:, cs], pi_[:, 0:sub], AF.Sigmoid, bias=bi)
                    S.act(aa[:, 0:L], rg[:, 0:L], AF.Exp, scale=nsl[:, dr:dr + 1])
                    S.act(a2[:, 0:L], rg[:, 0:L], AF.Exp, scale=nsl[:, 2 + dr:3 + dr])
                    S.act(a2[:, 0:L], a2[:, 0:L], AF.Sqrt, scale=-1.0, bias=1.0)
                    S.tt(ig[:, 0:L], ig[:, 0:L], xc[:, 0:L], ALU.mult, eng=("dve" if dr == 0 else "pool"))
                    S.tt(bb[:, 0:L], a2[:, 0:L], ig[:, 0:L], ALU.mult, eng=("pool" if dr == 0 else "dve"))
                    if sq["init"]:
                        c_ = FM["stl"] + i * 16 + dr * 8 + g
                        h0 = self.fm[:, c_:c_ + 1]
                    else:
                        h0 = 0.0
                    hh_ = hd_[dr]
                    if dr == 0:
                        S.scan(hh_[:, 0:L], aa[:, 0:L], bb[:, 0:L], h0)
                        last = hh_[:, L - 1:L]
                    else:
                        S.scan(hh_[:, 0:L][:, ::-1], aa[:, 0:L][:, ::-1], bb[:, 0:L][:, ::-1], h0)
                        last = hh_[:, 0:1]
                    if not sq["init"]:
                        dst = self.dout["ns_lru"][sq["b"], i, dr, g * 128:(g + 1) * 128].rearrange("(p o) -> p o", o=1)
                        S.dma(V(self.OUTB, dst), last)
                S.tt(hd_[0][:, 0:L], hd_[0][:, 0:L], hd_[1][:, 0:L], ALU.add)
                S.tt(yr[:, 0:L], hd_[0][:, 0:L], zrs[:, 0:L], ALU.mult, eng="pool")
                self.store_yT(grp, sq, 8 + g, yr[:, 0:L])

    def cd_hyena(self, i, grp):
        S = self.S
        d = self.din
        L = grp["seqs"][0]["L"]
        nT = L // 128
        nft = 2 * nT + 1
        sub = min(512, L)
        nsub = L // sub
        CW = 512 if L <= 256 else 256
        self.CW = CW
        ngi = CW // 128
        fwd = d["c_fw%d" % L]
        ivd = d["c_iv%d" % L]
        g2a = S.sbuf([65, L], BF16, "g2a")
        S.memset(g2a[64:65, :], 1.0)
        m1 = S.mark()
        feats = S.sbuf([33, L], F32, "feats")
        S.dma(feats.v(), d["c_feats%d" % L].v())
        w1 = S.sbuf([33, 64], F32, "w1")
        S.dma(w1.v(), d["w_hy_w1"][i])
        w2 = S.sbuf([64, 64], F32, "w2")
        S.dma(w2.v(), d["w_hy_w2"][i])
        hyp = S.sbuf([64, 4], F32, "hyp")
        S.dma(hyp.v(), d["w_hyp"][i])
        hsc = S.sbuf([64, 4], F32, "hsc")
        for j in range(2):
            S.ts(hsc[:, 2 * j:2 * j + 1], hyp[:, 2 * j + 1:2 * j + 2], 0.5, None, ALU.mult)
            S.tt(hsc[:, 2 * j + 1:2 * j + 2], hsc[:, 2 * j:2 * j + 1], hyp[:, 2 * j:2 * j + 1], ALU.mult)
        g1 = S.sbuf([64, L], F32, "g1")
        sh = S.sbuf([64, 512], F32, "sh")
        ch = S.sbuf([64, 512], F32, "ch")
        for stage in range(2):
            for sb_ in range(nsub):
                cs = slice(sb_ * sub, (sb_ + 1) * sub)
                p = self.ps()
                if stage == 0:
                    S.mm(p[0:64, 0:sub], w1.v(), feats[:, cs])
                else:
                    S.mm(p[0:64, 0:sub], w2.v(), g1[:, cs])
                scl = hsc[:, 2 * stage:2 * stage + 1]
                bia = hsc[:, 2 * stage + 1:2 * stage + 2]
                S.act(sh[:, 0:sub], p[0:64, 0:sub], AF.Sin, scale=scl, bias=bia)
                S.act(ch[:, 0:sub], p[0:64, 0:sub], AF.Abs, scale=scl, bias=bia)
                S.act(ch[:, 0:sub], ch[:, 0:sub], AF.Sin, scale=-1.0, bias=self.halfpi[0:64, :])
                dst = g1[:, cs] if stage == 0 else g2a[0:64, cs]
                S.stt(dst, sh[:, 0:sub], 2.0, ch[:, 0:sub], ALU.mult, ALU.mult)
        S.release(m1)
        delt = S.sbuf([128, CW], F32, "delt")
        tcol = S.sbuf([128, nT], F32, "tcol")
        S.dma(tcol.v(), d["c_tcol%d" % L].v())
        skipr = S.sbuf([1, 2, CW], F32, "skipr")
        G = [S.sbuf([128, 2, CW], BF16, "G%d" % f) for f in range(nft)]
        NSET = 2 if L <= 256 else 1
        sets = []
        for q_ in range(NSET):
            sets.append(dict(
                Y=[S.sbuf([128, CW], BF16, "Y%d_%d" % (q_, f)) for f in range(nft)],
                vtm=S.sbuf([128, nT, CW], BF16, "vtmh%d" % q_),
                x1s=[S.sbuf([128, L], BF16, "x1s%d_%d" % (q_, j)) for j in range(ngi)],
                fmv=[S.sbuf([128, L], BF16, "fmv%d_%d" % (q_, j)) for j in range(ngi)]))
        raws = [S.sbuf([128, L + 2], BF16, "raw%d" % j) for j in range(2)]
        self.fwb = [S.sbuf([128, max(nT * 128, L)], BF16, "fwb%d" % j) for j in range(4)]
        wseg = S.sbuf([128, 8, 256], BF16, "wseg")
        wsegs = None
        if NSET > 1:
            wsegs = [[S.sbuf([128, 8, 256], BF16, "wsg%d_%d" % (gi, pt)) for pt in range(2)] for gi in range(ngi)]
        ct = [S.sbuf([128, CW], F32, "ct%d" % j) for j in range(8)]
        c3ts = [S.sbuf([128, 512], F32, "c3t%d" % j) for j in range(2)]
        c3t = c3ts[0]
        self.cti = 0
        self.c3i = 0
        taps = S.sbuf([128, nT * 2 * CW], BF16, "taps")
        tapv = taps.v().rearrange("p (a s c) -> p a s c", s=2, c=CW)
        if NSET == 1:
            sets[0]["zhs"] = [taps[:, j * L:(j + 1) * L] for j in range(ngi)]
        else:
            for q_ in range(NSET):
                sets[q_]["zhs"] = [S.sbuf([128, L], BF16, "zhs%d_%d" % (q_, j)).v() for j in range(ngi)]
        w3q = S.sbuf([65, 4, CW], BF16, "w3q")
        wn1 = [S.sbuf([128, CW], F32, "wn%d" % j) for j in range(2)]
        self.fwi = 0

        def conv3(seg, g, wv, sq, dst):
            Ls = sq["L"]
            raw = raws[self.c3i % 2]
            self.c3i += 1
            S.memset(raw[:, 0:1], 0.0)
            S.memset(raw[:, Ls + 1:Ls + 2], 0.0)
            self.cd_proj_fm(wv, grp, sq, lambda t0, n, pv: S.copy(raw[:, 1 + t0:1 + t0 + n], pv, eng="act"))
            w0 = FM["hsw"] + i * 72 + seg * 24 + g * 3
            b0 = FM["hsb"] + i * 24 + seg * 8 + g
            for sb_ in range(Ls // sub):
                c_ = sb_ * sub
                c3t = c3ts[(self.c3i + sb_) % 2]
                S.ts(c3t[:, 0:sub], raw[:, c_:c_ + sub], self.fm[:, w0:w0 + 1], self.fm[:, b0:b0 + 1], ALU.mult, ALU.add)
                S.stt(c3t[:, 0:sub], raw[:, c_ + 1:c_ + 1 + sub], self.fm[:, w0 + 1:w0 + 2], c3t[:, 0:sub], ALU.mult, ALU.add)
                S.stt(dst[:, c_:c_ + sub], raw[:, c_ + 2:c_ + 2 + sub], self.fm[:, w0 + 2:w0 + 3], c3t[:, 0:sub], ALU.mult, ALU.add)

        def make_taps(cv, c0):
            for a in range(nT):
                pff = self.ps()
                pfb = self.ps()
                S.mm(pff[:, 0:CW], g2a[:, a * 128:(a + 1) * 128], w3q[:, 2 * cv, :])
                S.mm(pfb[:, 0:CW], g2a[:, a * 128:(a + 1) * 128], w3q[:, 2 * cv + 1, :])
                wn = wn1[a % 2]
                S.act(wn.v(), delt.v(), AF.Exp, scale=tcol[:, a:a + 1])
                hf = ct[(2 * a) % 8]
                hb = ct[(2 * a + 1) % 8]
                S.tt(hf.v(), pff[:, 0:CW], wn.v(), ALU.mult)
                S.tt(hb.v(), pfb[:, 0:CW], wn.v(), ALU.mult)
                if a == 0:
                    S.tt(hf[0:1, :], hf[0:1, :], skipr[0:1, cv, :], ALU.add)
                    S.tt(tapv[:, a, 0, :], hf.v(), hb.v(), ALU.add, eng="pool")
                    S.tt(hf[0:1, :], hf[0:1, :], skipr[0:1, cv, :], ALU.subtract)
                    S.tt(tapv[:, a, 1, :], hf.v(), hb.v(), ALU.subtract, eng="pool")
                else:
                    S.tt(tapv[:, a, 0, :], hf.v(), hb.v(), ALU.add, eng="pool")
                    S.tt(tapv[:, a, 1, :], hf.v(), hb.v(), ALU.subtract, eng="pool")

        for cq in range(1024 // CW):
            c0 = cq * CW
            S.dma(delt.v(), V(d["c_delta"], d["c_delta"].ap[0, c0:c0 + CW].partition_broadcast(128)))
            S.dma(skipr.v(), d["w_hy_skip"][i:i + 1, :, c0:c0 + CW])
            for k in range(4):
                S.dma(w3q[:, k, :], d["w_w3aug"][i, :, k * 1024 + c0:k * 1024 + c0 + CW], eng="pool")
            if wsegs is not None:
                for gi in range(ngi):
                    g = cq * ngi + gi
                    wv_ = d["w_cd_wg"][i, g].rearrange("p (k n) -> p k n", n=768)
                    S.dma(wsegs[gi][0].v(), wv_[:, :, 0:256], eng="pool")
                    S.dma(wsegs[gi][1].v(), wv_[:, :, 256:512], eng="pool")
            firstseq = True
            for si_, sq in enumerate(grp["seqs"]):
                bs_ = sets[si_ % NSET]
                Y, vtm, x1s, fmv, zhs = bs_["Y"], bs_["vtm"], bs_["x1s"], bs_["fmv"], bs_["zhs"]
                for gi in range(ngi):
                    g = cq * ngi + gi
                    if wsegs is None:
                        S.dma(wseg.v(), d["w_cd_wg"][i, g].rearrange("p (k n) -> p k n", n=768)[:, :, 0:256], eng="pool")
                        w_ = wseg
                    else:
                        w_ = wsegs[gi][0]
                    conv3(0, g, w_[:, :, 0:128], sq, fmv[gi])
                    conv3(1, g, w_[:, :, 128:256], sq, x1s[gi])
                    self.to_tm(fmv[gi], vtm, gi, nT)
                if firstseq:
                    make_taps(0, c0)
                self.hy_fwd(fwd, nT, vtm, G, Y, 0, tapv if firstseq else None, ct)
                self.hy_inv(ivd, nT, L, Y, lambda gi, cs, pv, fmv=fmv, x1s=x1s: S.tt(fmv[gi][:, cs], pv, x1s[gi][:, cs], ALU.mult))
                for gi in range(ngi):
                    self.to_tm(fmv[gi], vtm, gi, nT)
                if firstseq:
                    make_taps(1, c0)
                self.hy_fwd(fwd, nT, vtm, G, Y, 1, tapv if firstseq else None, ct)
                for gi in range(ngi):
                    g = cq * ngi + gi
                    if wsegs is None:
                        S.dma(wseg.v(), d["w_cd_wg"][i, g].rearrange("p (k n) -> p k n", n=768)[:, :, 256:512], eng="pool")
                        w_ = wseg
                    else:
                        w_ = wsegs[gi][1]
                    conv3(2, g, w_[:, :, 0:128], sq, x1s[gi])
                    self.cd_proj_fm(w_[:, :, 128:256], grp, sq,
                                    lambda t0, n, pv, gi=gi, zhs=zhs: S.act(zhs[gi][:, t0:t0 + n], pv, AF.Silu))

                def fin(gi, cs, pv, fmv=fmv, x1s=x1s, zhs=zhs):
                    n_ = cs.stop - cs.start
                    c3f = c3ts[self.c3i % 2]
                    self.c3i += 1
                    S.tt(c3f[:, 0:n_], pv, x1s[gi][:, cs], ALU.mult)
                    S.tt(fmv[gi][:, cs], c3f[:, 0:n_], zhs[gi][:, cs], ALU.mult, eng="pool")
                self.hy_inv(ivd, nT, L, Y, fin)
                for gi in range(ngi):
                    g = cq * ngi + gi
                    self.store_yT(grp, sq, g, fmv[gi][:, 0:L])
                firstseq = False

    def to_tm(self, src, vtm, gi, nT):
        S = self.S
        for a0 in range(0, nT, 8):
            na = min(8, nT - a0)
            pt = self.ps().v().bitcast(BF16)
            for a in range(na):
                S.transpose(pt[:, a * 128:(a + 1) * 128], src[:, (a0 + a) * 128:(a0 + a + 1) * 128], self.ident.v())
            S.copy(vtm[:, a0:a0 + na, gi * 128:(gi + 1) * 128],
                   pt[:, 0:na * 128].rearrange("p (a c) -> p a c", c=128), eng="act")

    def hy_fwd(self, fwd, nT, vtm, G, Y, cv, tapv, ct):
        S = self.S
        CW = self.CW
        fwb = self.fwb

        def load(ft):
            b = fwb[self.fwi % len(fwb)]
            self.fwi += 1
            S.dma(b[:, 0:nT * 128], fwd[ft].rearrange("p a q -> p (a q)"))
            return b

        def dft(b, M, rhs_fn):
            p = self.ps()
            for a in range(nT):
                S.mm(p[0:M, 0:CW], b[:, a * 128:a * 128 + M], rhs_fn(a), start=(a == 0), stop=(a == nT - 1))
            return p

        for j in range(nT + 1):
            if j < nT:
                fts = (j, nT + 1 + j)
                M = 128
            else:
                fts = (nT,)
                M = 1
            pu = []
            for idx, ft in enumerate(fts):
                b = load(ft)
                if tapv is not None:
                    sel = 0 if idx == 0 else 1
                    pg = self.ps()
                    pd = self.ps()
                    for a in range(nT):
                        S.mm(pg[0:M, 0:CW], b[:, a * 128:a * 128 + M], tapv[:, a, sel, :], start=(a == 0), stop=(a == nT - 1))
                        S.mm(pd[0:M, 0:CW], b[:, a * 128:a * 128 + M], vtm[:, a, :], start=(a == 0), stop=(a == nT - 1))
                    S.copy(G[ft][0:M, cv, :], pg[0:M, 0:CW], eng="act")
                    pu.append(pd)
                else:
                    pu.append(dft(b, M, lambda a: vtm[:, a, :]))
            if j < nT:
                ur, ui = pu
                gr = G[fts[0]][:, cv, :]
                gi_ = G[fts[1]][:, cv, :]
                c0_ = 4 * (j % 2)
                S.tt(ct[c0_].v(), ur[:, 0:CW], gr, ALU.mult)
                S.tt(ct[c0_ + 1].v(), ui[:, 0:CW], gi_, ALU.mult)
                S.tt(Y[fts[0]].v(), ct[c0_].v(), ct[c0_ + 1].v(), ALU.subtract, eng="pool")
                S.tt(ct[c0_ + 2].v(), ur[:, 0:CW], gi_, ALU.mult)
                S.tt(ct[c0_ + 3].v(), ui[:, 0:CW], gr, ALU.mult)
                S.tt(Y[fts[1]].v(), ct[c0_ + 2].v(), ct[c0_ + 3].v(), ALU.add, eng="pool")
            else:
                S.tt(Y[nT][0:1, :], pu[0][0:1, 0:CW], G[nT][0:1, cv, :], ALU.mult)

    def hy_inv(self, ivd, nT, L, Y, evac):
        S = self.S
        nft = 2 * nT + 1
        sub = min(512, L)
        nsub = L // sub
        CW = self.CW
        ngi = CW // 128
        acc = [[self.ps() for _ in range(nsub)] for _ in range(ngi)]
        for ft in range(nft):
            b = self.fwb[self.fwi % len(self.fwb)]
            self.fwi += 1
            S.dma(b[:, 0:L], ivd[ft])
            K = 1 if ft == nT else 128
            for gi in range(ngi):
                for sb_ in range(nsub):
                    S.mm(acc[gi][sb_][:, 0:sub], Y[ft][0:K, gi * 128:(gi + 1) * 128], b[0:K, sb_ * sub:(sb_ + 1) * sub],
                         start=(ft == 0), stop=(ft == nft - 1))
        for gi in range(ngi):
            for sb_ in range(nsub):
                evac(gi, slice(sb_ * sub, (sb_ + 1) * sub), acc[gi][sb_][:, 0:sub])


_PROG_CACHE = {}


def _get_prog(consts, wts, n_layers=4, groups=("P", "S")):
    key = (n_layers, groups)
    if key not in _PROG_CACHE:
        pr = Prog(n_layers=n_layers, groups=groups)
        pr.declare(consts, wts)
        pr.build()
        _PROG_CACHE[key] = pr
    return _PROG_CACHE[key]


def make_in_maps(inputs, consts, wts):
    xp = np.asarray(inputs["x_prompt"], np.float32)
    xs = np.asarray(inputs["x_sample"], np.float32)
    c = np.asarray(inputs["c"], np.float32)
    cctx = np.asarray(inputs["c_ctx"], np.float32)
    sg = np.asarray(inputs["state_gla"], np.float32)
    sr = np.asarray(inputs["state_ret"], np.float32)
    sl = np.asarray(inputs["state_lru"], np.float32)
    maps = []
    for core in range(8):
        m = {}
        for k, v in consts.items():
            m["c_" + k] = v
        for k, v in wts.items():
            m["w_" + k] = v
        fm = wts["fm"].copy()
        fm[:, FM["stl"]:FM["stl"] + 32] = sl[core].reshape(2, 2, 8, 128).transpose(3, 0, 1, 2).reshape(128, 32)
        m["w_fm"] = fm
        m["xp"] = np.ascontiguousarray(xp[core * NP_SEQ:(core + 1) * NP_SEQ].reshape(TP, D))
        m["xs"] = np.ascontiguousarray(xs[core])
        cond = np.stack([cctx, c[core]], 0)
        m["condT"] = np.ascontiguousarray(cond.reshape(2, 8, 128).transpose(2, 1, 0).reshape(128, 16))
        m["st_gla"] = np.ascontiguousarray(sg[core])
        m["st_ret"] = np.ascontiguousarray(sr[core])
        maps.append(m)
    return maps


def kernel(**inputs):
    consts = make_consts()
    wts = prep_weights(inputs)
    prog = _get_prog(consts, wts)
    maps = make_in_maps(inputs, consts, wts)
    res = run_bass_kernel_spmd(prog.nc, maps, core_ids=list(range(8)))
    r = res.results
    y_p = np.concatenate([r[c]["y_p"].reshape(NP_SEQ, LP, D) for c in range(8)], 0).astype(np.float32)
    y_s = np.stack([r[c]["y_s"] for c in range(8)], 0).astype(np.float32)
    ng = np.concatenate([r[c]["ns_gla"] for c in range(8)], 0).astype(np.float32)
    nr = np.concatenate([r[c]["ns_ret"] for c in range(8)], 0).astype(np.float32)
    nl = np.concatenate([r[c]["ns_lru"] for c in range(8)], 0).astype(np.float32)
    return (y_p, y_s, ng, nr, nl)
```

```python
import os
import sys
import numpy as np
import ml_dtypes
import concourse.bass as bass
import concourse.mybir as mybir
from concourse.bass_utils import run_bass_kernel_spmd

F32 = mybir.dt.float32
BF16 = mybir.dt.bfloat16
U8 = mybir.dt.uint8
AF = mybir.ActivationFunctionType
ALU = mybir.AluOpType
DSIZE = {F32: 4, BF16: 2, U8: 1}

SAME_ENGINE_SYNC = True
N_DMA_SEMS = 48
STG_N = 768
N_STG = 4
CAST_ENG = "pool"


class Tok:
    __slots__ = ("sem", "val", "op", "key")

    def __init__(self, sem, val, op, key):
        self.sem, self.val, self.op, self.key = sem, val, op, key


class Op:
    __slots__ = ("eng", "fn", "waits", "tok", "is_dma", "needs_inc", "seq")


class Buf:
    def __init__(self, ap, name=""):
        self.ap = ap
        self.name = name
        self.writes = {}
        self.reads = {}

    def __getitem__(self, idx):
        return V(self, self.ap[idx])

    def v(self):
        return V(self, self.ap)


class V:
    __slots__ = ("buf", "ap")

    def __init__(self, buf, ap):
        self.buf, self.ap = buf, ap

    def __getitem__(self, idx):
        return V(self.buf, self.ap[idx])

    def bitcast(self, dt):
        return V(self.buf, self.ap.bitcast(dt))

    def rearrange(self, *a, **k):
        return V(self.buf, self.ap.rearrange(*a, **k))


class Sched:
    ENG = ("pe", "act", "dve", "pool", "sp")

    def __init__(self, nc):
        self.nc = nc
        self.q = {e: [] for e in self.ENG}
        self.esem = {e: nc.alloc_semaphore("es_" + e) for e in self.ENG}
        self.dsems = [nc.alloc_semaphore("ds%d" % i) for i in range(N_DMA_SEMS)]
        self.dval = [0] * N_DMA_SEMS
        self.dlast = [None] * N_DMA_SEMS
        self.dnext = 0
        self.last_tok = {e: None for e in self.ENG}
        self.all_dma = []
        self.sb_base = 16384 + 1024
        self.sb_top = 207 * 1024
        self.live = []
        self.dead = []
        self.opseq = 0
        self.sb_ptr = self.sb_base
        self.uid = 0
        self.sb_max = 0
        self.stg = [self.sbuf([128, STG_N], F32, "stg%d" % i) for i in range(N_STG)]
        self.stgi = 0

    def sbuf(self, shape, dt, name="t"):
        per = int(np.prod(shape[1:])) * DSIZE[dt]
        off = (self.sb_ptr + 63) // 64 * 64
        assert off + per <= self.sb_top, "SBUF overflow %s need %d at %d" % (name, per, off)
        self.sb_ptr = off + per
        self.sb_max = max(self.sb_max, self.sb_ptr)
        self.uid += 1
        t = self.nc.alloc_sbuf_tensor_at("%s_%d" % (name, self.uid), list(shape), dt, offset=off)
        b = Buf(t.ap(), name)
        end = off + per
        keep = []
        for (o2, e2, b2) in self.dead:
            if o2 < end and off < e2:
                for tk in list(b2.writes.values()) + list(b2.reads.values()):
                    old = b.writes.get(tk.key)
                    if old is None or old.op.seq < tk.op.seq:
                        b.writes[tk.key] = tk
                if not (off <= o2 and e2 <= end):
                    keep.append((o2, e2, b2))
            else:
                keep.append((o2, e2, b2))
        self.dead = keep
        self.live.append((off, end, b))
        return b

    def mark(self):
        return self.sb_ptr

    def release(self, m):
        self.sb_ptr = m
        nl = []
        for (o2, e2, b2) in self.live:
            if o2 >= m:
                if b2.writes or b2.reads:
                    self.dead.append((o2, e2, b2))
            else:
                nl.append((o2, e2, b2))
        self.live = nl

    def add(self, eng, fn, reads=(), writes=(), dma=False):
        lim = int(os.environ.get("DBG_LIMIT", "0"))
        self.nrec = getattr(self, "nrec", 0) + 1
        if lim and self.nrec > lim:
            return Tok(self.esem[eng], 0, None, ("x", eng))
        if lim and self.nrec == lim:
            f = sys._getframe(2)
            print("LAST OP #%d eng=%s line=%d / caller line=%d" % (self.nrec, eng, f.f_lineno, f.f_back.f_lineno))
        op = Op()
        op.eng, op.fn, op.is_dma, op.needs_inc = eng, fn, dma, False
        self.opseq += 1
        op.seq = self.opseq
        deps = {}

        def dep(t):
            if t is None:
                return
            deps[id(t)] = t

        wb = set(id(w.buf) for w in writes)
        for r in reads:
            for t in r.buf.writes.values():
                dep(t)
        for w in writes:
            for t in w.buf.writes.values():
                dep(t)
            for t in w.buf.reads.values():
                dep(t)
        if dma:
            i = self.dnext
            self.dnext = (self.dnext + 1) % N_DMA_SEMS
            dep(self.dlast[i])
            self.dval[i] += 16
            tok = Tok(self.dsems[i], self.dval[i], op, ("d", i))
            self.dlast[i] = tok
            self.all_dma.append(tok)
        else:
            tok = Tok(self.esem[eng], None, op, ("e", eng))
        op.tok = tok
        waits = []
        for t in deps.values():
            if (not t.op.is_dma) and t.op.eng == eng:
                if eng == "pe" or not SAME_ENGINE_SYNC:
                    continue
            if not t.op.is_dma:
                t.op.needs_inc = True
            waits.append(t)
        op.waits = waits
        for w in writes:
            w.buf.writes = {tok.key: tok}
            w.buf.reads = {}
        for r in reads:
            if id(r.buf) not in wb:
                r.buf.reads[tok.key] = tok
        self.q[eng].append(op)
        if not dma:
            self.last_tok[eng] = tok
        return tok

    def barrier(self):
        toks = [t for t in self.last_tok.values() if t is not None]
        toks += [t for t in self.dlast if t is not None]
        for e in self.ENG:
            op = Op()
            op.eng, op.fn, op.is_dma, op.needs_inc = e, None, False, False
            self.opseq += 1
            op.seq = self.opseq
            op.tok = Tok(self.esem[e], None, op, ("e", e))
            op.waits = []
            for t in toks:
                if (not t.op.is_dma) and t.op.eng == e:
                    continue
                if not t.op.is_dma:
                    t.op.needs_inc = True
                op.waits.append(t)
            self.q[e].append(op)

    def emit(self):
        nc = self.nc
        for e in self.ENG:
            c = 0
            for op in self.q[e]:
                if not op.is_dma and op.needs_inc:
                    c += 1
                if not op.is_dma:
                    op.tok.val = c
        engobj = {"pe": "tensor", "act": "scalar", "dve": "vector", "pool": "gpsimd", "sp": "sync"}
        self.stats = {}

        def replay(e, eng):
            seen = {}
            nw = 0
            dump = os.environ.get("DBG_DUMP") == e
            for oi, op in enumerate(self.q[e]):
                if dump and oi >= len(self.q[e]) - 12:
                    print("  [%s %d] fn=%s dma=%s inc=%s val=%s waits=%s" % (e, oi, "none" if op.fn is None else "op", op.is_dma, op.needs_inc, op.tok.val,
                          [(t.sem.name, t.val) for t in op.waits]))
                for t in op.waits:
                    k = t.sem.num
                    if seen.get(k, 0) >= t.val:
                        continue
                    seen[k] = t.val
                    eng.wait_ge(t.sem, t.val)
                    nw += 1
                if op.fn is None:
                    continue
                ins = op.fn(eng)
                if op.is_dma:
                    ins.then_inc(op.tok.sem, 16)
                elif op.needs_inc:
                    ins.then_inc(op.tok.sem, 1)
            self.stats[e] = (len(self.q[e]), nw)

        with nc.Block() as block:
            @block.tensor
            def _(eng):
                replay("pe", eng)

            @block.scalar
            def _(eng):
                replay("act", eng)

            @block.vector
            def _(eng):
                replay("dve", eng)

            @block.gpsimd
            def _(eng):
                replay("pool", eng)

            @block.sync
            def _(eng):
                replay("sp", eng)

    def dma(self, out, in_, eng="sp", **kw):
        if eng == "pool":
            return self.load_cast(out, in_)
        return self.add(eng, lambda e: e.dma_start(out=out.ap, in_=in_.ap, **kw),
                        reads=[in_], writes=[out], dma=True)

    def cast_eng(self):
        self.casti = getattr(self, "casti", 0) + 1
        return ("act", "dve")[self.casti % 2]

    def load_cast(self, dst, src):
        shp = list(dst.ap.shape)
        P = shp[0]
        tok = None
        if len(shp) == 2:
            n = shp[1]
            for c0 in range(0, n, STG_N):
                c1 = min(n, c0 + STG_N)
                st = self.stg[self.stgi % N_STG]
                self.stgi += 1
                self.dma(st[0:P, 0:c1 - c0], src[:, c0:c1])
                tok = self.copy(dst[:, c0:c1], st[0:P, 0:c1 - c0], eng=self.cast_eng())
        else:
            assert len(shp) == 3
            A, Bn = shp[1], shp[2]
            if Bn > STG_N:
                for a_ in range(A):
                    tok = self.load_cast(dst[:, a_, :], src[:, a_, :])
                return tok
            step = max(1, STG_N // Bn)
            for a0 in range(0, A, step):
                a1 = min(A, a0 + step)
                st = self.stg[self.stgi % N_STG]
                self.stgi += 1
                sv = st[0:P, 0:(a1 - a0) * Bn].rearrange("p (a b) -> p a b", b=Bn)
                self.dma(sv, src[:, a0:a1, :])
                tok = self.copy(dst[:, a0:a1, :], sv, eng=self.cast_eng())
        return tok

    def mm(self, out, lhsT, rhs, start=True, stop=True):
        return self.add("pe", lambda e: e.matmul(out.ap, lhsT.ap, rhs.ap, start=start, stop=stop),
                        reads=[lhsT, rhs], writes=[out])

    def transpose(self, out, in_, ident):
        return self.add("pe", lambda e: e.transpose(out.ap, in_.ap, ident.ap),
                        reads=[in_, ident], writes=[out])

    def act(self, out, in_, func, bias=None, scale=None, accum_out=None, eng="act"):
        reads = [in_]
        kw = {}
        if bias is not None:
            if isinstance(bias, V):
                reads.append(bias)
                kw["bias"] = bias.ap
            else:
                kw["bias"] = bias
        if scale is not None:
            if isinstance(scale, V):
                reads.append(scale)
                kw["scale"] = scale.ap
            else:
                kw["scale"] = scale
        writes = [out]
        if accum_out is not None:
            writes.append(accum_out)
            kw["accum_out"] = accum_out.ap
        return self.add(eng, lambda e: e.activation(out.ap, in_.ap, func, **kw), reads=reads, writes=writes)

    def tt(self, out, in0, in1, op, eng="dve"):
        return self.add(eng, lambda e: e.tensor_tensor(out.ap, in0.ap, in1.ap, op), reads=[in0, in1], writes=[out])

    def ts(self, out, in0, s1, s2, op0, op1=None, eng="dve"):
        reads = [in0]
        a1 = s1.ap if isinstance(s1, V) else s1
        a2 = s2.ap if isinstance(s2, V) else s2
        if isinstance(s1, V):
            reads.append(s1)
        if isinstance(s2, V):
            reads.append(s2)
        if op1 is None:
            return self.add(eng, lambda e: e.tensor_scalar(out.ap, in0.ap, a1, a2, op0), reads=reads, writes=[out])
        return self.add(eng, lambda e: e.tensor_scalar(out.ap, in0.ap, a1, a2, op0, op1), reads=reads, writes=[out])

    def stt(self, out, in0, scalar, in1, op0, op1):
        reads = [in0, in1]
        a = scalar.ap if isinstance(scalar, V) else scalar
        if isinstance(scalar, V):
            reads.append(scalar)
        return self.add("dve", lambda e: e.scalar_tensor_tensor(out.ap, in0.ap, a, in1.ap, op0, op1),
                        reads=reads, writes=[out])

    def scan(self, out, d0, d1, init, op0=None, op1=None):
        reads = [d0, d1]
        a = init.ap if isinstance(init, V) else init
        if isinstance(init, V):
            reads.append(init)
        o0 = op0 or ALU.mult
        o1 = op1 or ALU.add
        return self.add("dve", lambda e: e.tensor_tensor_scan(out.ap, d0.ap, d1.ap, a, o0, o1),
                        reads=reads, writes=[out])

    def copy(self, out, in_, eng="dve"):
        if eng == "act":
            return self.add("act", lambda e: e.copy(out.ap, in_.ap), reads=[in_], writes=[out])
        return self.add(eng, lambda e: e.tensor_copy(out.ap, in_.ap), reads=[in_], writes=[out])

    def memset(self, out, val, eng="dve"):
        return self.add(eng, lambda e: e.memset(out.ap, val), reads=[], writes=[out])

    def bn_stats(self, out, in_):
        return self.add("dve", lambda e: e.bn_stats(out.ap, in_.ap), reads=[in_], writes=[out])

    def bn_aggr(self, out, in_):
        return self.add("dve", lambda e: e.bn_aggr(out.ap, in_.ap), reads=[in_], writes=[out])

    def recip(self, out, in_):
        return self.add("dve", lambda e: e.reciprocal(out.ap, in_.ap), reads=[in_], writes=[out])

import math

D = 1024
EPS = 1e-6
NP_SEQ = 4
LP = 256
LS = 2048
TP = NP_SEQ * LP
TTOT = TP + LS
CW = 256
HY_DELTA = np.abs(np.linspace(math.log(1e-2) / 0.3, math.log(1e-2) / 1.5, 1024, dtype=np.float32)).astype(np.float32)

FM = {}
_o = 0
for _n, _w in [("norm_g", 32), ("hsw", 144), ("hsb", 48), ("lcw", 64), ("lcb", 16), ("lbr", 32),
               ("lbi", 32), ("llam", 32), ("stl", 32)]:
    FM[_n] = _o
    _o += _w
FM_W = _o


def _bf(a):
    return np.ascontiguousarray(a).astype(ml_dtypes.bfloat16)


def make_consts():
    c = {}
    c["ident"] = np.eye(128, dtype=np.float32)
    tp = np.arange(128)[:, None]
    t = np.arange(128)[None, :]
    tri = np.zeros((128, 4, 128), np.float32)
    tri[:, 0] = (tp <= t) * (-1.0 / 16)
    tri[:, 1] = (tp >= t) * (-1.0 / 16)
    tri[:, 2] = (tp > t) * (-1.0 / 16)
    tri[:, 3] = (tp < t) * (-1.0 / 16)
    c["tri"] = tri
    mask = np.zeros((128, 2, 128), np.float32)
    mask[:, 0] = (tp <= t)
    mask[:, 1] = (tp >= t)
    c["mask"] = mask
    step = (12.0 - 5.0) / 3.0
    rsp = np.zeros((128, 4, 2, 128), np.float32)
    for h in range(4):
        for d in range(2):
            expo = np.float32(5.0 + step * (h + 0.5 * d))
            lg = np.log1p(-np.exp2(-np.float64(expo)))
            rsp[:, h, d, :] = -16.0 * lg
    c["retsp"] = rsp
    c["delta"] = HY_DELTA[None, :].copy()
    for L in (LP, LS):
        nT = L // 128
        nft = 2 * nT + 1
        tt = np.arange(L, dtype=np.float64)
        FW = np.zeros((nft, L, 128), np.float64)
        IV = np.zeros((nft, 128, L), np.float64)
        for j in range(nT):
            f = (np.arange(128) + 128 * j).astype(np.float64)
            th = np.pi * np.outer(tt, f) / L
            FW[j] = np.cos(th)
            FW[nT + 1 + j] = -np.sin(th)
            w = np.where(f == 0, 1.0, 2.0)
            IV[j] = (w[:, None] * np.cos(th.T)) / (2 * L)
            IV[nT + 1 + j] = (-2.0 * np.sin(th.T)) / (2 * L)
        FW[nT, :, 0] = np.cos(np.pi * tt)
        IV[nT, 0, :] = np.cos(np.pi * tt) / (2 * L)
        c["fw%d" % L] = _bf(FW.reshape(nft, nT, 128, 128).transpose(0, 2, 1, 3))
        c["iv%d" % L] = _bf(IV)
        tl = np.linspace(0.0, 1.0, L, dtype=np.float32)[:, None]
        bands = 16
        fb = np.linspace(1e-4, bands - 1, bands, dtype=np.float32)[None, :]
        ang = (np.float32(2.0 * math.pi / L) * np.arange(L, dtype=np.float32)[:, None] * fb).astype(np.float32)
        feats = np.concatenate([tl, np.cos(ang), -np.sin(ang)], axis=-1).astype(np.float32)
        c["feats%d" % L] = np.ascontiguousarray(feats.T)
        c["tcol%d" % L] = np.ascontiguousarray((-tl[:, 0]).reshape(nT, 128).T).astype(np.float32)
    return c


def prep_weights(inp):
    w = {}
    g = lambda k: np.asarray(inp[k], dtype=np.float32)
    w["w_mod"] = g("w_mod")
    w["b_mod"] = g("b_mod")
    abw = g("ab_w_in")
    offs = dict(qa=0, ka=512, va=1024, ga=2048, za=2080, qb=3104, kb=3616, vb=4128, zb=5152)
    wh = np.zeros((2, 8, 128, 8, 768), np.float32)
    wga = np.zeros((2, 128, 8, 32), np.float32)
    for i in range(2):
        for hh in range(8):
            h = hh % 4
            if hh < 4:
                cols = [abw[i][:, offs["qa"] + h * 128: offs["qa"] + (h + 1) * 128],
                        abw[i][:, offs["ka"] + h * 128: offs["ka"] + (h + 1) * 128],
                        abw[i][:, offs["va"] + h * 256: offs["va"] + (h + 1) * 256],
                        abw[i][:, offs["za"] + h * 256: offs["za"] + (h + 1) * 256]]
            else:
                cols = [abw[i][:, offs["qb"] + h * 128: offs["qb"] + (h + 1) * 128],
                        abw[i][:, offs["kb"] + h * 128: offs["kb"] + (h + 1) * 128],
                        abw[i][:, offs["vb"] + h * 256: offs["vb"] + (h + 1) * 256],
                        abw[i][:, offs["zb"] + h * 256: offs["zb"] + (h + 1) * 256]]
            m = np.concatenate(cols, axis=1)
            wh[i, hh] = m.reshape(8, 128, 768).transpose(1, 0, 2)
        wga[i] = abw[i][:, 2048:2080].reshape(8, 128, 32).transpose(1, 0, 2)
    w["ab_wh"] = wh.reshape(2, 8, 128, 8 * 768)
    w["ab_wga"] = wga.reshape(2, 128, 8 * 32)
    gw2 = g("ab_gate_w2")
    gb = g("ab_gate_b")
    w2b = np.zeros((2, 33, 1024), np.float32)
    for i in range(2):
        w2b[i, 0:16, 0:512] = gw2[i, 0]
        w2b[i, 16:32, 512:1024] = gw2[i, 1]
        w2b[i, 32, 0:512] = gb[i, 0]
        w2b[i, 32, 512:1024] = gb[i, 1]
    w["ab_w2b"] = w2b
    w["ab_head_g"] = g("ab_head_g")
    w["ab_w_out"] = np.ascontiguousarray(g("ab_w_out").reshape(2, 16, 128, 1024).transpose(0, 2, 1, 3)).reshape(2, 128, 16 * 1024)
    w["cd_w_out"] = np.ascontiguousarray(g("cd_w_out").reshape(2, 16, 128, 1024).transpose(0, 2, 1, 3)).reshape(2, 128, 16 * 1024)
    cdw = g("cd_w_in")
    wg = np.zeros((2, 8, 128, 8, 768), np.float32)
    for i in range(2):
        for gg in range(8):
            cols = [cdw[i][:, s * 1024 + gg * 128: s * 1024 + (gg + 1) * 128] for s in range(6)]
            m = np.concatenate(cols, axis=1)
            wg[i, gg] = m.reshape(8, 128, 768).transpose(1, 0, 2)
    w["cd_wg"] = wg.reshape(2, 8, 128, 8 * 768)
    fm = np.zeros((128, FM_W), np.float32)
    fm[:, FM["norm_g"]:FM["norm_g"] + 32] = g("norm_g").reshape(4, 8, 128).transpose(2, 0, 1).reshape(128, 32)
    hsw = g("hy_short_w")
    fm[:, FM["hsw"]:FM["hsw"] + 144] = hsw.reshape(2, 3, 3, 8, 128).transpose(4, 0, 2, 3, 1).reshape(128, 144)
    fm[:, FM["hsb"]:FM["hsb"] + 48] = g("hy_short_b").reshape(2, 3, 8, 128).transpose(3, 0, 1, 2).reshape(128, 48)
    fm[:, FM["lcw"]:FM["lcw"] + 64] = g("lru_conv_w").reshape(2, 4, 8, 128).transpose(3, 0, 2, 1).reshape(128, 64)
    fm[:, FM["lcb"]:FM["lcb"] + 16] = g("lru_conv_b").reshape(2, 8, 128).transpose(2, 0, 1).reshape(128, 16)
    for nm, key in (("lbr", "lru_b_r"), ("lbi", "lru_b_i"), ("llam", "lru_lambda")):
        fm[:, FM[nm]:FM[nm] + 32] = g(key).reshape(2, 2, 8, 128).transpose(3, 0, 1, 2).reshape(128, 32)
    w["fm"] = fm
    w["hy_skip"] = g("hy_skip")
    w["lru_w_r"] = g("lru_w_r")
    w["lru_w_i"] = g("lru_w_i")
    w["hy_w1"] = g("hy_w1")
    w["hy_w2"] = g("hy_w2")
    hyp = np.zeros((2, 64, 4), np.float32)
    hyp[:, :, 0] = g("hy_b1")
    hyp[:, :, 1] = g("hy_freq1")
    hyp[:, :, 2] = g("hy_b2")
    hyp[:, :, 3] = g("hy_freq2")
    w["hyp"] = hyp
    w["w3aug"] = np.concatenate([g("hy_w3"), g("hy_b3")[:, None, :]], axis=1)
    w["final_g"] = g("final_g").reshape(1, 1024)
    return w


DRAM_IN_SPECS = None


class Prog:
    def __init__(self, n_layers=4, groups=("P", "S")):
        self.n_layers = n_layers
        self.groups = groups
        nc = bass.Bass("TRN2", target_bir_lowering=False)
        self.nc = nc
        self.S = Sched(nc)
        self.din = {}
        self.dout = {}

    def inp(self, name, shape, dt=F32):
        ap = self.nc.dram_tensor(name, list(shape), dt, kind="ExternalInput").ap()
        self.din[name] = Buf(ap, name)
        return self.din[name]

    def outp(self, name, shape):
        ap = self.nc.dram_tensor(name, list(shape), F32, kind="ExternalOutput").ap()
        self.dout[name] = ap
        return ap

    def ps(self):
        b = self.PS[self.ps_i % len(self.PS)]
        self.ps_i += 1
        return b

    def declare(self, consts, wts):
        for k, v in consts.items():
            self.inp("c_" + k, v.shape, BF16 if v.dtype == ml_dtypes.bfloat16 else F32)
        for k, v in wts.items():
            self.inp("w_" + k, v.shape, F32)
        self.inp("xp", (TP, D))
        self.inp("xs", (LS, D))
        self.inp("condT", (128, 16))
        self.inp("st_gla", (2, 2, 4, 128, 256))
        self.inp("st_ret", (2, 2, 4, 128, 256))
        self.outp("y_p", (TP, D))
        self.outp("y_s", (LS, D))
        self.outp("ns_gla", (NP_SEQ, 2, 2, 4, 128, 256))
        self.outp("ns_ret", (NP_SEQ, 2, 2, 4, 128, 256))
        self.outp("ns_lru", (NP_SEQ, 2, 2, 1024))
        nc = self.nc
        self.Xd = nc.dram_tensor("x_scr", [TTOT, D], F32, kind="Internal").ap()
        self.X = [Buf(self.Xd[t * 128:(t + 1) * 128, :], "x%d" % t) for t in range(TTOT // 128)]
        self.YTd = nc.dram_tensor("yt_scr", [2048, TTOT], BF16, kind="Internal").ap()
        self.YT = [[Buf(self.YTd[e * 128:(e + 1) * 128, b * 512:(b + 1) * 512], "yt") for b in range(TTOT // 512)]
                   for e in range(16)]
        self.MODd = nc.dram_tensor("mod_scr", [4, 2, 1024], F32, kind="Internal").ap()
        self.MOD = Buf(self.MODd, "modscr")
        self.YP = [Buf(self.dout["y_p"][t * 128:(t + 1) * 128, :]) for t in range(TP // 128)]
        self.YS = [Buf(self.dout["y_s"][t * 128:(t + 1) * 128, :]) for t in range(LS // 128)]
        self.OUTB = Buf(self.dout["ns_gla"], "o")
        self.PS = [Buf(nc.alloc_psum_tensor("ps%d" % i, [128, 512], F32).ap(), "ps%d" % i) for i in range(8)]
        self.ps_i = 0

    def build(self):
        S = self.S
        d = self.din
        self.ident = S.sbuf([128, 128], BF16, "ident")
        S.dma(self.ident.v(), d["c_ident"].v(), eng="pool")
        self.halfpi = S.sbuf([128, 1], F32, "halfpi")
        S.memset(self.halfpi.v(), math.pi / 2)
        self.fm = S.sbuf([128, FM_W], F32, "fm")
        S.dma(self.fm.v(), d["w_fm"].v())
        self.zeros = S.sbuf([128, 16], F32, "zeros")
        S.memset(self.zeros.v(), 0.0)
        self.modT = S.sbuf([128, 4, 2, 16], F32, "modT")
        self.gs = S.sbuf([128, 4, 2, 8], F32, "gs")
        self.prologue()
        S.barrier()
        base_mark = S.mark()
        for layer in range(self.n_layers):
            for gname in self.groups:
                if gname == "P":
                    grp = dict(name="P", tok0=0, T=TP, cond=0,
                               seqs=[dict(o=j * LP, L=LP, b=j, init=False) for j in range(NP_SEQ)])
                else:
                    grp = dict(name="S", tok0=TP, T=LS, cond=1, seqs=[dict(o=0, L=LS, b=None, init=True)])
                S.release(base_mark)
                self.phase_a(layer, grp)
                if layer % 2 == 0:
                    self.ab_layer(layer // 2, layer, grp)
                else:
                    self.cd_layer(layer // 2, layer, grp)
                S.release(self.mix_mark)
                self.phase_c(layer, grp, last=(layer == self.n_layers - 1))
        S.barrier()
        S.emit()

    def prologue(self):
        S = self.S
        d = self.din
        m0 = S.mark()
        condT = S.sbuf([128, 8, 2], F32, "condT")
        S.dma(condT.v(), d["condT"].v().rearrange("p (k c) -> p k c", c=2))
        scT = S.sbuf([128, 8, 2], F32, "scT")
        S.act(scT.v(), condT.v(), AF.Silu)
        wm = [S.sbuf([128, 8, 512], F32, "wm%d" % i) for i in range(4)]
        bm = S.sbuf([2, 3072], F32, "bm")
        modrow = S.sbuf([2, 3072], F32, "modrow")
        id2 = S.sbuf([2, 2], F32, "id2")
        S.dma(id2.v(), d["c_ident"][0:2, 0:2])
        cnt = 0
        for l in range(self.n_layers):
            S.dma(bm[0:1, :], d["w_b_mod"][l:l + 1, :])
            S.dma(bm[1:2, :], d["w_b_mod"][l:l + 1, :])
            for nb in range(6):
                w = wm[cnt % 4]
                cnt += 1
                S.dma(w.v(), d["w_w_mod"][l, :, nb * 512:(nb + 1) * 512].rearrange("(k p) n -> p k n", p=128))
                p = self.ps()
                for k in range(8):
                    S.mm(p[0:2, :], scT[:, k, :], w[:, k, :], start=(k == 0), stop=(k == 7))
                S.tt(modrow[:, nb * 512:(nb + 1) * 512], p[0:2, :], bm[:, nb * 512:(nb + 1) * 512], ALU.add)
            S.dma(V(self.MOD, self.MODd[l]), modrow[:, 2048:3072])
            p = self.ps()
            for j in range(16):
                S.transpose(p[:, j * 2:(j + 1) * 2], modrow[:, j * 128:(j + 1) * 128], id2.v())
            for c in range(2):
                S.copy(self.modT[:, l, c, :], p[:, 0:32].rearrange("p (j c) -> p c j", c=2)[:, c, :], eng="dve")
                S.ts(self.gs[:, l, c, :], self.modT[:, l, c, 8:16], 1.0, None, ALU.add)
                S.tt(self.gs[:, l, c, :], self.gs[:, l, c, :], self.fm[:, FM["norm_g"] + l * 8: FM["norm_g"] + (l + 1) * 8], ALU.mult)
        S.barrier()
        S.release(m0)

    def xsrc(self, layer, grp, ti):
        if layer == 0:
            src = self.din["xp"] if grp["name"] == "P" else self.din["xs"]
            return src[ti * 128:(ti + 1) * 128, :]
        return self.X[grp["tok0"] // 128 + ti].v()

    def hT_cols(self, gtok, n):
        blk = gtok // 512
        c0 = gtok % 512
        assert c0 + n <= 512
        return blk, slice(c0, c0 + n)

    def phase_a(self, layer, grp):
        S = self.S
        T = grp["T"]
        c = grp["cond"]
        self.hT = [S.sbuf([128, 8, 512], BF16, "hT%d" % b) for b in range(T // 512)]
        self.wout_pre = None
        if False and grp["name"] == "P":
            i_ = layer // 2
            wname_ = "w_ab_w_out" if layer % 2 == 0 else "w_cd_w_out"
            self.wout_pre = S.sbuf([128, 16, 1024], BF16, "woutp")
        self.mix_mark = S.mark()
        m0 = S.mark()
        xt = [S.sbuf([128, 1024], F32, "xt%d" % i) for i in range(2)]
        xn = [S.sbuf([128, 1024], BF16, "xn%d" % i) for i in range(2)]
        junk = S.sbuf([128, 1024], BF16, "junk")
        ss = [S.sbuf([128, 1], F32, "ss%d" % i) for i in range(2)]
        nTt = T // 128

        def stage1(ti):
            xb = xt[ti % 2]
            S.dma(xb.v(), self.xsrc(layer, grp, ti))
            s_ = ss[ti % 2]
            S.act(junk.v(), xb.v(), AF.Square, accum_out=s_.v())
            S.act(s_.v(), s_.v(), AF.Sqrt, bias=EPS, scale=1.0 / D)
            S.recip(s_.v(), s_.v())
            S.ts(xn[ti % 2].v(), xb.v(), s_.v(), None, ALU.mult)

        def stage2(ti):
            xnb = xn[ti % 2]
            pst = self.ps().v().bitcast(BF16)
            for k in range(8):
                S.transpose(pst[:, k * 128:(k + 1) * 128], xnb[:, k * 128:(k + 1) * 128], self.ident.v())
            blk, cs = self.hT_cols(ti * 128, 128)
            for k in range(8):
                if ti % 2 == 0:
                    S.act(self.hT[blk][:, k, cs], pst[:, k * 128:(k + 1) * 128], AF.Identity,
                          scale=self.gs[:, layer, c, k:k + 1], bias=self.modT[:, layer, c, k:k + 1])
                else:
                    S.ts(self.hT[blk][:, k, cs], pst[:, k * 128:(k + 1) * 128],
                         self.gs[:, layer, c, k:k + 1], self.modT[:, layer, c, k:k + 1], ALU.mult, ALU.add)

        for ti in range(nTt + 1):
            if ti < nTt:
                stage1(ti)
            if ti >= 1:
                stage2(ti - 1)
        if self.wout_pre is not None:
            for q in range(4):
                S.dma(self.wout_pre[:, q * 4:(q + 1) * 4, :],
                      self.din[wname_][i_, :, q * 4096:(q + 1) * 4096].rearrange("p (e n) -> p e n", n=1024), eng="pool")
        S.release(m0)

    def phase_c(self, layer, grp, last):
        S = self.S
        d = self.din
        T = grp["T"]
        c = grp["cond"]
        i = layer // 2
        wname = "w_ab_w_out" if layer % 2 == 0 else "w_cd_w_out"
        if self.wout_pre is not None:
            wout = self.wout_pre
        else:
            wout = S.sbuf([128, 16, 1024], BF16, "wout")
            for q in range(4):
                S.dma(wout[:, q * 4:(q + 1) * 4, :],
                      d[wname][i, :, q * 4096:(q + 1) * 4096].rearrange("p (e n) -> p e n", n=1024), eng="pool")
        gate = S.sbuf([128, 1024], F32, "gate")
        S.dma(gate.v(), V(self.MOD, self.MODd[layer, c, :].partition_broadcast(128)))
        if last:
            fg = S.sbuf([128, 1024], F32, "fg")
            S.dma(fg.v(), V(d["w_final_g"], d["w_final_g"].ap[0, :].partition_broadcast(128)))
            junk = S.sbuf([128, 1024], BF16, "junkc")
        ytb = [S.sbuf([128, 16, 512], BF16, "ytb%d" % j) for j in range(2)]
        xt = [S.sbuf([128, 1024], F32, "xtc%d" % j) for j in range(2)]
        tmp = [S.sbuf([128, 1024], F32, "tmpc%d" % j) for j in range(2)]
        ss = [S.sbuf([128, 1], F32, "ssc%d" % j) for j in range(2)]
        nblk = T // 512

        def load_yt(blk_):
            gb_ = (grp["tok0"] + blk_ * 512) // 512
            yb_ = ytb[blk_ % 2]
            for e in range(16):
                S.dma(yb_[:, e, :], self.YT[e][gb_].v())

        def load_x(ti_):
            S.dma(xt[ti_ % 2].v(), self.xsrc(layer, grp, ti_))

        load_yt(0)
        load_x(0)
        for blk in range(nblk):
            yb = ytb[blk % 2]
            if blk + 1 < nblk:
                load_yt(blk + 1)
            for tt_ in range(4):
                ti = blk * 4 + tt_
                xi = grp["tok0"] // 128 + ti
                xb = xt[ti % 2]
                if ti + 1 < nblk * 4:
                    load_x(ti + 1)
                tb = tmp[ti % 2]
                pp = [self.ps(), self.ps()]
                for e in range(16):
                    for nb in range(2):
                        S.mm(pp[nb].v(), yb[:, e, tt_ * 128:(tt_ + 1) * 128], wout[:, e, nb * 512:(nb + 1) * 512],
                             start=(e == 0), stop=(e == 15))
                for nb in range(2):
                    S.tt(tb[:, nb * 512:(nb + 1) * 512], pp[nb].v(), gate[:, nb * 512:(nb + 1) * 512], ALU.mult)
                S.tt(xb.v(), tb.v(), xb.v(), ALU.add)
                if not last:
                    S.dma(self.X[xi].v(), xb.v())
                else:
                    s_ = ss[ti % 2]
                    S.act(junk.v(), xb.v(), AF.Square, accum_out=s_.v())
                    S.act(s_.v(), s_.v(), AF.Sqrt, bias=EPS, scale=1.0 / D)
                    S.recip(s_.v(), s_.v())
                    S.stt(tb.v(), xb.v(), s_.v(), fg.v(), ALU.mult, ALU.mult)
                    ob = self.YP[ti] if grp["name"] == "P" else self.YS[ti]
                    S.dma(ob.v(), tb.v())

    def ab_layer(self, i, layer, grp):
        S = self.S
        d = self.din
        T = grp["T"]
        bs = 512
        self.trif = S.sbuf([128, 4, 128], F32, "trif")
        S.dma(self.trif.v(), d["c_tri"].v())
        self.trib = S.sbuf([128, 4, 128], BF16, "trib")
        S.dma(self.trib.v(), d["c_tri"].v(), eng="pool")
        self.mask = S.sbuf([128, 2, 128], F32, "mask")
        S.dma(self.mask.v(), d["c_mask"].v())
        self.retsp = S.sbuf([128, 4, 2, 128], F32, "retsp")
        S.dma(self.retsp.v(), d["c_retsp"].v())
        wga = S.sbuf([128, 8, 32], BF16, "wga")
        S.dma(wga.v(), d["w_ab_wga"][i].rearrange("p (k n) -> p k n", n=32), eng="pool")
        w2b = S.sbuf([33, 1024], BF16, "w2b")
        S.dma(w2b.v(), d["w_ab_w2b"][i], eng="pool")
        gaT = S.sbuf([33, T], BF16, "gaT")
        S.memset(gaT[32:33, :], 1.0)
        for blk in range(T // bs):
            p = self.ps()
            for k in range(8):
                S.mm(p[0:32, :], wga[:, k, :], self.hT[blk][:, k, :], start=(k == 0), stop=(k == 7))
            S.copy(gaT[0:32, blk * bs:(blk + 1) * bs], p[0:32, :], eng="act")
        NT = T // 128
        B = {}
        B["wh"] = [S.sbuf([128, 8, 768], BF16, "wh%d" % j) for j in range(2)]
        B["gbc"] = [S.sbuf([128, 256], F32, "gbc%d" % j) for j in range(2)]
        PSET = 2 if T <= 1024 else 1
        B["qT"] = [S.sbuf([128, T], BF16, "qT%d" % j) for j in range(PSET)]
        B["kT"] = [S.sbuf([128, T], BF16, "kT%d" % j) for j in range(PSET)]
        B["ktm"] = [S.sbuf([128, NT, 128], BF16, "ktm%d" % j) for j in range(PSET)]
        B["vtm"] = [S.sbuf([128, NT, 256], BF16, "vtm%d" % j) for j in range(PSET)]
        B["gz"] = [S.sbuf([128, NT, 256], BF16, "gz%d" % j) for j in range(PSET)]
        B["sp"] = [S.sbuf([128, NT, 256], BF16, "sp%d" % j) for j in range(PSET)]
        B["qinb"] = S.sbuf([128, NT, 128], BF16, "qinb")
        B["attb"] = S.sbuf([128, NT, 128], BF16, "attb")
        B["Sbb"] = S.sbuf([128, NT, 256], BF16, "Sbb")
        B["osb"] = S.sbuf([128, NT, 256], F32, "osb")
        B["yTs"] = S.sbuf([128, 2, T], BF16, "yTs")
        B["stats"] = S.sbuf([128, NT, 6], F32, "stats")
        B["mv"] = S.sbuf([128, NT, 2], F32, "mv")
        B["ms"] = S.sbuf([128, NT], F32, "ms")
        B["rstd"] = S.sbuf([128, NT], F32, "rstd")
        DP = 4
        B["etmp"] = [S.sbuf([128, 256], F32, "etmp%d" % j) for j in range(2)]
        B["sz"] = [S.sbuf([128, 256], F32, "sz%d" % j) for j in range(2)]
        B["E"] = [S.sbuf([128, 3, 128], F32, "E%d" % j) for j in range(DP)]
        B["Ec"] = [S.sbuf([128, 3, 128], F32, "Ec%d" % j) for j in range(2)]
        B["qin"] = [S.sbuf([128, 128], BF16, "qin%d" % j) for j in range(DP)]
        B["kin"] = [S.sbuf([128, 128], BF16, "kin%d" % j) for j in range(DP)]
        B["kst"] = [S.sbuf([128, 128], BF16, "kst%d" % j) for j in range(DP)]
        B["att"] = [S.sbuf([128, 128], BF16, "att%d" % j) for j in range(DP)]
        B["Sf"] = [S.sbuf([128, 256], F32, "Sf%d" % j) for j in range(2)]
        B["Sb16"] = [S.sbuf([128, 256], BF16, "Sb16%d" % j) for j in range(2)]
        B["tn"] = [S.sbuf([128, 256], F32, "tn%d" % j) for j in range(2)]
        B["ybf"] = [S.sbuf([128, 256], BF16, "ybf%d" % j) for j in range(2)]
        heads = list(getattr(self, "dbg_heads", range(8)))

        def load_head(hh, slot):
            S.dma(B["wh"][slot].v(), d["w_ab_wh"][i, hh].rearrange("p (k n) -> p k n", n=768), eng="pool")
            S.dma(B["gbc"][slot].v(), V(d["w_ab_head_g"], d["w_ab_head_g"].ap[i, hh * 256:(hh + 1) * 256].partition_broadcast(128)))
            S.ts(B["gbc"][slot].v(), B["gbc"][slot].v(), 0.5, None, ALU.mult)

        if heads:
            load_head(heads[0], 0)
        for hi, hh in enumerate(heads):
            ret = hh >= 4
            h = hh % 4
            if hi + 1 < len(heads):
                load_head(heads[hi + 1], (hi + 1) % 2)
            if ret:
                for dr in range(2):
                    p = self.ps()
                    spc = self.retsp[:, h, dr, :]
                    S.mm(p[:, 0:128], spc, self.trif[:, dr, :])
                    S.mm(p[:, 128:256], self.trif[:, 2 + dr, :], spc)
                    Ec = B["Ec"][dr]
                    S.act(Ec[:, 0, :], p[:, 0:128], AF.Exp)
                    S.act(Ec[:, 1, :], p[:, 0:128], AF.Exp, scale=-1.0)
                    S.act(Ec[:, 2, :], p[:, 128:256], AF.Exp)
            Bs = dict(B)
            for k_ in ("qT", "kT", "ktm", "vtm", "gz", "sp"):
                Bs[k_] = B[k_][hi % PSET]
            self.ab_head(i, hh, grp, Bs, gaT, w2b, B["wh"][hi % 2], B["gbc"][hi % 2], DP)

    def ab_head(self, i, hh, grp, B, gaT, w2b, wh, gbc, DP):
        S = self.S
        d = self.din
        ret = hh >= 4
        h = hh % 4
        sc = 128.0 ** -0.5
        sc_q, sc_k = (1.0, sc) if ret else (sc, 1.0)
        for sq in grp["seqs"]:
            L = sq["L"]
            o = sq["o"]
            nT = L // 128
            sub = min(512, L)
            for sb_ in range(L // sub):
                blk, cs = self.hT_cols(o + sb_ * sub, sub)
                for (dst, c0, scl) in ((B["qT"], 0, sc_q), (B["kT"], 128, sc_k)):
                    p = self.ps()
                    for k in range(8):
                        S.mm(p[:, 0:sub], wh[:, k, c0:c0 + 128], self.hT[blk][:, k, cs], start=(k == 0), stop=(k == 7))
                    S.act(dst[:, o + sb_ * sub:o + (sb_ + 1) * sub], p[:, 0:sub], AF.Copy, scale=scl)
            for n in range(nT):
                gc = o // 128 + n
                blk, cs = self.hT_cols(o + n * 128, 128)
                pA = self.ps()
                pB = self.ps()
                for k in range(8):
                    S.mm(pA[:, 0:384], self.hT[blk][:, k, cs], wh[:, k, 128:512], start=(k == 0), stop=(k == 7))
                    S.mm(pB[:, 0:256], self.hT[blk][:, k, cs], wh[:, k, 512:768], start=(k == 0), stop=(k == 7))
                S.ts(B["ktm"][:, gc, :], pA[:, 0:128], sc_k, None, ALU.mult)
                S.ts(B["vtm"][:, gc, :], pA[:, 128:384], 1.0, None, ALU.mult)
                sz = B["sz"][gc % 2]
                zc = B["etmp"][gc % 2]
                S.act(sz.v(), pB[:, 0:256], AF.Tanh, scale=0.5)
                S.act(zc.v(), pB[:, 0:256], AF.Copy)
                S.stt(sz.v(), sz.v(), 1.0, zc.v(), ALU.add, ALU.mult)
                S.tt(B["gz"][:, gc, :], sz.v(), gbc.v(), ALU.mult, eng="pool")
                if not ret:
                    p = self.ps()
                    gt = gaT[:, o + n * 128: o + (n + 1) * 128]
                    S.mm(p[:, 0:128], gt, w2b[:, h * 128:(h + 1) * 128])
                    S.mm(p[:, 128:256], gt, w2b[:, 512 + h * 128: 512 + (h + 1) * 128])
                    S.act(B["sp"][:, gc, :], p[:, 0:256], AF.Exp, scale=-1.0)
        if not ret:
            spf = B["sp"].v().rearrange("p a c -> p (a c)")
            S.act(spf, spf, AF.Ln, bias=1.0)
        if getattr(self, "dbg_stage", 9) < 2:
            return
        for dr in (1, 0):
            its = []
            for sq in grp["seqs"]:
                nT = sq["L"] // 128
                order = list(range(nT - 1, -1, -1)) if dr == 1 else list(range(nT))
                for j_, n in enumerate(order):
                    its.append(dict(sq=sq, n=n, gc=sq["o"] // 128 + n, first=(j_ == 0), last=(j_ == nT - 1)))
            NI = len(its)
            st = dict(cur=0, p2={})
            Sf = B["Sf"]

            def E_of(k):
                return B["Ec"][dr] if ret else B["E"][k % DP]

            def qin_of(k):
                return B["qinb"][:, its[k]["gc"], :] if dr == 1 else B["qin"][k % DP].v()

            def att_of(k):
                return B["attb"][:, its[k]["gc"], :] if dr == 1 else B["att"][k % DP].v()

            def st1(k):
                if ret:
                    return
                gc = its[k]["gc"]
                E = B["E"][k % DP]
                p = self.ps()
                spn = B["sp"][:, gc, dr * 128:(dr + 1) * 128]
                S.mm(p[:, 0:128], spn, self.trib[:, dr, :])
                S.mm(p[:, 128:256], self.trib[:, 2 + dr, :], spn)
                S.act(E[:, 0, :], p[:, 0:128], AF.Exp)
                S.act(E[:, 1, :], p[:, 0:128], AF.Exp, scale=-1.0)
                S.act(E[:, 2, :], p[:, 128:256], AF.Exp)

            def st2(k):
                it = its[k]
                gc = it["gc"]
                E = E_of(k)
                S.tt(qin_of(k), B["qT"][:, gc * 128:(gc + 1) * 128], E[:, 0, :], ALU.mult)
                S.tt(B["kin"][k % DP].v(), B["kT"][:, gc * 128:(gc + 1) * 128], E[:, 1, :], ALU.mult, eng="pool")
                S.tt(B["kst"][k % DP].v(), B["ktm"][:, gc, :], E[:, 2, :], ALU.mult)

            def st3(k):
                gc = its[k]["gc"]
                p2 = self.ps()
                st["p2"][k] = p2
                S.mm(p2[:, 0:128], B["kin"][k % DP].v(), qin_of(k))
                S.mm(p2[:, 128:384], B["kst"][k % DP].v(), B["vtm"][:, gc, :])
                S.tt(att_of(k), p2[:, 0:128], self.mask[:, dr, :], ALU.mult)

            def st4(k):
                it = its[k]
                gc = it["gc"]
                sq = it["sq"]
                E = E_of(k)
                p2 = st["p2"].pop(k)
                cur = st["cur"]
                if it["first"]:
                    if sq["init"]:
                        S.dma(Sf[cur].v(), d["st_ret" if ret else "st_gla"][i, dr, h])
                    else:
                        S.memset(Sf[cur].v(), 0.0)
                dec = E[:, 0, 127:128] if dr == 0 else E[:, 0, 0:1]
                sb16 = B["Sbb"][:, gc, :] if dr == 1 else B["Sb16"][k % 2].v()
                S.copy(sb16, Sf[cur].v(), eng="act")
                S.stt(Sf[1 - cur].v(), Sf[cur].v(), dec, p2[:, 128:384], ALU.mult, ALU.add)
                st["cur"] = 1 - cur
                if dr == 0:
                    po = self.ps()
                    v_n = B["vtm"][:, gc, :]
                    S.mm(po[:, 0:256], att_of(k), v_n, start=True, stop=False)
                    S.mm(po[:, 0:256], qin_of(k), sb16, start=False, stop=False)
                    S.mm(po[:, 0:256], B["attb"][:, gc, :], v_n, start=False, stop=False)
                    S.mm(po[:, 0:256], B["qinb"][:, gc, :], B["Sbb"][:, gc, :], start=False, stop=True)
                    S.copy(B["osb"][:, gc, :], po[:, 0:256], eng="act")
                    S.bn_stats(B["stats"][:, gc, :], B["osb"][:, gc, :])
                if it["last"] and not sq["init"]:
                    dst = self.dout["ns_ret" if ret else "ns_gla"][sq["b"], i, dr, h]
                    S.dma(V(self.OUTB, dst), Sf[st["cur"]].v())

            for s_ in range(NI + 3):
                if s_ < NI:
                    st1(s_)
                if 0 <= s_ - 1 < NI:
                    st2(s_ - 1)
                if 0 <= s_ - 2 < NI:
                    st3(s_ - 2)
                if 0 <= s_ - 3 < NI:
                    st4(s_ - 3)
        if getattr(self, "dbg_stage", 9) < 3:
            return
        NT = grp["T"] // 128
        for gc in range(NT):
            S.bn_aggr(B["mv"][:, gc, :], B["stats"][:, gc, :])
        mean = B["mv"][:, :, 0]
        var = B["mv"][:, :, 1]
        ms = B["ms"].v()
        rstd = B["rstd"].v()
        if ret:
            S.act(rstd, var, AF.Sqrt, bias=EPS)
        else:
            S.tt(ms, mean, mean, ALU.mult)
            S.tt(ms, ms, var, ALU.add)
            S.act(rstd, ms, AF.Sqrt, bias=EPS)
        S.recip(rstd, rstd)
        for gc in range(NT):
            tn = B["tn"][gc % 2]
            mcol = B["mv"][:, gc, 0:1] if ret else self.zeros[:, 0:1]
            S.ts(tn.v(), B["osb"][:, gc, :], mcol, B["rstd"][:, gc:gc + 1], ALU.subtract, ALU.mult)
            yb = B["ybf"][gc % 2]
            S.tt(yb.v(), tn.v(), B["gz"][:, gc, :], ALU.mult, eng="pool")
            pt = self.ps().v().bitcast(BF16)
            S.transpose(pt[:, 0:128], yb[:, 0:128], self.ident.v())
            S.transpose(pt[:, 128:256], yb[:, 128:256], self.ident.v())
            S.copy(B["yTs"][:, :, gc * 128:(gc + 1) * 128], pt[:, 0:256].rearrange("p (j c) -> p j c", c=128), eng="act")
        for sq in grp["seqs"]:
            o, L = sq["o"], sq["L"]
            self.store_yT(grp, sq, hh * 2, B["yTs"][:, 0, o:o + L])
            self.store_yT(grp, sq, hh * 2 + 1, B["yTs"][:, 1, o:o + L])

    def store_yT(self, grp, sq, e, src):
        S = self.S
        L = sq["L"]
        tok = grp["tok0"] + sq["o"]
        sub = min(512, L)
        for sb_ in range(L // sub):
            t0 = tok + sb_ * sub
            gb = t0 // 512
            c0 = t0 % 512
            S.dma(self.YT[e][gb][:, c0:c0 + sub], src[:, sb_ * sub:(sb_ + 1) * sub])

    def cd_layer(self, i, layer, grp):
        S = self.S
        d = self.din
        m_cd = S.mark()
        if os.environ.get("DBG_CD", "") != "hy":
            self.cd_lru(i, grp)
        S.release(m_cd)
        if os.environ.get("DBG_CD", "") != "lru":
            self.cd_hyena(i, grp)

    def cd_proj_fm(self, wseg, grp, sq, dst_fn):
        S = self.S
        L = sq["L"]
        sub = min(512, L)
        for sb_ in range(L // sub):
            blk, cs = self.hT_cols(sq["o"] + sb_ * sub, sub)
            p = self.ps()
            for k in range(8):
                S.mm(p[:, 0:sub], wseg[:, k, :], self.hT[blk][:, k, cs], start=(k == 0), stop=(k == 7))
            dst_fn(sb_ * sub, sub, p[:, 0:sub])

    def cd_lru(self, i, grp):
        S = self.S
        d = self.din
        Lmax = max(s["L"] for s in grp["seqs"])
        NS = 2 if Lmax <= 256 else 1
        wls = [S.sbuf([128, 8, 256], BF16, "wl%d" % j) for j in range(2)]
        wris = [S.sbuf([128, 2, 2, 128], BF16, "wri%d" % j) for j in range(2)]
        nsls = [S.sbuf([128, 4], F32, "nsl%d" % j) for j in range(2)]
        sets = []
        for j in range(NS):
            sets.append(dict(
                xrp=S.sbuf([128, Lmax + 3], F32, "xrp%d" % j), xc=S.sbuf([128, Lmax], F32, "xc%d" % j),
                xcb=S.sbuf([128, Lmax], BF16, "xcb%d" % j), zrs=S.sbuf([128, Lmax], BF16, "zrs%d" % j),
                rg=[S.sbuf([128, Lmax], F32, "rg%d_%d" % (j, q)) for q in range(2)],
                ig=[S.sbuf([128, Lmax], F32, "ig%d_%d" % (j, q)) for q in range(2)],
                aa=[S.sbuf([128, Lmax], F32, "aa%d_%d" % (j, q)) for q in range(2)],
                a2=[S.sbuf([128, Lmax], F32, "a2%d_%d" % (j, q)) for q in range(2)],
                bb=[S.sbuf([128, Lmax], F32, "bb%d_%d" % (j, q)) for q in range(2)],
                hd=[S.sbuf([128, Lmax], F32, "hd%d_%d" % (j, q)) for q in range(2)],
                yr=S.sbuf([128, Lmax], BF16, "yr%d" % j)))

        def load_g(g, slot):
            S.dma(wls[slot].v(), d["w_cd_wg"][i, g].rearrange("p (k n) -> p k n", n=768)[:, :, 512:768], eng="pool")
            for dr in range(2):
                S.dma(wris[slot][:, dr, 0, :], d["w_lru_w_r"][i, dr, g], eng="pool")
                S.dma(wris[slot][:, dr, 1, :], d["w_lru_w_i"][i, dr, g], eng="pool")

        load_g(0, 0)
        si = 0
        for g in range(8):
            wl = wls[g % 2]
            wri = wris[g % 2]
            nsl = nsls[g % 2]
            if g + 1 < 8:
                load_g(g + 1, (g + 1) % 2)
            for dr in range(2):
                lam = self.fm[:, FM["llam"] + i * 16 + dr * 8 + g: FM["llam"] + i * 16 + dr * 8 + g + 1]
                S.act(nsl[:, dr:dr + 1], lam, AF.Exp, scale=-1.0)
                S.act(nsl[:, dr:dr + 1], nsl[:, dr:dr + 1], AF.Ln, bias=1.0)
                S.ts(nsl[:, 2 + dr:3 + dr], nsl[:, dr:dr + 1], -16.0, None, ALU.mult)
                S.ts(nsl[:, dr:dr + 1], nsl[:, dr:dr + 1], -8.0, None, ALU.mult)
            for sq in grp["seqs"]:
                bs_ = sets[si % NS]
                si += 1
                xrp, xc, xcb, zrs, yr = bs_["xrp"], bs_["xc"], bs_["xcb"], bs_["zrs"], bs_["yr"]
                hd_ = bs_["hd"]
                L = sq["L"]
                S.memset(xrp[:, 0:2], 0.0)
                S.memset(xrp[:, L + 2:L + 3], 0.0)
                self.cd_proj_fm(wl[:, :, 0:128], grp, sq,
                                lambda t0, n, pv: S.copy(xrp[:, 2 + t0:2 + t0 + n], pv, eng="act"))
                self.cd_proj_fm(wl[:, :, 128:256], grp, sq,
                                lambda t0, n, pv: S.act(zrs[:, t0:t0 + n], pv, AF.Silu))
                cw0 = FM["lcw"] + i * 32 + g * 4
                cb = self.fm[:, FM["lcb"] + i * 8 + g: FM["lcb"] + i * 8 + g + 1]
                S.ts(xc[:, 0:L], xrp[:, 0:L], self.fm[:, cw0:cw0 + 1], cb, ALU.mult, ALU.add)
                for k in range(1, 4):
                    S.stt(xc[:, 0:L], xrp[:, k:k + L], self.fm[:, cw0 + k:cw0 + k + 1], xc[:, 0:L], ALU.mult, ALU.add)
                S.copy(xcb[:, 0:L], xc[:, 0:L], eng="act")
                sub = min(512, L)
                for dr in range(2):
                    rg, ig, aa, a2, bb = bs_["rg"][dr], bs_["ig"][dr], bs_["aa"][dr], bs_["a2"][dr], bs_["bb"][dr]
                    br = self.fm[:, FM["lbr"] + i * 16 + dr * 8 + g: FM["lbr"] + i * 16 + dr * 8 + g + 1]
                    bi = self.fm[:, FM["lbi"] + i * 16 + dr * 8 + g: FM["lbi"] + i * 16 + dr * 8 + g + 1]
                    for sb_ in range(L // sub):
                        cs = slice(sb_ * sub, (sb_ + 1) * sub)
                        pr = self.ps()
                        S.mm(pr[:, 0:sub], wri[:, dr, 0, :], xcb[:, cs])
                        S.act(rg[:, cs], pr[:, 0:sub], AF.Sigmoid, bias=br)
                        pi_ = self.ps()
                        S.mm(pi_[:, 0:sub], wri[:, dr, 1, :], xcb[:, cs])
                        S.act(ig[:, cs], pi_[:, 0:sub], AF.Sigmoid, bias=bi)
                    S.act(aa[:, 0:L], rg[:, 0:L], AF.Exp, scale=nsl[:, dr:dr + 1])
                    S.act(a2[:, 0:L], rg[:, 0:L], AF.Exp, scale=nsl[:, 2 + dr:3 + dr])
                    S.act(a2[:, 0:L], a2[:, 0:L], AF.Sqrt, scale=-1.0, bias=1.0)
                    S.tt(ig[:, 0:L], ig[:, 0:L], xc[:, 0:L], ALU.mult, eng=("dve" if dr == 0 else "pool"))
                    S.tt(bb[:, 0:L], a2[:, 0:L], ig[:, 0:L], ALU.mult, eng=("pool" if dr == 0 else "dve"))
                    if sq["init"]:
                        c_ = FM["stl"] + i * 16 + dr * 8 + g
                        h0 = self.fm[:, c_:c_ + 1]
                    else:
                        h0 = 0.0
                    hh_ = hd_[dr]
                    if dr == 0:
                        S.scan(hh_[:, 0:L], aa[:, 0:L], bb[:, 0:L], h0)
                        last = hh_[:, L - 1:L]
                    else:
                        S.scan(hh_[:, 0:L][:, ::-1], aa[:, 0:L][:, ::-1], bb[:, 0:L][:, ::-1], h0)
                        last = hh_[:, 0:1]
                    if not sq["init"]:
                        dst = self.dout["ns_lru"][sq["b"], i, dr, g * 128:(g + 1) * 128].rearrange("(p o) -> p o", o=1)
                        S.dma(V(self.OUTB, dst), last)
                S.tt(hd_[0][:, 0:L], hd_[0][:, 0:L], hd_[1][:, 0:L], ALU.add)
                S.tt(yr[:, 0:L], hd_[0][:, 0:L], zrs[:, 0:L], ALU.mult, eng="pool")
                self.store_yT(grp, sq, 8 + g, yr[:, 0:L])

    def cd_hyena(self, i, grp):
        S = self.S
        d = self.din
        L = grp["seqs"][0]["L"]
        nT = L // 128
        nft = 2 * nT + 1
        sub = min(512, L)
        nsub = L // sub
        CW = 512 if L <= 256 else 256
        self.CW = CW
        ngi = CW // 128
        fwd = d["c_fw%d" % L]
        ivd = d["c_iv%d" % L]
        g2a = S.sbuf([65, L], BF16, "g2a")
        S.memset(g2a[64:65, :], 1.0)
        m1 = S.mark()
        feats = S.sbuf([33, L], F32, "feats")
        S.dma(feats.v(), d["c_feats%d" % L].v())
        w1 = S.sbuf([33, 64], F32, "w1")
        S.dma(w1.v(), d["w_hy_w1"][i])
        w2 = S.sbuf([64, 64], F32, "w2")
        S.dma(w2.v(), d["w_hy_w2"][i])
        hyp = S.sbuf([64, 4], F32, "hyp")
        S.dma(hyp.v(), d["w_hyp"][i])
        hsc = S.sbuf([64, 4], F32, "hsc")
        for j in range(2):
            S.ts(hsc[:, 2 * j:2 * j + 1], hyp[:, 2 * j + 1:2 * j + 2], 0.5, None, ALU.mult)
            S.tt(hsc[:, 2 * j + 1:2 * j + 2], hsc[:, 2 * j:2 * j + 1], hyp[:, 2 * j:2 * j + 1], ALU.mult)
        g1 = S.sbuf([64, L], F32, "g1")
        sh = S.sbuf([64, 512], F32, "sh")
        ch = S.sbuf([64, 512], F32, "ch")
        for stage in range(2):
            for sb_ in range(nsub):
                cs = slice(sb_ * sub, (sb_ + 1) * sub)
                p = self.ps()
                if stage == 0:
                    S.mm(p[0:64, 0:sub], w1.v(), feats[:, cs])
                else:
                    S.mm(p[0:64, 0:sub], w2.v(), g1[:, cs])
                scl = hsc[:, 2 * stage:2 * stage + 1]
                bia = hsc[:, 2 * stage + 1:2 * stage + 2]
                S.act(sh[:, 0:sub], p[0:64, 0:sub], AF.Sin, scale=scl, bias=bia)
                S.act(ch[:, 0:sub], p[0:64, 0:sub], AF.Abs, scale=scl, bias=bia)
                S.act(ch[:, 0:sub], ch[:, 0:sub], AF.Sin, scale=-1.0, bias=self.halfpi[0:64, :])
                dst = g1[:, cs] if stage == 0 else g2a[0:64, cs]
                S.stt(dst, sh[:, 0:sub], 2.0, ch[:, 0:sub], ALU.mult, ALU.mult)
        S.release(m1)
        delt = S.sbuf([128, CW], F32, "delt")
        tcol = S.sbuf([128, nT], F32, "tcol")
        S.dma(tcol.v(), d["c_tcol%d" % L].v())
        skipr = S.sbuf([1, 2, CW], F32, "skipr")
        G = [S.sbuf([128, 2, CW], BF16, "G%d" % f) for f in range(nft)]
        NSET = 2 if L <= 256 else 1
        sets = []
        for q_ in range(NSET):
            sets.append(dict(
                Y=[S.sbuf([128, CW], BF16, "Y%d_%d" % (q_, f)) for f in range(nft)],
                vtm=S.sbuf([128, nT, CW], BF16, "vtmh%d" % q_),
                x1s=[S.sbuf([128, L], BF16, "x1s%d_%d" % (q_, j)) for j in range(ngi)],
                fmv=[S.sbuf([128, L], BF16, "fmv%d_%d" % (q_, j)) for j in range(ngi)]))
        raws = [S.sbuf([128, L + 2], BF16, "raw%d" % j) for j in range(2)]
        self.fwb = [S.sbuf([128, max(nT * 128, L)], BF16, "fwb%d" % j) for j in range(8 if L <= 256 else 4)]
        wseg = S.sbuf([128, 8, 256], BF16, "wseg")
        wsegs = None
        if NSET > 1:
            wsegs = [[S.sbuf([128, 8, 256], BF16, "wsg%d_%d" % (gi, pt)) for pt in range(2)] for gi in range(ngi)]
        ct = [S.sbuf([128, CW], F32, "ct%d" % j) for j in range(8)]
        c3ts = [S.sbuf([128, 512], F32, "c3t%d" % j) for j in range(2)]
        c3t = c3ts[0]
        self.cti = 0
        self.c3i = 0
        taps = S.sbuf([128, nT * 2 * CW], BF16, "taps")
        tapv = taps.v().rearrange("p (a s c) -> p a s c", s=2, c=CW)
        if NSET == 1:
            sets[0]["zhs"] = [taps[:, j * L:(j + 1) * L] for j in range(ngi)]
        else:
            for q_ in range(NSET):
                sets[q_]["zhs"] = [S.sbuf([128, L], BF16, "zhs%d_%d" % (q_, j)).v() for j in range(ngi)]
        w3q = S.sbuf([65, 4, CW], BF16, "w3q")
        wn1 = [S.sbuf([128, CW], F32, "wn%d" % j) for j in range(2)]
        self.fwi = 0

        def conv3(seg, g, wv, sq, dst):
            Ls = sq["L"]
            raw = raws[self.c3i % 2]
            self.c3i += 1
            S.memset(raw[:, 0:1], 0.0)
            S.memset(raw[:, Ls + 1:Ls + 2], 0.0)
            self.cd_proj_fm(wv, grp, sq, lambda t0, n, pv: S.copy(raw[:, 1 + t0:1 + t0 + n], pv, eng="act"))
            w0 = FM["hsw"] + i * 72 + seg * 24 + g * 3
            b0 = FM["hsb"] + i * 24 + seg * 8 + g
            for sb_ in range(Ls // sub):
                c_ = sb_ * sub
                c3t = c3ts[(self.c3i + sb_) % 2]
                S.ts(c3t[:, 0:sub], raw[:, c_:c_ + sub], self.fm[:, w0:w0 + 1], self.fm[:, b0:b0 + 1], ALU.mult, ALU.add)
                S.stt(c3t[:, 0:sub], raw[:, c_ + 1:c_ + 1 + sub], self.fm[:, w0 + 1:w0 + 2], c3t[:, 0:sub], ALU.mult, ALU.add)
                S.stt(dst[:, c_:c_ + sub], raw[:, c_ + 2:c_ + 2 + sub], self.fm[:, w0 + 2:w0 + 3], c3t[:, 0:sub], ALU.mult, ALU.add)

        def make_taps(cv, c0):
            for a in range(nT):
                pff = self.ps()
                pfb = self.ps()
                S.mm(pff[:, 0:CW], g2a[:, a * 128:(a + 1) * 128], w3q[:, 2 * cv, :])
                S.mm(pfb[:, 0:CW], g2a[:, a * 128:(a + 1) * 128], w3q[:, 2 * cv + 1, :])
                wn = wn1[a % 2]
                S.act(wn.v(), delt.v(), AF.Exp, scale=tcol[:, a:a + 1])
                hf = ct[(2 * a) % 8]
                hb = ct[(2 * a + 1) % 8]
                S.tt(hf.v(), pff[:, 0:CW], wn.v(), ALU.mult)
                S.tt(hb.v(), pfb[:, 0:CW], wn.v(), ALU.mult)
                if a == 0:
                    S.tt(hf[0:1, :], hf[0:1, :], skipr[0:1, cv, :], ALU.add)
                    S.tt(tapv[:, a, 0, :], hf.v(), hb.v(), ALU.add, eng="pool")
                    S.tt(hf[0:1, :], hf[0:1, :], skipr[0:1, cv, :], ALU.subtract)
                    S.tt(tapv[:, a, 1, :], hf.v(), hb.v(), ALU.subtract, eng="pool")
                else:
                    S.tt(tapv[:, a, 0, :], hf.v(), hb.v(), ALU.add, eng="pool")
                    S.tt(tapv[:, a, 1, :], hf.v(), hb.v(), ALU.subtract, eng="pool")

        for cq in range(1024 // CW):
            c0 = cq * CW
            S.dma(delt.v(), V(d["c_delta"], d["c_delta"].ap[0, c0:c0 + CW].partition_broadcast(128)))
            S.dma(skipr.v(), d["w_hy_skip"][i:i + 1, :, c0:c0 + CW])
            for k in range(4):
                S.dma(w3q[:, k, :], d["w_w3aug"][i, :, k * 1024 + c0:k * 1024 + c0 + CW], eng="pool")
            if wsegs is not None:
                for gi in range(ngi):
                    g = cq * ngi + gi
                    wv_ = d["w_cd_wg"][i, g].rearrange("p (k n) -> p k n", n=768)
                    S.dma(wsegs[gi][0].v(), wv_[:, :, 0:256], eng="pool")
                    S.dma(wsegs[gi][1].v(), wv_[:, :, 256:512], eng="pool")
            firstseq = True
            for si_, sq in enumerate(grp["seqs"]):
                bs_ = sets[si_ % NSET]
                Y, vtm, x1s, fmv, zhs = bs_["Y"], bs_["vtm"], bs_["x1s"], bs_["fmv"], bs_["zhs"]
                for gi in range(ngi):
                    g = cq * ngi + gi
                    if wsegs is None:
                        S.dma(wseg.v(), d["w_cd_wg"][i, g].rearrange("p (k n) -> p k n", n=768)[:, :, 0:256], eng="pool")
                        w_ = wseg
                    else:
                        w_ = wsegs[gi][0]
                    conv3(0, g, w_[:, :, 0:128], sq, fmv[gi])
                    conv3(1, g, w_[:, :, 128:256], sq, x1s[gi])
                    self.to_tm(fmv[gi], vtm, gi, nT)
                if firstseq:
                    make_taps(0, c0)
                self.hy_fwd(fwd, nT, vtm, G, Y, 0, tapv if firstseq else None, ct)
                self.hy_inv(ivd, nT, L, Y, lambda gi, cs, pv, fmv=fmv, x1s=x1s: S.tt(fmv[gi][:, cs], pv, x1s[gi][:, cs], ALU.mult))
                for gi in range(ngi):
                    self.to_tm(fmv[gi], vtm, gi, nT)
                if firstseq:
                    make_taps(1, c0)
                self.hy_fwd(fwd, nT, vtm, G, Y, 1, tapv if firstseq else None, ct)
                for gi in range(ngi):
                    g = cq * ngi + gi
                    if wsegs is None:
                        S.dma(wseg.v(), d["w_cd_wg"][i, g].rearrange("p (k n) -> p k n", n=768)[:, :, 256:512], eng="pool")
                        w_ = wseg
                    else:
                        w_ = wsegs[gi][1]
                    conv3(2, g, w_[:, :, 0:128], sq, x1s[gi])
                    self.cd_proj_fm(w_[:, :, 128:256], grp, sq,
                                    lambda t0, n, pv, gi=gi, zhs=zhs: S.act(zhs[gi][:, t0:t0 + n], pv, AF.Silu))

                def fin(gi, cs, pv, fmv=fmv, x1s=x1s, zhs=zhs):
                    n_ = cs.stop - cs.start
                    c3f = c3ts[self.c3i % 2]
                    self.c3i += 1
                    S.tt(c3f[:, 0:n_], pv, x1s[gi][:, cs], ALU.mult)
                    S.tt(fmv[gi][:, cs], c3f[:, 0:n_], zhs[gi][:, cs], ALU.mult, eng="pool")
                self.hy_inv(ivd, nT, L, Y, fin)
                for gi in range(ngi):
                    g = cq * ngi + gi
                    self.store_yT(grp, sq, g, fmv[gi][:, 0:L])
                firstseq = False

    def to_tm(self, src, vtm, gi, nT):
        S = self.S
        for a0 in range(0, nT, 8):
            na = min(8, nT - a0)
            pt = self.ps().v().bitcast(BF16)
            for a in range(na):
                S.transpose(pt[:, a * 128:(a + 1) * 128], src[:, (a0 + a) * 128:(a0 + a + 1) * 128], self.ident.v())
            S.copy(vtm[:, a0:a0 + na, gi * 128:(gi + 1) * 128],
                   pt[:, 0:na * 128].rearrange("p (a c) -> p a c", c=128), eng="act")

    def hy_fwd(self, fwd, nT, vtm, G, Y, cv, tapv, ct):
        S = self.S
        CW = self.CW
        fwb = self.fwb

        def load(ft):
            b = fwb[self.fwi % len(fwb)]
            self.fwi += 1
            S.dma(b[:, 0:nT * 128], fwd[ft].rearrange("p a q -> p (a q)"))
            return b

        def dft(b, M, rhs_fn):
            p = self.ps()
            for a in range(nT):
                S.mm(p[0:M, 0:CW], b[:, a * 128:a * 128 + M], rhs_fn(a), start=(a == 0), stop=(a == nT - 1))
            return p

        for j in range(nT + 1):
            if j < nT:
                fts = (j, nT + 1 + j)
                M = 128
            else:
                fts = (nT,)
                M = 1
            pu = []
            for idx, ft in enumerate(fts):
                b = load(ft)
                if tapv is not None:
                    sel = 0 if idx == 0 else 1
                    pg = self.ps()
                    pd = self.ps()
                    for a in range(nT):
                        S.mm(pg[0:M, 0:CW], b[:, a * 128:a * 128 + M], tapv[:, a, sel, :], start=(a == 0), stop=(a == nT - 1))
                        S.mm(pd[0:M, 0:CW], b[:, a * 128:a * 128 + M], vtm[:, a, :], start=(a == 0), stop=(a == nT - 1))
                    S.copy(G[ft][0:M, cv, :], pg[0:M, 0:CW], eng="act")
                    pu.append(pd)
                else:
                    pu.append(dft(b, M, lambda a: vtm[:, a, :]))
            if j < nT:
                ur, ui = pu
                gr = G[fts[0]][:, cv, :]
                gi_ = G[fts[1]][:, cv, :]
                c0_ = 4 * (j % 2)
                S.tt(ct[c0_].v(), ur[:, 0:CW], gr, ALU.mult)
                S.tt(ct[c0_ + 1].v(), ui[:, 0:CW], gi_, ALU.mult)
                S.tt(Y[fts[0]].v(), ct[c0_].v(), ct[c0_ + 1].v(), ALU.subtract, eng="pool")
                S.tt(ct[c0_ + 2].v(), ur[:, 0:CW], gi_, ALU.mult)
                S.tt(ct[c0_ + 3].v(), ui[:, 0:CW], gr, ALU.mult)
                S.tt(Y[fts[1]].v(), ct[c0_ + 2].v(), ct[c0_ + 3].v(), ALU.add, eng="pool")
            else:
                S.tt(Y[nT][0:1, :], pu[0][0:1, 0:CW], G[nT][0:1, cv, :], ALU.mult)

    def hy_inv(self, ivd, nT, L, Y, evac):
        S = self.S
        nft = 2 * nT + 1
        sub = min(512, L)
        nsub = L // sub
        CW = self.CW
        ngi = CW // 128
        acc = [[self.ps() for _ in range(nsub)] for _ in range(ngi)]
        for ft in range(nft):
            b = self.fwb[self.fwi % len(self.fwb)]
            self.fwi += 1
            S.dma(b[:, 0:L], ivd[ft])
            K = 1 if ft == nT else 128
            for gi in range(ngi):
                for sb_ in range(nsub):
                    S.mm(acc[gi][sb_][:, 0:sub], Y[ft][0:K, gi * 128:(gi + 1) * 128], b[0:K, sb_ * sub:(sb_ + 1) * sub],
                         start=(ft == 0), stop=(ft == nft - 1))
        for gi in range(ngi):
            for sb_ in range(nsub):
                evac(gi, slice(sb_ * sub, (sb_ + 1) * sub), acc[gi][sb_][:, 0:sub])


_PROG_CACHE = {}


def _get_prog(consts, wts, n_layers=4, groups=("P", "S")):
    key = (n_layers, groups)
    if key not in _PROG_CACHE:
        pr = Prog(n_layers=n_layers, groups=groups)
        pr.declare(consts, wts)
        pr.build()
        _PROG_CACHE[key] = pr
    return _PROG_CACHE[key]


def make_in_maps(inputs, consts, wts):
    xp = np.asarray(inputs["x_prompt"], np.float32)
    xs = np.asarray(inputs["x_sample"], np.float32)
    c = np.asarray(inputs["c"], np.float32)
    cctx = np.asarray(inputs["c_ctx"], np.float32)
    sg = np.asarray(inputs["state_gla"], np.float32)
    sr = np.asarray(inputs["state_ret"], np.float32)
    sl = np.asarray(inputs["state_lru"], np.float32)
    maps = []
    for core in range(8):
        m = {}
        for k, v in consts.items():
            m["c_" + k] = v
        for k, v in wts.items():
            m["w_" + k] = v
        fm = wts["fm"].copy()
        fm[:, FM["stl"]:FM["stl"] + 32] = sl[core].reshape(2, 2, 8, 128).transpose(3, 0, 1, 2).reshape(128, 32)
        m["w_fm"] = fm
        m["xp"] = np.ascontiguousarray(xp[core * NP_SEQ:(core + 1) * NP_SEQ].reshape(TP, D))
        m["xs"] = np.ascontiguousarray(xs[core])
        cond = np.stack([cctx, c[core]], 0)
        m["condT"] = np.ascontiguousarray(cond.reshape(2, 8, 128).transpose(2, 1, 0).reshape(128, 16))
        m["st_gla"] = np.ascontiguousarray(sg[core])
        m["st_ret"] = np.ascontiguousarray(sr[core])
        maps.append(m)
    return maps


def kernel(**inputs):
    consts = make_consts()
    wts = prep_weights(inputs)
    prog = _get_prog(consts, wts)
    maps = make_in_maps(inputs, consts, wts)
    res = run_bass_kernel_spmd(prog.nc, maps, core_ids=list(range(8)))
    r = res.results
    y_p = np.concatenate([r[c]["y_p"].reshape(NP_SEQ, LP, D) for c in range(8)], 0).astype(np.float32)
    y_s = np.stack([r[c]["y_s"] for c in range(8)], 0).astype(np.float32)
    ng = np.concatenate([r[c]["ns_gla"] for c in range(8)], 0).astype(np.float32)
    nr = np.concatenate([r[c]["ns_ret"] for c in range(8)], 0).astype(np.float32)
    nl = np.concatenate([r[c]["ns_lru"] for c in range(8)], 0).astype(np.float32)
    return (y_p, y_s, ng, nr, nl)
```
